# Optimizing a Trainium2 kernel written in Bass

```python
import math
import jax, jax.numpy as jnp
from jax import lax
import numpy as np

D_MODEL = 1024
BATCH = 8
SEQ = 4096
DEPTH = 1

HY_WIDTH = 512
HY_GROUPS = 8
HY_ORDER = 2
HY_EMB = 33
HY_BANDS = (HY_EMB - 1) // 2
HY_FILT_HIDDEN = 64
HY_FAST_DECAY = 0.3
HY_SLOW_DECAY = 1.5
HY_DECAY_TARGET = 1e-2
HY_MAX_DECAY = math.log(HY_DECAY_TARGET) / HY_FAST_DECAY
HY_MIN_DECAY = math.log(HY_DECAY_TARGET) / HY_SLOW_DECAY
HY_COLS = 3 * HY_WIDTH

RW_WIDTH = 512
RW_HEAD = 64
RW_HEADS = RW_WIDTH // RW_HEAD
RW_LORA_W = 64
RW_LORA_A = 64
RW_LORA_G = 128
RW_GN_EPS = 64e-5
RW_SIZES = [RW_WIDTH, RW_WIDTH, RW_WIDTH, RW_LORA_W, RW_LORA_W, RW_LORA_A, RW_LORA_A, RW_LORA_G]
RW_OFFSETS = [sum(RW_SIZES[:i + 1]) for i in range(len(RW_SIZES) - 1)]
RW_COLS = sum(RW_SIZES)

GATE_COLS = 2 * D_MODEL
IN_COLS = HY_COLS + RW_COLS + GATE_COLS

FFN_HIDDEN = ((8 * D_MODEL + 3 * 256 - 1) // (3 * 256)) * 256

DN_ALPHA = (2.0 * DEPTH) ** 0.25
DN_BETA = (8.0 * DEPTH) ** -0.25
LN_EPS = 1e-5

kernel_name = "hyena_rwkv7_gated_hybrid_encoder_layer"


def layer_norm(x, w, b):
    xf = x.astype(jnp.float32)
    mu = jnp.mean(xf, axis=-1, keepdims=True)
    var = jnp.mean(jnp.square(xf - mu), axis=-1, keepdims=True)
    return ((xf - mu) * lax.rsqrt(var + LN_EPS) * w + b).astype(x.dtype)


def shift_prev(z):
    return jnp.pad(z[:, :-1], ((0, 0), (1, 0), (0, 0)))


def shift_next(z):
    return jnp.pad(z[:, 1:], ((0, 0), (0, 1), (0, 0)))


def short_conv3(z, w, b):
    return w[0] * shift_prev(z) + w[1] * z + w[2] * shift_next(z) + b


def hyena_filters(L, fw1, fb1, fw2, fb2, fw3, fb3, fw4, sin_freq):
    f32 = jnp.float32
    t = jnp.linspace(0.0, 1.0, L, dtype=f32)[:, None]
    w = 2.0 * math.pi * jnp.arange(L, dtype=f32) / L
    f = jnp.linspace(1e-4, HY_BANDS - 1, HY_BANDS, dtype=f32)
    ang = w[:, None] * f[None, :]
    feats = jnp.concatenate([t, jnp.cos(ang), -jnp.sin(ang)], axis=-1)
    sf = sin_freq.astype(f32)
    h = jnp.sin(sf[0] * (feats @ fw1.astype(f32) + fb1.astype(f32)))
    h = jnp.sin(sf[1] * (h @ fw2.astype(f32) + fb2.astype(f32)))
    h = jnp.sin(sf[2] * (h @ fw3.astype(f32) + fb3.astype(f32)))
    h = (h @ fw4.astype(f32)).reshape(L, HY_ORDER, 2, HY_WIDTH)
    deltas = jnp.abs(jnp.linspace(HY_MIN_DECAY, HY_MAX_DECAY, HY_WIDTH, dtype=f32))
    window = jnp.exp(-t * deltas[None, :])
    return h * window[:, None, None, :]


def long_conv(z, h_fwd, h_bwd, d_skip):
    L = z.shape[1]
    n = 2 * L
    zf = jnp.fft.rfft(z, n=n, axis=1)
    hf = jnp.fft.rfft(h_fwd, n=n, axis=0) + jnp.conj(jnp.fft.rfft(h_bwd, n=n, axis=0))
    y = jnp.fft.irfft(zf * hf[None], n=n, axis=1)[:, :L]
    return y + z * d_skip


def hyena_mixer(u, conv_w, conv_b, fw1, fb1, fw2, fb2, fw3, fb3, fw4, sin_freq, skip):
    L = u.shape[1]
    u = short_conv3(u, conv_w, conv_b).astype(jnp.float32)
    v, x1, x2 = jnp.split(u, 3, axis=-1)
    h = hyena_filters(L, fw1, fb1, fw2, fb2, fw3, fb3, fw4, sin_freq)
    sk = skip.astype(jnp.float32)
    z = x1 * long_conv(v, h[:, 0, 0], h[:, 0, 1], sk[0])
    z = x2 * long_conv(z, h[:, 1, 0], h[:, 1, 1], sk[1])
    return z


def rwkv_step(S, inp):
    r, w, k, v, kk, a = inp
    sa = jnp.einsum('dbhvk,dbhk->dbhv', S, -kk)
    S = S * w[..., None, :] + sa[..., :, None] * (kk * a)[..., None, :] + v[..., :, None] * k[..., None, :]
    y = jnp.einsum('dbhvk,dbhk->dbhv', S, r)
    return S, y


def rwkv_mixer(u, mu, w0, w2, a0, a2, g2, k_k, k_a, r_k, gn_w, gn_b):
    f32 = jnp.float32
    B, L, _ = u.shape
    u = u + mu * (0.5 * (shift_prev(u) + shift_next(u)) - u)
    r, k, v, wd_f, wd_b, ad_f, ad_b, gd = jnp.split(u.astype(f32), RW_OFFSETS, axis=-1)
    d = w0.astype(f32)[:, None, None, :] + jnp.einsum('dblr,drc->dblc', jnp.tanh(jnp.stack([wd_f, wd_b])), w2.astype(f32))
    decay = jnp.exp(-math.exp(-0.5) * jax.nn.sigmoid(d))
    a = jax.nn.sigmoid(a0.astype(f32)[:, None, None, :] + jnp.einsum('dblr,drc->dblc', jnp.stack([ad_f, ad_b]), a2.astype(f32)))
    g = jax.nn.sigmoid(gd) @ g2.astype(f32)

    def heads(z):
        return z.reshape(z.shape[:-1] + (RW_HEADS, RW_HEAD))

    kk = heads(k * k_k.astype(f32))
    kk = kk / jnp.maximum(jnp.sqrt(jnp.sum(kk * kk, axis=-1, keepdims=True)), 1e-12)
    kd = heads(k[None] * (1.0 + (a - 1.0) * k_a.astype(f32)))
    r_h, v_h, w_h, a_h = heads(r), heads(v), heads(decay), heads(a)

    def dirs(z_f, z_b):
        return jnp.moveaxis(jnp.stack([z_f, z_b[:, ::-1]]), 2, 0)

    xs = (dirs(r_h, r_h), dirs(w_h[0], w_h[1]), dirs(kd[0], kd[1]),
          dirs(v_h, v_h), dirs(kk, kk), dirs(a_h[0], a_h[1]))
    S0 = jnp.zeros((2, B, RW_HEADS, RW_HEAD, RW_HEAD), f32)
    _, ys = lax.scan(rwkv_step, S0, xs)
    ys = jnp.moveaxis(ys, 0, 2)
    y = ys[0] + ys[1][:, ::-1]
    mu_y = jnp.mean(y, axis=-1, keepdims=True)
    var_y = jnp.mean(jnp.square(y - mu_y), axis=-1, keepdims=True)
    y = (y - mu_y) * lax.rsqrt(var_y + RW_GN_EPS) * heads(gn_w.astype(f32)) + heads(gn_b.astype(f32))
    bonus = jnp.sum(r_h[None] * kd * r_k.astype(f32), axis=-1, keepdims=True) * v_h[None]
    y = y + jnp.sum(bonus, axis=0)
    return y.reshape(B, L, RW_WIDTH) * g


def hybrid_layer(x, w_in, hy_conv_w, hy_conv_b, hy_filt_w1, hy_filt_b1, hy_filt_w2, hy_filt_b2,
                 hy_filt_w3, hy_filt_b3, hy_filt_w4, hy_sin_freq, hy_skip, rw_mu, rw_w0, rw_w2,
                 rw_a0, rw_a2, rw_g2, rw_k_k, rw_k_a, rw_r_k, rw_gn_w, rw_gn_b, w_hy_out, w_rw_out,
                 w_o, ln1_w, ln1_b, ffn_w_gate, ffn_w_up, ffn_w_down, ln2_w, ln2_b):
    dt = x.dtype
    proj = x @ w_in
    u_h, u_r, gates = jnp.split(proj, [HY_COLS, HY_COLS + RW_COLS], axis=-1)
    y_h = hyena_mixer(u_h, hy_conv_w, hy_conv_b, hy_filt_w1, hy_filt_b1, hy_filt_w2, hy_filt_b2,
                      hy_filt_w3, hy_filt_b3, hy_filt_w4, hy_sin_freq, hy_skip).astype(dt) @ w_hy_out
    y_r = rwkv_mixer(u_r, rw_mu, rw_w0, rw_w2, rw_a0, rw_a2, rw_g2, rw_k_k, rw_k_a, rw_r_k,
                     rw_gn_w, rw_gn_b).astype(dt) @ w_rw_out
    g_h, g_r = jnp.split(jax.nn.sigmoid(gates), 2, axis=-1)
    mix = (g_h * y_h + g_r * y_r) @ w_o
    x = layer_norm(DN_ALPHA * x + mix, ln1_w, ln1_b)
    ffn = (jax.nn.silu(x @ ffn_w_gate) * (x @ ffn_w_up)) @ ffn_w_down
    return layer_norm(DN_ALPHA * x + ffn, ln2_w, ln2_b)


def setup_inputs(seed: int = 0) -> dict:
    key = jax.random.key(seed)
    keys = iter(jax.random.split(key, 48))

    def nrm(shape, scale):
        return jax.random.normal(next(keys), shape, jnp.float32) * scale

    Ld = DEPTH
    return {
        "x": nrm((BATCH, SEQ, D_MODEL), 1.0),
        "w_in": nrm((Ld, D_MODEL, IN_COLS), D_MODEL ** -0.5),
        "hy_conv_w": nrm((Ld, 3, HY_COLS), 3 ** -0.5),
        "hy_conv_b": nrm((Ld, HY_COLS), 0.02),
        "hy_filt_w1": nrm((Ld, HY_EMB, HY_FILT_HIDDEN), HY_EMB ** -0.5),
        "hy_filt_b1": nrm((Ld, HY_FILT_HIDDEN), 0.1),
        "hy_filt_w2": nrm((Ld, HY_FILT_HIDDEN, HY_FILT_HIDDEN), HY_FILT_HIDDEN ** -0.5),
        "hy_filt_b2": nrm((Ld, HY_FILT_HIDDEN), 0.1),
        "hy_filt_w3": nrm((Ld, HY_FILT_HIDDEN, HY_FILT_HIDDEN), HY_FILT_HIDDEN ** -0.5),
        "hy_filt_b3": nrm((Ld, HY_FILT_HIDDEN), 0.1),
        "hy_filt_w4": nrm((Ld, HY_FILT_HIDDEN, HY_ORDER * 2 * HY_WIDTH), 0.1 * HY_FILT_HIDDEN ** -0.5),
        "hy_sin_freq": 1.0 + nrm((Ld, 3, HY_FILT_HIDDEN), 0.05),
        "hy_skip": nrm((Ld, HY_ORDER, HY_WIDTH), 0.5),
        "rw_mu": 0.5 + nrm((Ld, RW_COLS), 0.1),
        "rw_w0": -0.5 + nrm((Ld, 2, RW_WIDTH), 0.5),
        "rw_w2": nrm((Ld, 2, RW_LORA_W, RW_WIDTH), 0.5 * RW_LORA_W ** -0.5),
        "rw_a0": nrm((Ld, 2, RW_WIDTH), 0.1),
        "rw_a2": nrm((Ld, 2, RW_LORA_A, RW_WIDTH), 0.5 * RW_LORA_A ** -0.5),
        "rw_g2": nrm((Ld, RW_LORA_G, RW_WIDTH), RW_LORA_G ** -0.5),
        "rw_k_k": 0.85 + nrm((Ld, RW_WIDTH), 0.05),
        "rw_k_a": 1.0 + nrm((Ld, RW_WIDTH), 0.05),
        "rw_r_k": nrm((Ld, RW_HEADS, RW_HEAD), 0.1),
        "rw_gn_w": 1.0 + nrm((Ld, RW_WIDTH), 0.05),
        "rw_gn_b": nrm((Ld, RW_WIDTH), 0.02),
        "w_hy_out": nrm((Ld, HY_WIDTH, D_MODEL), HY_WIDTH ** -0.5),
        "w_rw_out": nrm((Ld, RW_WIDTH, D_MODEL), RW_WIDTH ** -0.5),
        "w_o": nrm((Ld, D_MODEL, D_MODEL), DN_BETA * D_MODEL ** -0.5),
        "ln1_w": 1.0 + nrm((Ld, D_MODEL), 0.05),
        "ln1_b": nrm((Ld, D_MODEL), 0.02),
        "ffn_w_gate": nrm((Ld, D_MODEL, FFN_HIDDEN), D_MODEL ** -0.5),
        "ffn_w_up": nrm((Ld, D_MODEL, FFN_HIDDEN), D_MODEL ** -0.5),
        "ffn_w_down": nrm((Ld, FFN_HIDDEN, D_MODEL), DN_BETA * FFN_HIDDEN ** -0.5),
        "ln2_w": 1.0 + nrm((Ld, D_MODEL), 0.05),
        "ln2_b": nrm((Ld, D_MODEL), 0.02),
    }


def reference(x, w_in, hy_conv_w, hy_conv_b, hy_filt_w1, hy_filt_b1, hy_filt_w2, hy_filt_b2,
              hy_filt_w3, hy_filt_b3, hy_filt_w4, hy_sin_freq, hy_skip, rw_mu, rw_w0, rw_w2,
              rw_a0, rw_a2, rw_g2, rw_k_k, rw_k_a, rw_r_k, rw_gn_w, rw_gn_b, w_hy_out, w_rw_out,
              w_o, ln1_w, ln1_b, ffn_w_gate, ffn_w_up, ffn_w_down, ln2_w, ln2_b):
    for l in range(DEPTH):
        x = hybrid_layer(x, w_in[l], hy_conv_w[l], hy_conv_b[l], hy_filt_w1[l], hy_filt_b1[l],
                         hy_filt_w2[l], hy_filt_b2[l], hy_filt_w3[l], hy_filt_b3[l], hy_filt_w4[l],
                         hy_sin_freq[l], hy_skip[l], rw_mu[l], rw_w0[l], rw_w2[l], rw_a0[l],
                         rw_a2[l], rw_g2[l], rw_k_k[l], rw_k_a[l], rw_r_k[l], rw_gn_w[l], rw_gn_b[l],
                         w_hy_out[l], w_rw_out[l], w_o[l], ln1_w[l], ln1_b[l], ffn_w_gate[l],
                         ffn_w_up[l], ffn_w_down[l], ln2_w[l], ln2_b[l])
    return x
```

```python
import contextlib
import math
import numpy as np
import concourse.bass as bass
import concourse.mybir as mybir
from concourse.bass_utils import run_bass_kernel_spmd

F32 = mybir.dt.float32
BF16 = mybir.dt.bfloat16
ALU = mybir.AluOpType
AF = mybir.ActivationFunctionType

ENGS = ("pe", "act", "dve", "pool", "sp")
NDMASEM = 12
L = 4096
D = 1024
ALPHA = 2.0 ** 0.25
LN_EPS = 1e-5
GN_EPS = 64e-5
MAGIC = 12582912.0
PI = float(np.pi)


class Prog:
    def __init__(self, nc, stack):
        self.nc = nc
        self.csem = {e: stack.enter_context(nc.semaphore("c_" + e)) for e in ENGS}
        self.dsem = {}
        for e in ("sp", "pool", "act"):
            for k in range(NDMASEM):
                self.dsem[(e, k)] = stack.enter_context(nc.semaphore("d_%s_%d" % (e, k)))
        self.ctot = {e: 0 for e in ENGS}
        self.dtot = {e: 0 for e in ENGS}
        self._reset()

    def _reset(self):
        self.ops = {e: [] for e in ENGS}
        self.last_w = {}
        self.readers = {}

    def _add(self, eng, fn, reads, writes, dma=False):
        idx = len(self.ops[eng])
        ev = (eng, idx)
        deps = set()
        for r in reads:
            w = self.last_w.get(r)
            if w is not None:
                deps.add(w)
        for w_ in writes:
            w = self.last_w.get(w_)
            if w is not None:
                deps.add(w)
            for rd in self.readers.get(w_, ()):
                deps.add(rd)
        deps.discard(ev)
        self.ops[eng].append(dict(fn=fn, deps=deps, dma=dma))
        for r in reads:
            self.readers.setdefault(r, []).append(ev)
        for w_ in writes:
            self.last_w[w_] = ev
            self.readers[w_] = []
        return ev

    def op(self, eng, fn, reads=(), writes=()):
        return self._add(eng, fn, tuple(reads), tuple(writes), dma=False)

    def dma(self, eng, out, in_, reads=(), writes=(), **kw):
        return self._add(eng, lambda e: e.dma_start(out=out, in_=in_, **kw),
                         tuple(reads), tuple(writes), dma=True)

    def emit_stage(self):
        nc = self.nc
        ops = self.ops
        needed = {e: set() for e in ENGS}
        for e in ENGS:
            for o in ops[e]:
                for (de, di) in o["deps"]:
                    if not ops[de][di]["dma"]:
                        if de == "pe" and e == "pe":
                            continue
                        needed[de].add(di)
        for e in ENGS:
            for i in range(len(ops[e]) - 1, -1, -1):
                if not ops[e][i]["dma"]:
                    needed[e].add(i)
                    break
        count_at = {e: {} for e in ENGS}
        cend = {}
        for e in ENGS:
            c = self.ctot[e]
            for i, o in enumerate(ops[e]):
                if (not o["dma"]) and i in needed[e]:
                    c += 1
                    count_at[e][i] = c
            cend[e] = c
        dma_info = {}
        dend = {}
        for e in ENGS:
            n = self.dtot[e]
            for i, o in enumerate(ops[e]):
                if o["dma"]:
                    dma_info[(e, i)] = (e, n % NDMASEM, 16 * (n // NDMASEM + 1), n)
                    n += 1
            dend[e] = n
        csem, dsem = self.csem, self.dsem
        ftargets = []
        for e in ENGS:
            n = dend[e]
            for k in range(min(n, NDMASEM)):
                ftargets.append((dsem[(e, k)], 16 * ((n - 1 - k) // NDMASEM + 1)))

        def run_engine(ename, eng):
            known = {e: -1 for e in ENGS}
            known_dma = set()
            for i, o in enumerate(ops[ename]):
                for (de, di) in sorted(o["deps"]):
                    if ops[de][di]["dma"]:
                        if (de, di) in known_dma:
                            continue
                        q, k, tgt, n = dma_info[(de, di)]
                        eng.wait_ge(dsem[(q, k)], tgt)
                        known_dma.add((de, di))
                    else:
                        if de == "pe" and ename == "pe":
                            continue
                        if known[de] >= di:
                            continue
                        eng.wait_ge(csem[de], count_at[de][di])
                        known[de] = di
                if o["dma"]:
                    q, k, tgt, n = dma_info[(ename, i)]
                    if n >= NDMASEM:
                        eng.wait_ge(dsem[(q, k)], tgt - 16)
                    o["fn"](eng).then_inc(dsem[(q, k)], 16)
                else:
                    ins = o["fn"](eng)
                    if i in count_at[ename]:
                        ins.then_inc(csem[ename], 1)
            for e in ENGS:
                if cend[e] > 0:
                    eng.wait_ge(csem[e], cend[e])
            for (s, v) in ftargets:
                eng.wait_ge(s, v)

        with nc.Block() as block:
            @block.tensor
            def _(eng):
                run_engine("pe", eng)

            @block.scalar
            def _(eng):
                run_engine("act", eng)

            @block.vector
            def _(eng):
                run_engine("dve", eng)

            @block.gpsimd
            def _(eng):
                run_engine("pool", eng)

            @block.sync
            def _(eng):
                run_engine("sp", eng)

        self.ctot = cend
        self.dtot = dend
        n_ops = {e: len(ops[e]) for e in ENGS}
        self._reset()
        return n_ops


def const_tables():
    f = np.float64
    c = {}
    c["ident"] = np.eye(128)
    ob = np.zeros((128, 128))
    ob[:64, :64] = 1.0
    ob[64:, 64:] = 1.0
    c["onesblk"] = ob
    t = np.linspace(0.0, 1.0, L)[:, None]
    w = 2.0 * math.pi * np.arange(L) / L
    fr = np.linspace(1e-4, 15.0, 16)
    ang = w[:, None] * fr[None, :]
    c["featsT"] = np.concatenate([t, np.cos(ang), -np.sin(ang)], axis=-1).T
    deltas = np.abs(np.linspace(math.log(1e-2) / 1.5, math.log(1e-2) / 0.3, 512))
    win = np.exp(-t * deltas[None, :])
    c["win"] = win.reshape(64, 64, 16, 32).transpose(2, 0, 1, 3).reshape(16, 64, 64 * 32)
    i64 = np.arange(64)
    k1 = np.arange(65)
    F1 = np.zeros((64, 2, 65))
    th = 2 * math.pi * np.outer(i64, k1) / 128.0
    F1[:, 0], F1[:, 1] = np.cos(th), -np.sin(th)
    c["F1"] = F1.reshape(64, 130)
    th = 2 * math.pi * np.outer(i64, k1) / 8192.0
    c["TWf"] = np.stack([np.cos(th), np.sin(th)], axis=1)
    th = 2 * math.pi * np.outer(i64, i64) / 64.0
    C, S = np.cos(th), np.sin(th)
    cat = lambda p, q: np.concatenate([p, q], axis=1)
    c["S3"] = np.stack([cat(C, -S), cat(S, C), cat(-S, C), cat(C, S), cat(C, C), cat(S, S),
                        cat(S, -S), cat(-C, C), cat(-S, S), cat(C, -C)], axis=1)
    G = np.zeros((128, 128))
    G[:64, :64], G[:64, 64:], G[64:, :64], G[64:, 64:] = C, S, -S, C
    c["G"] = G
    th = 2 * math.pi * np.outer(k1, i64) / 8192.0
    c["TWi"] = np.stack([np.cos(th), np.sin(th)], axis=1)
    ck = np.full(65, 2.0)
    ck[0] = ck[64] = 1.0
    th = 2 * math.pi * np.outer(k1, i64) / 128.0
    c["I3t"] = np.stack([ck[:, None] / 8192.0 * np.cos(th), -ck[:, None] / 8192.0 * np.sin(th)], axis=1)
    row = np.arange(64)[:, None]
    col = np.arange(64)[None, :]
    def stk(f0, f1):
        m = np.zeros((2, 64, 2, 4, 64))
        m[:, :, 0, :, :] = f0[None, :, None, :]
        m[:, :, 1, :, :] = f1[None, :, None, :]
        return m.reshape(128, 512)
    c["mSU"] = stk((row < col) * 1.0, (row > col) * 1.0)
    c["mIU"] = stk((row <= col) * 1.0, (row >= col) * 1.0)
    c["mSM"] = stk((col < row) * 1.0, (col > row) * 1.0)
    c["Istk"] = stk((row == col) * 1.0, (row == col) * 1.0)
    cm = np.ones((128, 1024))
    cm[:, ::64] = 0.0
    c["cmask"] = cm
    return {k: np.ascontiguousarray(v, dtype=np.float32) for k, v in c.items()}


class K:
    def __init__(self, dbg=()):
        self.dbg = set(dbg)
        self.nc = bass.Bass("TRN2", target_bir_lowering=False)
        self.st = contextlib.ExitStack()
        self.P = Prog(self.nc, self.st)
        self.inputs = {}
        self.ps = [self.st.enter_context(self.nc.psum_tensor("ps%d" % i, [128, 512], F32)) for i in range(8)]
        self.psk = ["ps%d" % i for i in range(8)]
        self.bank_i = 0

    def bank(self):
        i = self.bank_i % 8
        self.bank_i += 1
        return self.ps[i], self.psk[i]

    def din(self, name, shape, dt=F32):
        return self.nc.dram_tensor(name, list(shape), dt, kind="ExternalInput").ap()

    def dscr(self, name, shape, dt=F32, ext_in=False):
        kind = "Internal"
        if name in self.dbg:
            kind = "ExternalOutput"
        if ext_in:
            kind = "ExternalInput"
        return self.nc.dram_tensor(name, list(shape), dt, kind=kind).ap()

    _uid = [0]

    @staticmethod
    def sb(s, nc, name, shape, dt=F32):
        K._uid[0] += 1
        return s.enter_context(nc.sbuf_tensor("s%d_%s" % (K._uid[0], name), list(shape), dt))


def layer_norm_rows(P, nc, pre, outt, lnw, lnb, sm, junk, epsb, key, eng2="pool"):
    P.op("dve", lambda e: e.memset(sm[:, 0:2], 0.0), writes=[key + "sm"])
    P.op("act", lambda e: e.activation(out=junk[:], in_=pre[:], func=AF.Identity, accum_out=sm[:, 0:1]),
         reads=[key + "pre", key + "sm"], writes=[key + "sm", key + "junk"])
    P.op("act", lambda e: e.activation(out=junk[:], in_=pre[:], func=AF.Square, accum_out=sm[:, 1:2]),
         reads=[key + "pre", key + "sm", key + "junk"], writes=[key + "sm", key + "junk"])
    P.op("dve", lambda e: e.tensor_scalar(out=sm[:, 2:4], in0=sm[:, 0:2], scalar1=1.0 / 1024, scalar2=None, op0=ALU.mult),
         reads=[key + "sm"], writes=[key + "sm"])
    P.op("dve", lambda e: e.tensor_tensor(out=sm[:, 4:5], in0=sm[:, 2:3], in1=sm[:, 2:3], op=ALU.mult),
         reads=[key + "sm"], writes=[key + "sm"])
    P.op("dve", lambda e: e.tensor_tensor(out=sm[:, 5:6], in0=sm[:, 3:4], in1=sm[:, 4:5], op=ALU.subtract),
         reads=[key + "sm"], writes=[key + "sm"])
    P.op("act", lambda e: e.activation(out=sm[:, 6:7], in_=sm[:, 5:6], func=AF.Sqrt, bias=epsb[:, 0:1], scale=1.0),
         reads=[key + "sm"], writes=[key + "sm"])
    P.op("dve", lambda e: e.reciprocal(sm[:, 7:8], sm[:, 6:7]), reads=[key + "sm"], writes=[key + "sm"])
    P.op("dve", lambda e: e.tensor_scalar(out=outt[:], in0=pre[:], scalar1=sm[:, 2:3], scalar2=sm[:, 7:8],
                                          op0=ALU.subtract, op1=ALU.mult),
         reads=[key + "pre", key + "sm"], writes=[key + "out"])
    P.op(eng2, lambda e: e.tensor_tensor(out=outt[:], in0=outt[:], in1=lnw[:], op=ALU.mult),
         reads=[key + "out", "lnw"], writes=[key + "out"])
    P.op(eng2, lambda e: e.tensor_tensor(out=outt[:], in0=outt[:], in1=lnb[:], op=ALU.add),
         reads=[key + "out", "lnb"], writes=[key + "out"])


def stage1(kb):
    nc, P = kb.nc, kb.P
    d = kb.d
    with contextlib.ExitStack() as s:
        sb = lambda n, sh, dt=F32: K.sb(s, nc, n, sh, dt)
        xT = sb("xT", [128, 8, 4096], BF16)
        wbuf = [sb("wb%d" % i, [128, 8, 512], BF16) for i in range(2)]
        raw = [sb("raw%d" % i, [128, 4098], F32) for i in range(2)]
        o = [sb("o%d" % i, [128, 4096], F32) for i in range(2)]
        uht = sb("uht", [64, 4, 64, 32], F32)
        pcs = sb("pcs", [128, 4, 27], F32)
        mus = sb("mus", [128, 15], F32)
        ident = sb("ident", [128, 128], F32)
        P.dma("sp", ident[:], d["ident"], writes=["ident"])
        P.dma("sp", pcs[:, 0:3, 0:12], d["cw"], writes=["pcs"])
        P.dma("sp", pcs[:, 3, 0:12], d["cb"], reads=["pcs"], writes=["pcs"])
        P.dma("sp", mus[:], d["mu"], writes=["mus"])
        P.op("dve", lambda e: e.memset(pcs[:, 3, 12:27], 0.0), reads=["pcs"], writes=["pcs"])
        P.op("dve", lambda e: e.tensor_scalar(out=pcs[:, 0, 12:27], in0=mus[:], scalar1=0.5, scalar2=None, op0=ALU.mult),
             reads=["mus", "pcs"], writes=["pcs"])
        P.op("dve", lambda e: e.tensor_scalar(out=pcs[:, 2, 12:27], in0=mus[:], scalar1=0.5, scalar2=None, op0=ALU.mult),
             reads=["mus", "pcs"], writes=["pcs"])
        P.op("dve", lambda e: e.tensor_scalar(out=pcs[:, 1, 12:27], in0=mus[:], scalar1=-1.0, scalar2=1.0, op0=ALU.mult, op1=ALU.add),
             reads=["mus", "pcs"], writes=["pcs"])
        for i in range(2):
            P.op("pool", lambda e, i=i: e.memset(raw[i][:, 0:1], 0.0), writes=["rawpad%d" % i])
            P.op("pool", lambda e, i=i: e.memset(raw[i][:, 4097:4098], 0.0), writes=["rawpad%d" % i])
        xTv = d["xT"].rearrange("(k p) t -> p k t", p=128)
        for k in range(8):
            P.dma("pool", xT[:, k, :], xTv[:, k, :], writes=[("xT", k)])
        wv = d["w_in"].rearrange("(k p) c -> p k c", p=128)
        evi = 0
        for cc in getattr(kb, "cc_list", range(43)):
            wb = cc // 4
            if cc % 4 == 0 or getattr(kb, "cc_list", None) is not None:
                ncol = min(512, 5504 - wb * 512)
                P.dma("pool", wbuf[wb % 2][:, :, 0:ncol], wv[:, :, wb * 512: wb * 512 + ncol], writes=["wb%d" % (wb % 2)])
            wt = wbuf[wb % 2]
            wk = "wb%d" % (wb % 2)
            c0 = (cc % 4) * 128
            rb = raw[cc % 2]
            ob = o[cc % 2]
            rk = "raw%d" % (cc % 2)
            ok = "o%d" % (cc % 2)
            gate = cc >= 27
            for tb in range(8):
                bk, bkk = kb.bank()
                for k in range(8):
                    P.op("pe", lambda e, bk=bk, wt=wt, k=k, c0=c0, tb=tb: e.matmul(
                        bk[:, :], lhsT=wt[:, k, c0:c0 + 128], rhs=xT[:, k, tb * 512:(tb + 1) * 512],
                        start=(k == 0), stop=(k == 7)), reads=[wk, ("xT", k)], writes=[bkk])
                if gate:
                    P.op("act", lambda e, bk=bk, ob=ob, tb=tb: e.activation(
                        out=ob[:, tb * 512:(tb + 1) * 512], in_=bk[:, :], func=AF.Sigmoid),
                        reads=[bkk], writes=[(ok, tb)])
                else:
                    eng = "act" if evi % 2 == 0 else "dve"
                    evi += 1
                    if eng == "act":
                        P.op("act", lambda e, bk=bk, rb=rb, tb=tb: e.copy(rb[:, 1 + tb * 512: 1 + (tb + 1) * 512], bk[:, :]),
                             reads=[bkk], writes=[(rk, tb)])
                    else:
                        P.op("dve", lambda e, bk=bk, rb=rb, tb=tb: e.tensor_copy(rb[:, 1 + tb * 512: 1 + (tb + 1) * 512], bk[:, :]),
                             reads=[bkk], writes=[(rk, tb)])
            okeys = [(ok, tb) for tb in range(8)]
            rkeys = [(rk, tb) for tb in range(8)] + ["rawpad%d" % (cc % 2)]
            if not gate:
                P.op("act", lambda e, rb=rb, ob=ob, cc=cc: e.activation(
                    out=ob[:, :], in_=rb[:, 1:4097], func=AF.Identity, bias=pcs[:, 3, cc:cc + 1], scale=pcs[:, 1, cc:cc + 1]),
                    reads=rkeys + ["pcs"], writes=okeys)
                P.op("dve", lambda e, rb=rb, ob=ob, cc=cc: e.scalar_tensor_tensor(
                    out=ob[:, :], in0=rb[:, 0:4096], scalar=pcs[:, 0, cc:cc + 1], in1=ob[:, :], op0=ALU.mult, op1=ALU.add),
                    reads=rkeys + ["pcs"] + okeys, writes=okeys)
                P.op("dve", lambda e, rb=rb, ob=ob, cc=cc: e.scalar_tensor_tensor(
                    out=ob[:, :], in0=rb[:, 2:4098], scalar=pcs[:, 2, cc:cc + 1], in1=ob[:, :], op0=ALU.mult, op1=ALU.add),
                    reads=rkeys + ["pcs"] + okeys, writes=okeys)
            if cc < 8:
                obv = ob[:, :].rearrange("p (b a) -> p a b", a=64)
                for a0 in range(0, 64, 4):
                    bk, bkk = kb.bank()
                    for ai in range(4):
                        P.op("pe", lambda e, bk=bk, obv=obv, a=a0 + ai, ai=ai: e.transpose(
                            bk[0:64, ai * 128:(ai + 1) * 128], obv[:, a, :], ident[:, :]),
                            reads=okeys + ["ident"], writes=[bkk])
                    eng = "act" if (a0 // 4) % 2 == 0 else "dve"
                    outap = uht[:, :, a0:a0 + 4, :].rearrange("p g a c -> p a g c")
                    inap = bk[0:64, :].rearrange("p (a g c) -> p a g c", a=4, g=4)
                    if eng == "act":
                        P.op("act", lambda e, outap=outap, inap=inap: e.copy(outap, inap), reads=[bkk], writes=[("uht", a0)])
                    else:
                        P.op("dve", lambda e, outap=outap, inap=inap: e.tensor_copy(outap, inap), reads=[bkk], writes=[("uht", a0)])
                P.dma("sp", d["UHg"][cc * 4:(cc + 1) * 4].rearrange("g b n -> b g n"),
                      uht[:].rearrange("b g a c -> b g (a c)"),
                      reads=[("uht", a0) for a0 in range(0, 64, 4)], writes=["UHg"])
            elif cc < 12:
                P.dma("sp", d["X2_fm"][(cc - 8) * 128:(cc - 7) * 128, :], ob[:, :], reads=okeys, writes=["X2_fm"])
            elif cc < 27:
                P.dma("sp", d["UR_fm"][(cc - 12) * 128:(cc - 11) * 128, :], ob[:, :], reads=okeys, writes=["UR_fm"])
            else:
                P.dma("sp", d["GT_fm"][(cc - 27) * 128:(cc - 26) * 128, :], ob[:, :], reads=okeys, writes=["GT_fm"])
        print("stage1", P.emit_stage())


def stage6(kb):
    nc, P = kb.nc, kb.P
    d = kb.d
    with contextlib.ExitStack() as s:
        sb = lambda n, sh, dt=F32: K.sb(s, nc, n, sh, dt)
        zhT = sb("zhT", [128, 4, 4096], BF16)
        zrT = sb("zrT", [128, 4, 4096], BF16)
        why = sb("why", [128, 4, 1024], BF16)
        wrw = sb("wrw", [128, 4, 1024], BF16)
        wo = sb("wo", [128, 8, 1024], BF16)
        mT = sb("mT", [128, 8, 4096], BF16)
        gh = [sb("gh%d" % i, [128, 512]) for i in range(2)]
        gr = [sb("gr%d" % i, [128, 512]) for i in range(2)]
        t1 = [sb("t1_%d" % i, [128, 512]) for i in range(2)]
        t2 = [sb("t2_%d" % i, [128, 512]) for i in range(2)]
        xt = [sb("xt%d" % i, [128, 1024]) for i in range(2)]
        pre = sb("pre", [128, 1024])
        junk = sb("junk", [128, 1024])
        x1o = [sb("x1o0", [128, 1024])] * 2
        lnw = sb("lnw", [128, 1024])
        lnb = sb("lnb", [128, 1024])
        sm = sb("sm", [128, 8])
        epsb = sb("epsb", [128, 1])
        P.op("pool", lambda e: e.memset(epsb[:], LN_EPS), writes=["epsb"])
        P.dma("sp", lnw[:], d["ln1_w"].partition_broadcast(128), writes=["lnw"])
        P.dma("sp", lnb[:], d["ln1_b"].partition_broadcast(128), writes=["lnb"])
        for k in range(4):
            P.dma("pool", zhT[:, k, :], d["ZH_fm"][k * 128:(k + 1) * 128, :], reads=["ZH_fm"], writes=["zhT"])
            P.dma("pool", zrT[:, k, :], d["ZR_fm"][k * 128:(k + 1) * 128, :], reads=["ZR_fm"], writes=["zrT"])
        P.dma("pool", why[:], d["w_hy_out"].rearrange("(k p) c -> p k c", p=128), writes=["why"])
        P.dma("pool", wrw[:], d["w_rw_out"].rearrange("(k p) c -> p k c", p=128), writes=["wrw"])
        P.dma("pool", wo[:], d["w_o"].rearrange("(k p) c -> p k c", p=128), writes=["wo"])
        it = 0
        for cc in range(8):
            for tb in range(8):
                i2 = it % 2
                it += 1
                ts_ = slice(tb * 512, (tb + 1) * 512)
                P.dma("sp", gh[i2][:], d["GT_fm"][cc * 128:(cc + 1) * 128, ts_], reads=["GT_fm"], writes=["gh%d" % i2])
                P.dma("sp", gr[i2][:], d["GT_fm"][1024 + cc * 128:1024 + (cc + 1) * 128, ts_], reads=["GT_fm"], writes=["gr%d" % i2])
                bh, bhk = kb.bank()
                br, brk = kb.bank()
                for k in range(4):
                    P.op("pe", lambda e, bh=bh, k=k, cc=cc, ts_=ts_: e.matmul(
                        bh[:, :], lhsT=why[:, k, cc * 128:(cc + 1) * 128], rhs=zhT[:, k, ts_], start=(k == 0), stop=(k == 3)),
                        reads=["why", "zhT"], writes=[bhk])
                for k in range(4):
                    P.op("pe", lambda e, br=br, k=k, cc=cc, ts_=ts_: e.matmul(
                        br[:, :], lhsT=wrw[:, k, cc * 128:(cc + 1) * 128], rhs=zrT[:, k, ts_], start=(k == 0), stop=(k == 3)),
                        reads=["wrw", "zrT"], writes=[brk])
                P.op("dve", lambda e, i2=i2, bh=bh: e.tensor_tensor(out=t1[i2][:], in0=bh[:, :], in1=gh[i2][:], op=ALU.mult),
                     reads=[bhk, "gh%d" % i2], writes=["t1_%d" % i2])
                P.op("dve", lambda e, i2=i2, br=br: e.tensor_tensor(out=t2[i2][:], in0=br[:, :], in1=gr[i2][:], op=ALU.mult),
                     reads=[brk, "gr%d" % i2], writes=["t2_%d" % i2])
                P.op("pool", lambda e, i2=i2, cc=cc, ts_=ts_: e.tensor_tensor(out=mT[:, cc, ts_], in0=t1[i2][:], in1=t2[i2][:], op=ALU.add),
                     reads=["t1_%d" % i2, "t2_%d" % i2], writes=[("mT", cc, tb)])
        mkeys = [("mT", cc, tb) for cc in range(8) for tb in range(8)]
        for blk in range(32):
            i2 = blk % 2
            P.dma("sp", xt[i2][:], d["x"][blk * 128:(blk + 1) * 128, :], writes=["xt%d" % i2])
            for nh in range(2):
                bk, bkk = kb.bank()
                for k in range(8):
                    P.op("pe", lambda e, bk=bk, k=k, blk=blk, nh=nh: e.matmul(
                        bk[:, :], lhsT=mT[:, k, blk * 128:(blk + 1) * 128], rhs=wo[:, k, nh * 512:(nh + 1) * 512],
                        start=(k == 0), stop=(k == 7)), reads=mkeys + ["wo"] if k == 0 else ["wo"], writes=[bkk])
                P.op("dve", lambda e, bk=bk, i2=i2, nh=nh: e.scalar_tensor_tensor(
                    out=pre[:, nh * 512:(nh + 1) * 512], in0=xt[i2][:, nh * 512:(nh + 1) * 512], scalar=ALPHA,
                    in1=bk[:, :], op0=ALU.mult, op1=ALU.add), reads=[bkk, "xt%d" % i2], writes=["Lpre"])
            layer_norm_rows(P, nc, pre, x1o[i2], lnw, lnb, sm, junk, epsb, "L")
            P.dma("sp", d["X1_tm"][blk * 128:(blk + 1) * 128, :], x1o[i2][:], reads=["Lout"], writes=["X1_tm"])
        print("stage6", P.emit_stage())


def stage7(kb):
    nc, P = kb.nc, kb.P
    d = kb.d
    with contextlib.ExitStack() as s:
        sb = lambda n, sh, dt=F32: K.sb(s, nc, n, sh, dt)
        x1q = sb("x1q", [128, 8, 1024])
        x1T = sb("x1T", [128, 8, 1024], BF16)
        hT = sb("hT", [128, 22, 1024], BF16)
        wdn = sb("wdn", [128, 22, 1024], BF16)
        wg = [sb("wg%d" % i, [128, 8, 128], BF16) for i in range(2)]
        wu = [sb("wu%d" % i, [128, 8, 128], BF16) for i in range(2)]
        sg = [sb("sg%d" % i, [128, 512]) for i in range(2)]
        pre = sb("pre7", [128, 1024])
        junk = sb("junk7", [128, 1024])
        xo = sb("xo7", [128, 1024])
        lnw = sb("lnw7", [128, 1024])
        lnb = sb("lnb7", [128, 1024])
        sm = sb("sm7", [128, 8])
        epsb = sb("epsb7", [128, 1])
        ident = sb("ident7", [128, 128])
        P.dma("sp", ident[:], d["ident"], writes=["ident"])
        P.op("pool", lambda e: e.memset(epsb[:], LN_EPS), writes=["epsb"])
        P.dma("sp", lnw[:], d["ln2_w"].partition_broadcast(128), writes=["lnw"])
        P.dma("sp", lnb[:], d["ln2_b"].partition_broadcast(128), writes=["lnb"])
        for f in range(22):
            P.dma("pool", wdn[:, f, :], d["ffn_w_down"][f * 128:(f + 1) * 128, :], writes=["wdn"])
        wgv = d["ffn_w_gate"].rearrange("(k p) f -> p k f", p=128)
        wuv = d["ffn_w_up"].rearrange("(k p) f -> p k f", p=128)
        wi = 0
        for q in range(4):
            for blk in range(8):
                r0 = q * 1024 + blk * 128
                P.dma("sp", x1q[:, blk, :], d["X1_tm"][r0:r0 + 128, :], reads=["X1_tm"], writes=[("x1q", blk)])
            for blk in range(8):
                for dc0 in range(0, 8, 4):
                    bk, bkk = kb.bank()
                    for j in range(4):
                        dc = dc0 + j
                        P.op("pe", lambda e, bk=bk, blk=blk, dc=dc, j=j: e.transpose(
                            bk[:, j * 128:(j + 1) * 128], x1q[:, blk, dc * 128:(dc + 1) * 128], ident[:, :]),
                            reads=[("x1q", blk), "ident"], writes=[bkk])
                    outap = x1T[:, dc0:dc0 + 4, blk * 128:(blk + 1) * 128]
                    inap = bk[:, :].rearrange("p (j t) -> p j t", j=4)
                    if (blk + dc0 // 4) % 2 == 0:
                        P.op("act", lambda e, outap=outap, inap=inap: e.copy(outap, inap), reads=[bkk], writes=[("x1T", blk, dc0)])
                    else:
                        P.op("dve", lambda e, outap=outap, inap=inap: e.tensor_copy(outap, inap), reads=[bkk], writes=[("x1T", blk, dc0)])
            xkeys = [("x1T", blk, dc0) for blk in range(8) for dc0 in (0, 4)]
            for f in range(22):
                i2 = wi % 2
                wi += 1
                P.dma("pool", wg[i2][:], wgv[:, :, f * 128:(f + 1) * 128], writes=["wg%d" % i2])
                P.dma("pool", wu[i2][:], wuv[:, :, f * 128:(f + 1) * 128], writes=["wu%d" % i2])
                for tb in range(2):
                    ts_ = slice(tb * 512, (tb + 1) * 512)
                    bg, bgk = kb.bank()
                    bu, buk = kb.bank()
                    for k in range(8):
                        P.op("pe", lambda e, bg=bg, k=k, i2=i2, ts_=ts_: e.matmul(
                            bg[:, :], lhsT=wg[i2][:, k, :], rhs=x1T[:, k, ts_], start=(k == 0), stop=(k == 7)),
                            reads=(xkeys if k == 0 else []) + ["wg%d" % i2], writes=[bgk])
                    for k in range(8):
                        P.op("pe", lambda e, bu=bu, k=k, i2=i2, ts_=ts_: e.matmul(
                            bu[:, :], lhsT=wu[i2][:, k, :], rhs=x1T[:, k, ts_], start=(k == 0), stop=(k == 7)),
                            reads=(xkeys if k == 0 else []) + ["wu%d" % i2], writes=[buk])
                    P.op("act", lambda e, bg=bg, tb=tb: e.activation(out=sg[tb][:], in_=bg[:, :], func=AF.Silu),
                         reads=[bgk], writes=["sg%d" % tb])
                    P.op("dve", lambda e, bu=bu, tb=tb, f=f, ts_=ts_: e.tensor_tensor(
                        out=hT[:, f, ts_], in0=bu[:, :], in1=sg[tb][:], op=ALU.mult),
                        reads=[buk, "sg%d" % tb], writes=[("hT", f, tb)])
            hkeys = [("hT", f, tb) for f in range(22) for tb in range(2)]
            for blk in range(8):
                for nh in range(2):
                    bk, bkk = kb.bank()
                    for f in range(22):
                        P.op("pe", lambda e, bk=bk, f=f, blk=blk, nh=nh: e.matmul(
                            bk[:, :], lhsT=hT[:, f, blk * 128:(blk + 1) * 128], rhs=wdn[:, f, nh * 512:(nh + 1) * 512],
                            start=(f == 0), stop=(f == 21)), reads=(hkeys if f == 0 else []) + ["wdn"], writes=[bkk])
                    P.op("dve", lambda e, bk=bk, blk=blk, nh=nh: e.scalar_tensor_tensor(
                        out=pre[:, nh * 512:(nh + 1) * 512], in0=x1q[:, blk, nh * 512:(nh + 1) * 512], scalar=ALPHA,
                        in1=bk[:, :], op0=ALU.mult, op1=ALU.add), reads=[bkk, ("x1q", blk)], writes=["Mpre"])
                layer_norm_rows(P, nc, pre, xo, lnw, lnb, sm, junk, epsb, "M")
                r0 = q * 1024 + blk * 128
                P.dma("sp", d["out"][r0:r0 + 128, :], xo[:], reads=["Mout"], writes=["out"])
        print("stage7", P.emit_stage())


def stage2(kb):
    nc, P = kb.nc, kb.P
    d = kb.d
    with contextlib.ExitStack() as s:
        sb = lambda n, sh, dt=F32: K.sb(s, nc, n, sh, dt)
        feats = sb("feats", [33, 4096])
        hbuf = [sb("hb%d" % i, [64, 4096]) for i in range(2)]
        ws = [sb("fw1", [33, 64]), sb("fw2", [64, 64]), sb("fw3", [64, 64])]
        fb = sb("fb", [64, 3])
        sf = sb("sf", [64, 3])
        sfb = sb("sfb", [64, 3])
        arg = [sb("arg%d" % i, [64, 512]) for i in range(2)]
        kq = [sb("kq%d" % i, [64, 512]) for i in range(2)]
        P.dma("sp", feats[:], d["featsT"], writes=["feats"])
        for i, nm in enumerate(("hy_filt_w1", "hy_filt_w2", "hy_filt_w3")):
            P.dma("sp", ws[i][:], d[nm], writes=["fw%d" % i])
        P.dma("sp", fb[:], d["hy_fb"], writes=["fb"])
        P.dma("sp", sf[:], d["hy_sf"], writes=["sf"])
        P.op("dve", lambda e: e.tensor_tensor(out=sfb[:], in0=sf[:], in1=fb[:], op=ALU.mult), reads=["fb", "sf"], writes=["sfb"])
        it = 0
        for l in range(3):
            kdim = 33 if l == 0 else 64
            hin = feats if l == 0 else hbuf[(l - 1) % 2]
            hink = "feats" if l == 0 else "hb%d" % ((l - 1) % 2)
            hout = hbuf[l % 2]
            houtk = "hb%d" % (l % 2)
            for tb in range(8):
                i2 = it % 2
                it += 1
                ts_ = slice(tb * 512, (tb + 1) * 512)
                bk, bkk = kb.bank()
                P.op("pe", lambda e, bk=bk, l=l, kdim=kdim, hin=hin, ts_=ts_: e.matmul(
                    bk[0:64, :], lhsT=ws[l][0:kdim, :], rhs=hin[0:kdim, ts_], start=True, stop=True),
                    reads=["fw%d" % l] + [(hink, tb)] + ([hink] if l == 0 else []), writes=[bkk])
                P.op("dve", lambda e, bk=bk, l=l, i2=i2: e.tensor_scalar(
                    out=arg[i2][:], in0=bk[0:64, :], scalar1=sf[:, l:l + 1], scalar2=sfb[:, l:l + 1], op0=ALU.mult, op1=ALU.add),
                    reads=[bkk, "sf", "sfb"], writes=["arg%d" % i2])
                P.op("dve", lambda e, i2=i2: e.tensor_scalar(
                    out=kq[i2][:], in0=arg[i2][:], scalar1=1.0 / (2 * PI), scalar2=MAGIC, op0=ALU.mult, op1=ALU.add),
                    reads=["arg%d" % i2], writes=["kq%d" % i2])
                P.op("dve", lambda e, i2=i2: e.tensor_scalar(
                    out=kq[i2][:], in0=kq[i2][:], scalar1=-MAGIC, scalar2=None, op0=ALU.add),
                    reads=["kq%d" % i2], writes=["kq%d" % i2])
                P.op("dve", lambda e, i2=i2: e.scalar_tensor_tensor(
                    out=arg[i2][:], in0=kq[i2][:], scalar=-2 * PI, in1=arg[i2][:], op0=ALU.mult, op1=ALU.add),
                    reads=["kq%d" % i2, "arg%d" % i2], writes=["arg%d" % i2])
                P.op("act", lambda e, i2=i2, hout=hout, ts_=ts_: e.activation(out=hout[:, ts_], in_=arg[i2][:], func=AF.Sin),
                     reads=["arg%d" % i2], writes=[(houtk, tb)])
        P.dma("sp", d["H3T"], hbuf[0][:], reads=[("hb0", tb) for tb in range(8)], writes=["H3T"])
        print("stage2", P.emit_stage())


CH7 = [(0, 7), (7, 7), (14, 7), (21, 7), (28, 4)]


def stage3(kb):
    nc, P = kb.nc, kb.P
    d = kb.d
    with contextlib.ExitStack() as s:
        sb = lambda n, sh, dt=F32: K.sb(s, nc, n, sh, dt)
        h3T = sb("h3T", [64, 4096])
        fw4 = sb("fw4", [64, 2048])
        F1 = sb("F1", [64, 130])
        TWf = sb("TWf", [64, 2, 65])
        S3 = sb("S3", [64, 10, 128])
        G = sb("G", [128, 128])
        TWi = sb("TWi", [65, 2, 64])
        I3t = sb("I3t", [65, 2, 64])
        skipb = [sb("skipb%d" % o, [128, 32]) for o in range(2)]
        win = sb("win", [64, 64, 32])
        hs = [sb("hs%d" % i, [64, 64, 32]) for i in range(2)]
        Zt = sb("Zt", [64, 64, 32])
        x1t = sb("x1t", [64, 64, 32])
        x2f = sb("x2f", [32, 4096])
        W1 = sb("W1", [128, 4160])
        W2 = sb("W2", [128, 4160])
        Bb = sb("Bb", [128, 4160])
        T1 = sb("T1", [128, 2080])
        T2 = sb("T2", [128, 2080])
        Ha = sb("Ha", [128, 2080])
        Hb = sb("Hb", [128, 2080])
        Y = sb("Y", [128, 2080])
        tmp = sb("tmp3", [128, 512])
        P.dma("sp", h3T[:], d["H3T"], reads=["H3T"], writes=["h3T"])
        P.dma("sp", fw4[:], d["hy_filt_w4"], writes=["fw4"])
        for nm, t_ in (("F1", F1), ("TWf", TWf), ("S3", S3), ("G", G), ("TWi", TWi), ("I3t", I3t)):
            P.dma("sp", t_[:], d[nm], writes=["tabs"])
        h3v = h3T[:, :].rearrange("p (b a) -> p a b", a=64)
        TWfc = TWf[:, 0, :].unsqueeze(1).to_broadcast([64, 32, 65])
        TWfs = TWf[:, 1, :].unsqueeze(1).to_broadcast([64, 32, 65])
        TWic = TWi[:, 0, :].unsqueeze(1).to_broadcast([65, 32, 64])
        TWis = TWi[:, 1, :].unsqueeze(1).to_broadcast([65, 32, 64])
        A3 = W1[0:64, :].rearrange("p (c k) -> p c k", k=130)
        E3 = W1[0:65, 0:4096].rearrange("p (c k) -> p c k", k=128)
        Et4 = W2[0:65, 0:4096].rearrange("p (r c k) -> p r c k", r=2, c=32)
        t1f = T1[0:64, :].rearrange("p (c k) -> p c k", k=65)
        t2f = T2[0:64, :].rearrange("p (c k) -> p c k", k=65)
        t1i = T1[0:65, 0:2048].rearrange("p (c k) -> p c k", k=64)
        t2i = T2[0:65, 0:2048].rearrange("p (c k) -> p c k", k=64)
        Y3 = Y[:, :].rearrange("p (c k) -> p c k", k=65)

        def fwd_AB(sig, sigkeys, Bt, Bkey):
            B4 = Bt[0:64, :].rearrange("p (r c k) -> p r c k", r=2, c=32)
            akeys = []
            for c0 in range(0, 32, 3):
                n = min(3, 32 - c0)
                bk, bkk = kb.bank()
                for j in range(n):
                    P.op("pe", lambda e, bk=bk, j=j, c=c0 + j: e.matmul(
                        bk[0:64, j * 130:(j + 1) * 130], lhsT=sig[:, :, c], rhs=F1[:, :], start=True, stop=True),
                        reads=list(sigkeys) + ["tabs"], writes=[bkk])
                P.op("act", lambda e, bk=bk, c0=c0, n=n: e.copy(
                    A3[:, c0:c0 + n, :], bk[0:64, 0:n * 130].rearrange("p (c k) -> p c k", k=130)),
                    reads=[bkk, "W1tokD", "W1tokP"], writes=[("W1", c0)])
                akeys.append(("W1", c0))
            Are, Aim = A3[:, :, 0:65], A3[:, :, 65:130]
            P.op("dve", lambda e: e.tensor_tensor(out=t1f, in0=Are, in1=TWfc, op=ALU.mult), reads=akeys + ["tabs"], writes=["T1"])
            P.op("pool", lambda e: e.tensor_tensor(out=t2f, in0=Aim, in1=TWfs, op=ALU.mult), reads=akeys + ["tabs"], writes=["T2"])
            P.op("dve", lambda e: e.tensor_tensor(out=B4[:, 0], in0=t1f, in1=t2f, op=ALU.add), reads=["T1", "T2"], writes=[(Bkey, 0)])
            P.op("pool", lambda e: e.tensor_tensor(out=B4[:, 1], in0=Aim, in1=TWfc, op=ALU.mult), reads=akeys + ["tabs"], writes=[(Bkey, 1), "W1tokP"])
            P.op("dve", lambda e: e.tensor_tensor(out=t1f, in0=Are, in1=TWfs, op=ALU.mult), reads=akeys + ["tabs"], writes=["T1", "W1tokD"])
            P.op("pool", lambda e: e.tensor_tensor(out=B4[:, 1], in0=B4[:, 1], in1=t1f, op=ALU.subtract),
                 reads=["T1", (Bkey, 1)], writes=[(Bkey, 1)])

        def bcols(Bt, r, c0, n):
            return Bt[0:64, r * 2080 + c0 * 65: r * 2080 + (c0 + n) * 65]

        for g in range(16):
            c0g = g * 32
            P.dma("sp", win[:].rearrange("p a c -> p (a c)"), d["win"][g], writes=["win"])
            P.dma("sp", Zt[:].rearrange("p a c -> p (a c)"), d["UHg"][g], reads=["UHg"], writes=["Zt"])
            P.dma("sp", x1t[:].rearrange("p a c -> p (a c)"), d["UHg"][16 + g], reads=["UHg"], writes=["x1t"])
            P.dma("sp", x2f[:], d["X2_fm"][c0g:c0g + 32, :], reads=["X2_fm"], writes=["x2f"])
            for o in range(2):
                P.dma("sp", skipb[o][:], d["hy_skip"][o:o + 1, c0g:c0g + 32].partition_broadcast(128), writes=["skipb%d" % o])
            for o in range(2):
                for dd in range(2):
                    col0 = o * 1024 + dd * 512 + c0g
                    for a0 in range(0, 64, 16):
                        bk, bkk = kb.bank()
                        for ai in range(16):
                            P.op("pe", lambda e, bk=bk, ai=ai, a=a0 + ai, col0=col0: e.matmul(
                                bk[0:64, ai * 32:(ai + 1) * 32], lhsT=h3v[:, a, :], rhs=fw4[:, col0:col0 + 32], start=True, stop=True),
                                reads=["h3T", "fw4"], writes=[bkk])
                        P.op("dve", lambda e, bk=bk, dd=dd, a0=a0: e.tensor_tensor(
                            out=hs[dd][:, a0:a0 + 16, :], in0=bk[0:64, :].rearrange("p (a c) -> p a c", c=32),
                            in1=win[:, a0:a0 + 16, :], op=ALU.mult), reads=[bkk, "win"], writes=[("hs%d" % dd, a0)])
                hk = lambda dd: [("hs%d" % dd, a0) for a0 in range(0, 64, 16)]
                fwd_AB(hs[0], hk(0), W2, "W2")
                fwd_AB(hs[1], hk(1), Bb, "Bb")
                for (cs, n) in CH7:
                    ncol = n * 65
                    ba, bak = kb.bank()
                    bb, bbk = kb.bank()
                    seq = [(4, W2, 0, "W2"), (5, W2, 1, "W2"), (4, Bb, 0, "Bb"), (5, Bb, 1, "Bb")]
                    for i, (ti, Bt, r, Bk) in enumerate(seq):
                        P.op("pe", lambda e, ba=ba, ti=ti, Bt=Bt, r=r, cs=cs, n=n, ncol=ncol, i=i: e.matmul(
                            ba[:, 0:ncol], lhsT=S3[:, ti, :], rhs=bcols(Bt, r, cs, n), start=(i == 0), stop=(i == 3)),
                            reads=[(Bk, r), "tabs"], writes=[bak])
                    seq = [(6, W2, 0, "W2"), (7, W2, 1, "W2"), (8, Bb, 0, "Bb"), (9, Bb, 1, "Bb")]
                    for i, (ti, Bt, r, Bk) in enumerate(seq):
                        P.op("pe", lambda e, bb=bb, ti=ti, Bt=Bt, r=r, cs=cs, n=n, ncol=ncol, i=i: e.matmul(
                            bb[:, 0:ncol], lhsT=S3[:, ti, :], rhs=bcols(Bt, r, cs, n), start=(i == 0), stop=(i == 3)),
                            reads=[(Bk, r), "tabs"], writes=[bbk])
                    P.op("dve", lambda e, ba=ba, o=o, cs=cs, n=n, ncol=ncol: e.tensor_tensor(
                        out=Ha[:, cs * 65:cs * 65 + ncol].rearrange("p (c k) -> p c k", k=65),
                        in0=ba[:, 0:ncol].rearrange("p (c k) -> p c k", k=65),
                        in1=skipb[o][:, cs:cs + n].unsqueeze(2).to_broadcast([128, n, 65]), op=ALU.add),
                        reads=[bak, "skipb%d" % o], writes=[("Ha", cs)])
                    P.op("act", lambda e, bb=bb, cs=cs, ncol=ncol: e.copy(Hb[:, cs * 65:cs * 65 + ncol], bb[:, 0:ncol]),
                         reads=[bbk], writes=[("Hb", cs)])
                sig = Zt
                fwd_AB(sig, ["Zt"], W2, "W2")
                for (cs, n) in CH7:
                    ncol = n * 65
                    bs, bsk = kb.bank()
                    bw, bwk = kb.bank()
                    for i, (ti, r) in enumerate([(0, 0), (1, 1)]):
                        P.op("pe", lambda e, bs=bs, ti=ti, r=r, cs=cs, n=n, ncol=ncol, i=i: e.matmul(
                            bs[:, 0:ncol], lhsT=S3[:, ti, :], rhs=bcols(W2, r, cs, n), start=(i == 0), stop=(i == 1)),
                            reads=[("W2", r), "tabs"], writes=[bsk])
                    for i, (ti, r) in enumerate([(2, 0), (3, 1)]):
                        P.op("pe", lambda e, bw=bw, ti=ti, r=r, cs=cs, n=n, ncol=ncol, i=i: e.matmul(
                            bw[:, 0:ncol], lhsT=S3[:, ti, :], rhs=bcols(W2, r, cs, n), start=(i == 0), stop=(i == 1)),
                            reads=[("W2", r), "tabs"], writes=[bwk])
                    ysl = Y[:, cs * 65:cs * 65 + ncol]
                    P.op("dve", lambda e, bw=bw, cs=cs, ncol=ncol: e.tensor_tensor(
                        out=tmp[:, 0:ncol], in0=bw[:, 0:ncol], in1=Hb[:, cs * 65:cs * 65 + ncol], op=ALU.mult),
                        reads=[bwk, ("Hb", cs)], writes=["tmp3"])
                    P.op("dve", lambda e, bs=bs, ysl=ysl, cs=cs, ncol=ncol: e.tensor_tensor(
                        out=ysl, in0=bs[:, 0:ncol], in1=Ha[:, cs * 65:cs * 65 + ncol], op=ALU.mult),
                        reads=[bsk, ("Ha", cs)], writes=[("Y", cs)])
                    P.op("pool", lambda e, ysl=ysl, ncol=ncol: e.tensor_tensor(out=ysl, in0=ysl, in1=tmp[:, 0:ncol], op=ALU.add),
                         reads=[("Y", cs), "tmp3"], writes=[("Y", cs)])
                ykeys = [("Y", cs) for (cs, n) in CH7]
                ekeys = []
                for c0 in range(0, 32, 4):
                    bk, bkk = kb.bank()
                    for j in range(4):
                        P.op("pe", lambda e, bk=bk, j=j, c=c0 + j: e.matmul(
                            bk[0:65, j * 128:(j + 1) * 128], lhsT=Y3[:, c, :], rhs=G[:, :], start=True, stop=True),
                            reads=ykeys + ["tabs"], writes=[bkk])
                    P.op("act", lambda e, bk=bk, c0=c0: e.copy(
                        E3[:, c0:c0 + 4, :], bk[0:65, :].rearrange("p (c k) -> p c k", k=128)),
                        reads=[bkk, "W1tokD", "W1tokP"], writes=[("W1", c0)])
                    ekeys.append(("W1", c0))
                Ere, Eim = E3[:, :, 0:64], E3[:, :, 64:128]
                P.op("dve", lambda e: e.tensor_tensor(out=t1i, in0=Ere, in1=TWic, op=ALU.mult), reads=ekeys + ["tabs"], writes=["T1"])
                P.op("pool", lambda e: e.tensor_tensor(out=t2i, in0=Eim, in1=TWis, op=ALU.mult), reads=ekeys + ["tabs"], writes=["T2"])
                P.op("dve", lambda e: e.tensor_tensor(out=Et4[:, 0], in0=t1i, in1=t2i, op=ALU.subtract),
                     reads=["T1", "T2"], writes=[("W2", 0)])
                P.op("pool", lambda e: e.tensor_tensor(out=Et4[:, 1], in0=Eim, in1=TWic, op=ALU.mult),
                     reads=ekeys + ["tabs"], writes=[("W2", 1), "W1tokP"])
                P.op("dve", lambda e: e.tensor_tensor(out=t1i, in0=Ere, in1=TWis, op=ALU.mult), reads=ekeys + ["tabs"], writes=["T1", "W1tokD"])
                P.op("pool", lambda e: e.tensor_tensor(out=Et4[:, 1], in0=Et4[:, 1], in1=t1i, op=ALU.add),
                     reads=["T1", ("W2", 1)], writes=[("W2", 1)])
                if o == 0:
                    for c8 in range(4):
                        bk, bkk = kb.bank()
                        for r in range(2):
                            P.op("pe", lambda e, bk=bk, r=r, c8=c8: e.matmul(
                                bk[0:64, :], lhsT=I3t[:, r, :], rhs=W2[0:65, r * 2048 + c8 * 512: r * 2048 + (c8 + 1) * 512], start=(r == 0), stop=(r == 1)),
                                reads=[("W2", r), "tabs"], writes=[bkk])
                        cs_ = slice(c8 * 8, (c8 + 1) * 8)
                        P.op("dve", lambda e, bk=bk, cs_=cs_: e.tensor_tensor(
                            out=Zt[:, :, cs_].rearrange("p a c -> p c a"),
                            in0=bk[0:64, :].rearrange("p (c a) -> p c a", a=64),
                            in1=x1t[:, :, cs_].rearrange("p a c -> p c a"), op=ALU.mult),
                            reads=[bkk, "x1t", "Zt"], writes=["Zt"])
                else:
                    x2v = x2f[:, :].rearrange("p (b a) -> p a b", a=64)
                    for a0 in range(0, 64, 8):
                        bk, bkk = kb.bank()
                        for ai in range(8):
                            for r in range(2):
                                P.op("pe", lambda e, bk=bk, ai=ai, a=a0 + ai, r=r: e.matmul(
                                    bk[0:32, ai * 64:(ai + 1) * 64], lhsT=Et4[:, r, :, a], rhs=I3t[:, r, :],
                                    start=(r == 0), stop=(r == 1)), reads=[("W2", r), "tabs"], writes=[bkk])
                        P.op("dve", lambda e, bk=bk, a0=a0: e.tensor_tensor(
                            out=x2v[:, a0:a0 + 8, :], in0=bk[0:32, :].rearrange("p (a b) -> p a b", b=64),
                            in1=x2v[:, a0:a0 + 8, :], op=ALU.mult), reads=[bkk, "x2f"], writes=["x2f"])
                    P.dma("sp", d["ZH_fm"][c0g:c0g + 32, :], x2f[:], reads=["x2f"], writes=["ZH_fm"])
        print("stage3", P.emit_stage())


DECAY_C = -math.exp(-0.5)


def stage4(kb):
    nc, P = kb.nc, kb.P
    d = kb.d
    with contextlib.ExitStack() as s:
        sb = lambda n, sh, dt=F32: K.sb(s, nc, n, sh, dt)
        ident = sb("ident", [128, 128])
        onesblk = sb("onesblk", [128, 128])
        mSU, mIU, mSM, Istk = sb("mSU", [128, 512]), sb("mIU", [128, 512]), sb("mSM", [128, 512]), sb("Istk", [128, 512])
        cmask = sb("cmask", [128, 1024])
        w2s, a2s = sb("w2s", [128, 512]), sb("a2s", [128, 512])
        w0s, a0s = sb("w0s", [128, 2, 4]), sb("a0s", [128, 2, 4])
        kks, kas = sb("kks", [128, 4]), sb("kas", [128, 4])
        S0T = sb("S0T", [128, 512])
        U = [sb("U%d" % i, [128, 14, 256]) for i in range(2)]
        nm7 = ("Qh", "Rh", "Bh", "Kh", "Bt", "Kt")
        OUT = [{n: sb("%s%d" % (n, i), [128, 4, 256]) for n in nm7} for i in range(2)]
        gE = [sb("gE%d" % i, [128, 16]) for i in range(2)]
        Yout = [sb("Yout%d" % i, [128, 4, 256]) for i in range(2)]
        tn = ("thT", )
        thT = sb("thT", [128, 256])
        TM = {n: sb("p_" + n, [128, 4, 256]) for n in ("lw", "a", "tk", "sq", "kk", "kd", "b", "cum", "cum2", "e")}
        totT = sb("totT", [128, 16])
        st_names = ("Vt", "Btt", "Ktt", "Nrb", "Mak", "Nrk", "Xs", "SA", "tmpS",
                    "Pu0", "Pu1", "Pm0", "Pm1", "T0", "T1", "Tt0", "Tt1")
        ST = {n: sb("c_" + n, [128, 512]) for n in st_names}
        for nm, t_ in (("ident", ident), ("onesblk", onesblk), ("mSU", mSU), ("mIU", mIU), ("mSM", mSM), ("Istk", Istk), ("cmask", cmask)):
            P.dma("sp", t_[:], d[nm], writes=["consts"])
        P.dma("sp", w2s[:], d["rw_w2"], writes=["consts"])
        P.dma("sp", a2s[:], d["rw_a2"], writes=["consts"])
        P.dma("sp", w0s[:], d["rw_w0"], writes=["consts"])
        P.dma("sp", a0s[:], d["rw_a0"], writes=["consts"])
        P.dma("sp", kks[:], d["rw_k_k"], writes=["consts"])
        P.dma("sp", kas[:], d["rw_k_a"], writes=["consts"])
        P.op("pool", lambda e: e.memset(S0T[:], 0.0), writes=["S0T"])
        urv = d["UR_fm"].rearrange("(j p) t -> p j t", p=128)
        kk_bc = kks[:, :].unsqueeze(2).to_broadcast([128, 4, 256])
        ka_bc = kas[:, :].unsqueeze(2).to_broadcast([128, 4, 256])
        fl = lambda t_: t_[:].rearrange("p h t -> p (h t)")

        def prep(dr, blk):
            Ud, Uk = U[dr], "U%d" % dr
            O = OUT[dr]
            ok = lambda n: "%s%d" % (n, dr)
            dR = slice(dr * 64, (dr + 1) * 64)
            t0 = blk * 256
            P.dma("sp", Ud[:, 0:7, :], urv[:, 0:7, t0:t0 + 256], reads=["UR_fm"], writes=[Uk])
            P.dma("sp", Ud[:, 7:14, :], urv[:, 7:14, t0:t0 + 256], reads=["UR_fm", Uk], writes=[Uk])
            r_, k_, v_ = Ud[:, 0:4, :], Ud[:, 4:8, :], Ud[:, 8:12, :]
            P.op("act", lambda e: e.activation(out=thT[dR, :], in_=Ud[dR, 12, :], func=AF.Tanh), reads=[Uk], writes=["thT"])
            for (wsb, rhs_ap, rkeys, w0t, outn) in ((w2s, thT[dR, :], ["thT"], w0s, "lw"), (a2s, Ud[dR, 13, :], [Uk], a0s, "a")):
                for half in range(2):
                    bk, bkk = kb.bank()
                    for j in range(2):
                        hq = half * 2 + j
                        P.op("pe", lambda e, bk=bk, j=j, hq=hq, wsb=wsb, rhs_ap=rhs_ap: e.matmul(
                            bk[:, j * 256:(j + 1) * 256], lhsT=wsb[dR, hq * 128:(hq + 1) * 128], rhs=rhs_ap,
                            start=True, stop=True, tile_position=(dr * 64, 0)), reads=rkeys + ["consts"], writes=[bkk])
                    for j in range(2):
                        hq = half * 2 + j
                        P.op("act", lambda e, bk=bk, j=j, hq=hq, w0t=w0t, outn=outn: e.activation(
                            out=TM[outn][:, hq, :], in_=bk[:, j * 256:(j + 1) * 256], func=AF.Sigmoid,
                            bias=w0t[:, dr, hq:hq + 1], scale=1.0), reads=[bkk, "consts"], writes=[("p_" + outn, hq)])
            lwk = [("p_lw", hq) for hq in range(4)]
            ak = [("p_a", hq) for hq in range(4)]
            P.op("dve", lambda e: e.tensor_scalar(out=fl(TM["lw"]), in0=fl(TM["lw"]), scalar1=DECAY_C, scalar2=None, op0=ALU.mult),
                 reads=lwk, writes=lwk)
            P.op("dve", lambda e: e.tensor_tensor(out=TM["tk"][:], in0=k_, in1=kk_bc, op=ALU.mult), reads=[Uk, "consts"], writes=["p_tk"])
            P.op("pool", lambda e: e.tensor_tensor(out=TM["sq"][:], in0=TM["tk"][:], in1=TM["tk"][:], op=ALU.mult), reads=["p_tk"], writes=["p_sq"])
            for half in range(2):
                bk, bkk = kb.bank()
                for j in range(2):
                    hq = half * 2 + j
                    P.op("pe", lambda e, bk=bk, j=j, hq=hq: e.matmul(
                        bk[:, j * 256:(j + 1) * 256], lhsT=onesblk[:, :], rhs=TM["sq"][:, hq, :], start=True, stop=True),
                        reads=["p_sq", "consts"], writes=[bkk])
                P.op("act", lambda e, bk=bk, half=half: e.activation(
                    out=TM["kk"][:, half * 2:half * 2 + 2, :], in_=bk[:, :].rearrange("p (h t) -> p h t", h=2), func=AF.Sqrt),
                    reads=[bkk], writes=[("p_kk", half)])
            kkk = [("p_kk", 0), ("p_kk", 1)]
            P.op("dve", lambda e: e.tensor_scalar(out=fl(TM["kk"]), in0=fl(TM["kk"]), scalar1=1e-12, scalar2=None, op0=ALU.max), reads=kkk, writes=kkk)
            P.op("dve", lambda e: e.reciprocal(fl(TM["kk"]), fl(TM["kk"])), reads=kkk, writes=kkk)
            P.op("pool", lambda e: e.tensor_tensor(out=TM["kk"][:], in0=TM["kk"][:], in1=TM["tk"][:], op=ALU.mult), reads=kkk + ["p_tk"], writes=kkk)
            P.op("dve", lambda e: e.scalar_tensor_tensor(out=TM["kd"][:], in0=TM["a"][:], scalar=-1.0, in1=ka_bc, op0=ALU.add, op1=ALU.mult),
                 reads=ak + ["consts"], writes=["p_kd"])
            P.op("dve", lambda e: e.scalar_tensor_tensor(out=TM["kd"][:], in0=TM["kd"][:], scalar=1.0, in1=k_, op0=ALU.add, op1=ALU.mult),
                 reads=["p_kd", Uk], writes=["p_kd"])
            P.op("pool", lambda e: e.tensor_tensor(out=TM["b"][:], in0=TM["kk"][:], in1=TM["a"][:], op=ALU.mult), reads=kkk + ak, writes=["p_b"])
            P.op("dve", lambda e: e.tensor_tensor_scan(out=fl(TM["cum"]), data0=cmask[:, :], data1=fl(TM["lw"]), initial=0.0,
                                                       op0=ALU.mult, op1=ALU.add), reads=lwk + ["consts"], writes=["p_cum"])
            cum3 = fl(TM["cum"]).rearrange("p (c i) -> p c i", i=64)
            P.op("act", lambda e: e.copy(totT[:, :], cum3[:, :, 63]), reads=["p_cum"], writes=["totT"])
            tot_bc = totT[:, :].unsqueeze(2).to_broadcast([128, 16, 64])
            c2 = fl(TM["cum2"]).rearrange("p (c i) -> p c i", i=64)
            if dr == 1:
                P.op("dve", lambda e: e.tensor_tensor(out=c2, in0=tot_bc, in1=cum3, op=ALU.subtract), reads=["totT", "p_cum"], writes=["p_cum2"])
                P.op("dve", lambda e: e.tensor_tensor(out=TM["cum"][:], in0=TM["cum2"][:], in1=TM["lw"][:], op=ALU.add),
                     reads=["p_cum2"] + lwk, writes=["p_cum"])
            P.op("act", lambda e: e.activation(out=gE[dr][:, :], in_=totT[:, :], func=AF.Exp), reads=["totT"], writes=["gE%d" % dr])
            P.op("act", lambda e: e.activation(out=TM["e"][:], in_=TM["cum"][:], func=AF.Exp), reads=["p_cum"], writes=["p_e"])
            P.op("pool", lambda e: e.tensor_tensor(out=O["Rh"][:], in0=r_, in1=TM["e"][:], op=ALU.mult), reads=[Uk, "p_e"], writes=[ok("Rh")])
            P.op("act", lambda e: e.activation(out=TM["e"][:], in_=TM["cum"][:], func=AF.Exp, scale=-1.0), reads=["p_cum"], writes=["p_e"])
            P.op("dve", lambda e: e.tensor_tensor(out=O["Bh"][:], in0=TM["b"][:], in1=TM["e"][:], op=ALU.mult), reads=["p_b", "p_e"], writes=[ok("Bh")])
            P.op("pool", lambda e: e.tensor_tensor(out=O["Kh"][:], in0=TM["kd"][:], in1=TM["e"][:], op=ALU.mult), reads=["p_kd", "p_e"], writes=[ok("Kh")])
            P.op("dve", lambda e: e.tensor_tensor(out=TM["cum2"][:], in0=TM["cum"][:], in1=TM["lw"][:], op=ALU.subtract),
                 reads=["p_cum"] + lwk, writes=["p_cum2"])
            P.op("act", lambda e: e.activation(out=TM["e"][:], in_=TM["cum2"][:], func=AF.Exp), reads=["p_cum2"], writes=["p_e"])
            P.op("pool", lambda e: e.tensor_tensor(out=O["Qh"][:], in0=TM["kk"][:], in1=TM["e"][:], op=ALU.mult), reads=kkk + ["p_e"], writes=[ok("Qh")])
            P.op("dve", lambda e: e.tensor_tensor(out=c2, in0=tot_bc, in1=fl(TM["cum"]).rearrange("p (c i) -> p c i", i=64), op=ALU.subtract),
                 reads=["totT", "p_cum"], writes=["p_cum2"])
            P.op("act", lambda e: e.activation(out=TM["e"][:], in_=TM["cum2"][:], func=AF.Exp), reads=["p_cum2"], writes=["p_e"])
            P.op("dve", lambda e: e.tensor_tensor(out=O["Bt"][:], in0=TM["b"][:], in1=TM["e"][:], op=ALU.mult), reads=["p_b", "p_e"], writes=[ok("Bt")])
            P.op("pool", lambda e: e.tensor_tensor(out=O["Kt"][:], in0=TM["kd"][:], in1=TM["e"][:], op=ALU.mult), reads=["p_kd", "p_e"], writes=[ok("Kt")])

        heads = [(dr, hq, hp) for dr in range(2) for hq in range(4) for hp in range(2)]

        def per_head(bk, bkk, fn_list, reads):
            nacc = len(fn_list)
            for (dr, hq, hp) in heads:
                hpR = slice(hp * 64, (hp + 1) * 64)
                cb = (dr * 4 + hq) * 64
                for i, (lf, rf) in enumerate(fn_list):
                    lt, rt = lf(dr, hq, hpR, cb), rf(dr, hq, hpR, cb)
                    P.op("pe", lambda e, bk=bk, hpR=hpR, cb=cb, lt=lt, rt=rt, i=i, hp=hp: e.matmul(
                        bk[hpR, cb:cb + 64], lhsT=lt, rhs=rt,
                        start=(i == 0), stop=(i == nacc - 1), tile_position=(hp * 64, hp * 64)),
                        reads=reads, writes=[bkk])

        stk = lambda name: (lambda dr, hq, hpR, cb: ST[name][hpR, cb:cb + 64])
        S0f = lambda dr, hq, hpR, cb: S0T[hpR, cb:cb + 64]
        Isf = lambda dr, hq, hpR, cb: Istk[hpR, cb:cb + 64]
        evi = [0]

        def evac(dst_key, dst_ap, bk, bkk, extra_reads=(), eng=None):
            e_ = eng or ("act" if evi[0] % 2 == 0 else "dve")
            evi[0] += 1
            if e_ == "act":
                P.op("act", lambda e: e.copy(dst_ap, bk[:, :]), reads=[bkk] + list(extra_reads), writes=[dst_key])
            else:
                P.op("dve", lambda e: e.tensor_copy(dst_ap, bk[:, :]), reads=[bkk] + list(extra_reads), writes=[dst_key])

        for j in range(getattr(kb, "rw_steps", 16)):
            blks = (j, 15 - j)
            for dr in range(2):
                prep(dr, blks[dr])
            for lc in range(4 if getattr(kb, "rw_phase", 9) > 0 else 0):
                lcd = (lc, 3 - lc)
                cs = [slice(lcd[dr] * 64, (lcd[dr] + 1) * 64) for dr in range(2)]
                fm = lambda name, cs=cs: (lambda dr, hq, hpR, cb, cs=cs: OUT[dr][name][hpR, hq, cs[dr]])
                Vf = lambda dr, hq, hpR, cb, cs=cs: U[dr][hpR, 8 + hq, cs[dr]]
                okeys = lambda n: ["%s0" % n, "%s1" % n]
                for (name, srcf, rk) in (("Vt", Vf, ["U0", "U1"]), ("Btt", fm("Bt"), okeys("Bt")), ("Ktt", fm("Kt"), okeys("Kt"))):
                    bk, bkk = kb.bank()
                    for (dr, hq, hp) in heads:
                        hpR = slice(hp * 64, (hp + 1) * 64)
                        cb = (dr * 4 + hq) * 64
                        src_ap = srcf(dr, hq, hpR, cb)
                        P.op("pe", lambda e, bk=bk, hpR=hpR, cb=cb, src_ap=src_ap, hp=hp: e.matmul(
                            bk[hpR, cb:cb + 64], lhsT=src_ap, rhs=ident[hpR, hpR], start=True, stop=True,
                            tile_position=(hp * 64, hp * 64)), reads=rk + ["consts"], writes=[bkk])
                    evac(name, ST[name][:, :], bk, bkk)
                if getattr(kb, "rw_phase", 9) < 2:
                    continue
                grams = (("Pu0", "Bh", "Qh", mSU), ("Nrb", "Bh", "Rh", mIU), ("Mak", "Kh", "Qh", mSU), ("Nrk", "Kh", "Rh", mIU), ("Pm0", "Qh", "Bh", mSM))
                for (dst, ln, rn, msk) in grams:
                    bk, bkk = kb.bank()
                    per_head(bk, bkk, [(fm(ln), fm(rn))], okeys(ln) + okeys(rn))
                    P.op("dve", lambda e, bk=bk, dst=dst, msk=msk: e.tensor_tensor(out=ST[dst][:, :], in0=bk[:, :], in1=msk[:, :], op=ALU.mult),
                         reads=[bkk, "consts"], writes=[dst])
                P.op("pool", lambda e: e.tensor_tensor(out=ST["T0"][:, :], in0=Istk[:, :], in1=ST["Pm0"][:, :], op=ALU.subtract),
                     reads=["Pm0", "consts"], writes=["T0"])
                P.op("pool", lambda e: e.tensor_tensor(out=ST["Tt0"][:, :], in0=Istk[:, :], in1=ST["Pu0"][:, :], op=ALU.subtract),
                     reads=["Pu0", "consts"], writes=["Tt0"])
                if getattr(kb, "rw_phase", 9) < 3:
                    continue
                cur = 0
                for lvl in range(1, 6):
                    nxt = 1 - cur
                    pu, pm, tt_, t_ = "Pu%d" % cur, "Pm%d" % cur, "Tt%d" % cur, "T%d" % cur
                    pun, pmn, ttn, tn_ = "Pu%d" % nxt, "Pm%d" % nxt, "Tt%d" % nxt, "T%d" % nxt
                    bk, bkk = kb.bank()
                    per_head(bk, bkk, [(stk(pm), stk(pu))], [pm, pu])
                    evac(pun, ST[pun][:, :], bk, bkk)
                    if lvl < 5:
                        bk, bkk = kb.bank()
                        per_head(bk, bkk, [(stk(pu), stk(pm))], [pm, pu])
                        evac(pmn, ST[pmn][:, :], bk, bkk)
                    bk, bkk = kb.bank()
                    per_head(bk, bkk, [(stk(t_), stk(pun)), (stk(t_), Isf)], [t_, pun, "consts"])
                    evac(ttn, ST[ttn][:, :], bk, bkk)
                    if lvl < 5:
                        bk, bkk = kb.bank()
                        per_head(bk, bkk, [(stk(tt_), stk(pmn)), (stk(tt_), Isf)], [tt_, pmn, "consts"])
                        evac(tn_, ST[tn_][:, :], bk, bkk)
                    cur = nxt
                ttf = "Tt%d" % cur
                if getattr(kb, "rw_phase", 9) < 4:
                    continue
                bk, bkk = kb.bank()
                per_head(bk, bkk, [(fm("Qh"), S0f), (stk("Mak"), stk("Vt"))], okeys("Qh") + ["S0T", "Mak", "Vt"])
                evac("Xs", ST["Xs"][:, :], bk, bkk)
                bk, bkk = kb.bank()
                per_head(bk, bkk, [(stk(ttf), stk("Xs"))], [ttf, "Xs"])
                P.op("act", lambda e, bk=bk: e.mul(ST["SA"][:, :], bk[:, :], -1.0), reads=[bkk], writes=["SA"])
                if getattr(kb, "rw_phase", 9) < 5:
                    continue
                bk1, bkk1 = kb.bank()
                per_head(bk1, bkk1, [(S0f, fm("Rh"))], okeys("Rh") + ["S0T"])
                bk, bkk = kb.bank()
                per_head(bk, bkk, [(stk("SA"), stk("Nrb")), (stk("Vt"), stk("Nrk"))], ["SA", "Nrb", "Vt", "Nrk"])
                P.op("act", lambda e, bk1=bk1: e.copy(ST["Xs"][:, :], bk1[:, :]), reads=[bkk1], writes=["Xs"])
                for dr in range(2):
                    src = bk[:, dr * 256:(dr + 1) * 256].rearrange("p (h t) -> p h t", h=4)
                    src2 = ST["Xs"][:, dr * 256:(dr + 1) * 256].rearrange("p (h t) -> p h t", h=4)
                    dst = Yout[dr][:, :, cs[dr]]
                    P.op("dve", lambda e, src=src, src2=src2, dst=dst: e.tensor_tensor(out=dst, in0=src, in1=src2, op=ALU.add),
                         reads=[bkk, "Xs"], writes=["Yout%d" % dr])
                if getattr(kb, "rw_phase", 9) < 6:
                    continue
                bk, bkk = kb.bank()
                per_head(bk, bkk, [(stk("Btt"), stk("SA")), (stk("Ktt"), stk("Vt"))], ["Btt", "SA", "Ktt", "Vt"])
                for dr in range(2):
                    gsl = gE[dr][:, :].rearrange("p (h c) -> p h c", c=4)[:, :, lcd[dr]].unsqueeze(2).to_broadcast([128, 4, 64])
                    sl = slice(dr * 256, (dr + 1) * 256)
                    P.op("pool", lambda e, gsl=gsl, sl=sl: e.tensor_tensor(
                        out=ST["tmpS"][:, sl].rearrange("p (h v) -> p h v", h=4), in0=S0T[:, sl].rearrange("p (h v) -> p h v", h=4),
                        in1=gsl, op=ALU.mult), reads=["S0T", "gE%d" % dr], writes=["tmpS"])
                P.op("dve", lambda e, bk=bk: e.tensor_tensor(out=S0T[:, :], in0=bk[:, :], in1=ST["tmpS"][:, :], op=ALU.add),
                     reads=[bkk, "tmpS", "S0T"], writes=["S0T"])
            for dr in range(2):
                t0 = blks[dr] * 256
                dst = d["YF_fm" if dr == 0 else "YB_fm"].rearrange("(h p) t -> p h t", p=128)[:, :, t0:t0 + 256]
                P.dma("sp", dst, Yout[dr][:], reads=["Yout%d" % dr], writes=["Y_fm%d" % dr])
        print("stage4", P.emit_stage())


def stage5(kb):
    nc, P = kb.nc, kb.P
    d = kb.d
    with contextlib.ExitStack() as s:
        sb = lambda n, sh, dt=F32: K.sb(s, nc, n, sh, dt)
        onesblk = sb("onesblk", [128, 128])
        a2s, g2s = sb("a2s", [128, 512]), sb("g2s", [128, 512])
        a0s = sb("a0s", [128, 2, 4])
        kas, rks, gnw, gnb = sb("kas", [128, 4]), sb("rks", [128, 4]), sb("gnw", [128, 4]), sb("gnb", [128, 4])
        epsb = sb("epsb", [128, 1])
        U5 = sb("U5", [128, 15, 512])
        yf, yb = sb("yf", [128, 4, 512]), sb("yb", [128, 4, 512])
        sq = sb("sq5", [128, 4, 512])
        af, ab = sb("af", [128, 4, 512]), sb("ab", [128, 4, 512])
        rb = sb("rb", [128, 4, 512])
        zr = sb("zr", [128, 4, 512])
        sgd = sb("sgd", [128, 512])
        mean = [sb("mean%d" % i, [128, 512]) for i in range(2)]
        var = [sb("var%d" % i, [128, 512]) for i in range(2)]
        bon = [sb("bon%d" % i, [128, 512]) for i in range(2)]
        P.dma("sp", onesblk[:], d["onesblk"], writes=["consts"])
        for nm, t_ in (("rw_a2", a2s), ("rw_g2", g2s), ("rw_a0", a0s), ("rw_k_a", kas), ("rw_r_k", rks), ("rw_gn_w", gnw), ("rw_gn_b", gnb)):
            P.dma("sp", t_[:], d[nm], writes=["consts"])
        P.op("pool", lambda e: e.memset(epsb[:], GN_EPS), writes=["consts"])
        urv = d["UR_fm"].rearrange("(j p) t -> p j t", p=128)
        yfv = d["YF_fm"].rearrange("(h p) t -> p h t", p=128)
        ybv = d["YB_fm"].rearrange("(h p) t -> p h t", p=128)
        zrv = d["ZR_fm"].rearrange("(h p) t -> p h t", p=128)
        ka_bc = kas[:, :].unsqueeze(2).to_broadcast([128, 4, 512])
        rk_bc = rks[:, :].unsqueeze(2).to_broadcast([128, 4, 512])
        for tb in range(8):
            ts_ = slice(tb * 512, (tb + 1) * 512)
            P.dma("sp", U5[:, 0:5, :], urv[:, 0:5, ts_], reads=["UR_fm"], writes=["U5"])
            P.dma("sp", U5[:, 5:10, :], urv[:, 5:10, ts_], reads=["UR_fm", "U5"], writes=["U5"])
            P.dma("sp", U5[:, 10:15, :], urv[:, 10:15, ts_], reads=["UR_fm", "U5"], writes=["U5"])
            P.dma("sp", yf[:], yfv[:, :, ts_], reads=["Y_fm0"], writes=["yf"])
            P.dma("sp", yb[:], ybv[:, :, ts_], reads=["Y_fm1"], writes=["yb"])
            r_, k_, v_ = U5[:, 0:4, :], U5[:, 4:8, :], U5[:, 8:12, :]
            P.op("pool", lambda e: e.tensor_tensor(out=yf[:], in0=yf[:], in1=yb[:], op=ALU.add), reads=["yf", "yb"], writes=["yf"])
            P.op("pool", lambda e: e.tensor_tensor(out=sq[:], in0=yf[:], in1=yf[:], op=ALU.mult), reads=["yf"], writes=["sq5"])
            for dr, at, atk in ((0, af, "af"), (1, ab, "ab")):
                dR = slice(dr * 64, (dr + 1) * 64)
                for hq in range(4):
                    bk, bkk = kb.bank()
                    P.op("pe", lambda e, bk=bk, hq=hq, dR=dR, dr=dr: e.matmul(
                        bk[:, :], lhsT=a2s[dR, hq * 128:(hq + 1) * 128], rhs=U5[dR, 13, :], start=True, stop=True,
                        tile_position=(dr * 64, 0)), reads=["U5", "consts"], writes=[bkk])
                    P.op("act", lambda e, bk=bk, hq=hq, at=at, dr=dr: e.activation(
                        out=at[:, hq, :], in_=bk[:, :], func=AF.Sigmoid, bias=a0s[:, dr, hq:hq + 1], scale=1.0),
                        reads=[bkk, "consts"], writes=[(atk, hq)])
            afk = [("af", h) for h in range(4)]
            abk = [("ab", h) for h in range(4)]
            P.op("pool", lambda e: e.tensor_tensor(out=af[:], in0=af[:], in1=ab[:], op=ALU.add), reads=afk + abk, writes=afk)
            P.op("dve", lambda e: e.scalar_tensor_tensor(out=af[:], in0=af[:], scalar=-2.0, in1=ka_bc, op0=ALU.add, op1=ALU.mult),
                 reads=afk + ["consts"], writes=afk)
            P.op("dve", lambda e: e.scalar_tensor_tensor(out=af[:], in0=af[:], scalar=2.0, in1=k_, op0=ALU.add, op1=ALU.mult),
                 reads=afk + ["U5"], writes=afk)
            P.op("pool", lambda e: e.tensor_tensor(out=rb[:], in0=r_, in1=rk_bc, op=ALU.mult), reads=["U5", "consts"], writes=["rb"])
            P.op("pool", lambda e: e.tensor_tensor(out=rb[:], in0=rb[:], in1=af[:], op=ALU.mult), reads=["rb"] + afk, writes=["rb"])
            P.op("act", lambda e: e.activation(out=sgd[:], in_=U5[:, 14, :], func=AF.Sigmoid), reads=["U5"], writes=["sgd"])
            for hq in range(4):
                i2 = hq % 2
                bm, bmk = kb.bank()
                bs, bsk = kb.bank()
                bb_, bbk = kb.bank()
                bg, bgk = kb.bank()
                P.op("pe", lambda e, bm=bm, hq=hq: e.matmul(bm[:, :], lhsT=onesblk[:, :], rhs=yf[:, hq, :], start=True, stop=True),
                     reads=["yf", "consts"], writes=[bmk])
                P.op("pe", lambda e, bs=bs, hq=hq: e.matmul(bs[:, :], lhsT=onesblk[:, :], rhs=sq[:, hq, :], start=True, stop=True),
                     reads=["sq5", "consts"], writes=[bsk])
                P.op("pe", lambda e, bb_=bb_, hq=hq: e.matmul(bb_[:, :], lhsT=onesblk[:, :], rhs=rb[:, hq, :], start=True, stop=True),
                     reads=["rb", "consts"], writes=[bbk])
                P.op("pe", lambda e, bg=bg, hq=hq: e.matmul(bg[:, :], lhsT=g2s[:, hq * 128:(hq + 1) * 128], rhs=sgd[:, :], start=True, stop=True),
                     reads=["sgd", "consts"], writes=[bgk])
                mk, vk, bk_ = "mean%d" % i2, "var%d" % i2, "bon%d" % i2
                P.op("act", lambda e, bm=bm, i2=i2: e.mul(mean[i2][:], bm[:, :], 1.0 / 64), reads=[bmk], writes=[mk])
                P.op("dve", lambda e, i2=i2: e.tensor_tensor(out=var[i2][:], in0=mean[i2][:], in1=mean[i2][:], op=ALU.mult), reads=[mk], writes=[vk])
                P.op("dve", lambda e, bs=bs, i2=i2: e.scalar_tensor_tensor(
                    out=var[i2][:], in0=bs[:, :], scalar=1.0 / 64, in1=var[i2][:], op0=ALU.mult, op1=ALU.subtract),
                    reads=[bsk, vk], writes=[vk])
                P.op("act", lambda e, i2=i2: e.activation(out=var[i2][:], in_=var[i2][:], func=AF.Sqrt, bias=epsb[:, 0:1], scale=1.0),
                     reads=[vk, "consts"], writes=[vk])
                P.op("dve", lambda e, i2=i2: e.reciprocal(var[i2][:], var[i2][:]), reads=[vk], writes=[vk])
                P.op("pool", lambda e, i2=i2, hq=hq: e.tensor_tensor(out=mean[i2][:], in0=yf[:, hq, :], in1=mean[i2][:], op=ALU.subtract),
                     reads=["yf", mk], writes=[mk])
                P.op("pool", lambda e, i2=i2: e.tensor_tensor(out=mean[i2][:], in0=mean[i2][:], in1=var[i2][:], op=ALU.mult),
                     reads=[mk, vk], writes=[mk])
                P.op("act", lambda e, i2=i2, hq=hq: e.activation(out=mean[i2][:], in_=mean[i2][:], func=AF.Identity,
                                                                  bias=gnb[:, hq:hq + 1], scale=gnw[:, hq:hq + 1]),
                     reads=[mk, "consts"], writes=[mk])
                P.op("dve", lambda e, bb_=bb_, i2=i2, hq=hq: e.tensor_tensor(out=bon[i2][:], in0=bb_[:, :], in1=U5[:, 8 + hq, :], op=ALU.mult),
                     reads=[bbk, "U5"], writes=[bk_])
                P.op("pool", lambda e, i2=i2: e.tensor_tensor(out=bon[i2][:], in0=bon[i2][:], in1=mean[i2][:], op=ALU.add),
                     reads=[bk_, mk], writes=[bk_])
                P.op("dve", lambda e, bg=bg, i2=i2, hq=hq: e.tensor_tensor(out=zr[:, hq, :], in0=bg[:, :], in1=bon[i2][:], op=ALU.mult),
                     reads=[bgk, bk_], writes=[("zr", hq)])
            P.dma("sp", zrv[:, :, ts_], zr[:], reads=[("zr", h) for h in range(4)], writes=["ZR_fm"])
        print("stage5", P.emit_stage())


def build(mode="full", dbg=()):
    kb = K(dbg)
    nc = kb.nc
    d = {}
    kb.d = d
    d["xT"] = kb.din("xT", [1024, 4096])
    d["x"] = kb.din("x", [4096, 1024])
    d["w_in"] = kb.din("w_in", [1024, 5504])
    d["cw"] = kb.din("cw", [128, 3, 12])
    d["cb"] = kb.din("cb", [128, 12])
    d["mu"] = kb.din("mu", [128, 15])
    d["ident"] = kb.din("ident", [128, 128])
    for nm, sh in (("w_hy_out", [512, 1024]), ("w_rw_out", [512, 1024]), ("w_o", [1024, 1024]),
                   ("ln1_w", [1, 1024]), ("ln1_b", [1, 1024]), ("ln2_w", [1, 1024]), ("ln2_b", [1, 1024]),
                   ("ffn_w_gate", [1024, 2816]), ("ffn_w_up", [1024, 2816]), ("ffn_w_down", [2816, 1024])):
        d[nm] = kb.din(nm, sh)
    ct_shapes = {"featsT": [33, 4096], "win": [16, 64, 2048], "F1": [64, 130], "TWf": [64, 2, 65], "S3": [64, 10, 128],
                 "G": [128, 128], "TWi": [65, 2, 64], "I3t": [65, 2, 64], "onesblk": [128, 128]}
    for nm, sh in ct_shapes.items():
        d[nm] = kb.din(nm, sh)
    for nm, sh in (("hy_filt_w1", [33, 64]), ("hy_filt_w2", [64, 64]), ("hy_filt_w3", [64, 64]), ("hy_fb", [64, 3]),
                   ("hy_sf", [64, 3]), ("hy_filt_w4", [64, 2048]), ("hy_skip", [2, 512])):
        d[nm] = kb.din(nm, sh)
    for nm, sh in (("mSU", [128, 512]), ("mIU", [128, 512]), ("mSM", [128, 512]), ("Istk", [128, 512]), ("cmask", [128, 1024]),
                   ("rw_w2", [128, 512]), ("rw_a2", [128, 512]), ("rw_g2", [128, 512]), ("rw_w0", [128, 2, 4]), ("rw_a0", [128, 2, 4]),
                   ("rw_k_k", [128, 4]), ("rw_k_a", [128, 4]), ("rw_r_k", [128, 4]), ("rw_gn_w", [128, 4]), ("rw_gn_b", [128, 4])):
        d[nm] = kb.din(nm, sh)
    d["YF_fm"] = kb.dscr("YF_fm", [512, 4096])
    d["YB_fm"] = kb.dscr("YB_fm", [512, 4096])
    d["H3T"] = kb.dscr("H3T", [64, 4096])
    skel = mode in ("skel", "s16")
    d["UHg"] = kb.dscr("UHg", [32, 64, 64 * 32])
    d["X2_fm"] = kb.dscr("X2_fm", [512, 4096])
    d["UR_fm"] = kb.dscr("UR_fm", [1920, 4096])
    d["GT_fm"] = kb.dscr("GT_fm", [2048, 4096])
    d["ZH_fm"] = kb.dscr("ZH_fm", [512, 4096], ext_in=skel)
    d["ZR_fm"] = kb.dscr("ZR_fm", [512, 4096], ext_in=skel)
    d["X1_tm"] = kb.dscr("X1_tm", [4096, 1024])
    d["out"] = nc.dram_tensor("out", [4096, 1024], F32, kind="ExternalOutput").ap()
    if mode == "s1" or mode.startswith("s1:"):
        if mode != "s1":
            kb.cc_list = [int(v) for v in mode.split(":")[1].split(",")]
        stage1(kb)
    elif mode.startswith("rw:"):
        kb.rw_steps = int(mode.split(":")[1])
        if len(mode.split(":")) > 2:
            kb.rw_phase = int(mode.split(":")[2])
        if len(mode.split(":")) > 3:
            kb.f_sel = [int(c) for c in mode.split(":")[3]]
        stage1(kb)
        stage4(kb)
    elif mode == "rw":
        stage1(kb)
        stage4(kb)
        stage5(kb)
    elif mode == "full":
        stage1(kb)
        stage2(kb)
        stage3(kb)
        stage4(kb)
        stage5(kb)
        stage6(kb)
        stage7(kb)
    elif mode == "hy":
        stage1(kb)
        stage2(kb)
        stage3(kb)
    elif mode == "s16":
        stage1(kb)
        stage6(kb)
    else:
        stage1(kb)
        stage6(kb)
        stage7(kb)
    kb.st.close()
    return nc


def host_inputs(inputs, b, mode="full"):
    g = lambda k: np.asarray(inputs[k][0], dtype=np.float32)
    ct = const_tables()
    x = np.asarray(inputs["x"][b], dtype=np.float32)
    m = {}
    m["xT"] = np.ascontiguousarray(x.T)
    m["x"] = np.ascontiguousarray(x)
    m["w_in"] = g("w_in")
    m["cw"] = np.ascontiguousarray(g("hy_conv_w").reshape(3, 12, 128).transpose(2, 0, 1))
    m["cb"] = np.ascontiguousarray(g("hy_conv_b").reshape(12, 128).T)
    m["mu"] = np.ascontiguousarray(g("rw_mu").reshape(15, 128).T)
    m["ident"] = ct["ident"]
    for nm in ("w_hy_out", "w_rw_out", "w_o", "ffn_w_gate", "ffn_w_up", "ffn_w_down"):
        m[nm] = g(nm)
    for nm in ("ln1_w", "ln1_b", "ln2_w", "ln2_b"):
        m[nm] = g(nm).reshape(1, 1024)
    for nm in ("featsT", "win", "F1", "TWf", "S3", "G", "TWi", "I3t", "onesblk"):
        m[nm] = ct[nm]
    for nm in ("hy_filt_w1", "hy_filt_w2", "hy_filt_w3", "hy_filt_w4", "hy_skip"):
        m[nm] = g(nm)
    m["hy_fb"] = np.ascontiguousarray(np.stack([g("hy_filt_b1"), g("hy_filt_b2"), g("hy_filt_b3")], axis=1))
    m["hy_sf"] = np.ascontiguousarray(g("hy_sin_freq").T)
    for nm in ("mSU", "mIU", "mSM", "Istk", "cmask"):
        m[nm] = ct[nm]
    m["rw_w2"] = np.ascontiguousarray(g("rw_w2").reshape(128, 512))
    m["rw_a2"] = np.ascontiguousarray(g("rw_a2").reshape(128, 512))
    m["rw_g2"] = g("rw_g2")
    m["rw_w0"] = np.ascontiguousarray(g("rw_w0").reshape(2, 4, 128).transpose(2, 0, 1))
    m["rw_a0"] = np.ascontiguousarray(g("rw_a0").reshape(2, 4, 128).transpose(2, 0, 1))
    for nm in ("rw_k_k", "rw_k_a", "rw_r_k", "rw_gn_w", "rw_gn_b"):
        m[nm] = np.ascontiguousarray(g(nm).reshape(4, 128).T)
    return m


_NC_CACHE = {}


def kernel(**inputs):
    if "nc" not in _NC_CACHE:
        _NC_CACHE["nc"] = build("full")
    nc = _NC_CACHE["nc"]
    in_maps = [host_inputs(inputs, b) for b in range(8)]
    res = run_bass_kernel_spmd(nc, in_maps, core_ids=list(range(8)))
    out = np.stack([np.asarray(r["out"], dtype=np.float32) for r in res.results], axis=0)
    return out
```

```python
import contextlib
import math
import numpy as np
import concourse.bass as bass
import concourse.mybir as mybir
from concourse.bass_utils import run_bass_kernel_spmd

F32 = mybir.dt.float32
BF16 = mybir.dt.bfloat16
ALU = mybir.AluOpType
AF = mybir.ActivationFunctionType

ENGS = ("pe", "act", "dve", "pool", "sp")
NDMASEM = 12
L = 4096
D = 1024
ALPHA = 2.0 ** 0.25
LN_EPS = 1e-5
GN_EPS = 64e-5
MAGIC = 12582912.0
PI = float(np.pi)


class Prog:
    def __init__(self, nc, stack):
        self.nc = nc
        self.csem = {e: stack.enter_context(nc.semaphore("c_" + e)) for e in ENGS}
        self.dsem = {}
        for e in ("sp", "pool", "act"):
            for k in range(NDMASEM):
                self.dsem[(e, k)] = stack.enter_context(nc.semaphore("d_%s_%d" % (e, k)))
        self.ctot = {e: 0 for e in ENGS}
        self.dtot = {e: 0 for e in ENGS}
        self._reset()

    def _reset(self):
        self.ops = {e: [] for e in ENGS}
        self.last_w = {}
        self.readers = {}

    def _add(self, eng, fn, reads, writes, dma=False):
        idx = len(self.ops[eng])
        ev = (eng, idx)
        deps = set()
        for r in reads:
            w = self.last_w.get(r)
            if w is not None:
                deps.add(w)
        for w_ in writes:
            w = self.last_w.get(w_)
            if w is not None:
                deps.add(w)
            for rd in self.readers.get(w_, ()):
                deps.add(rd)
        deps.discard(ev)
        self.ops[eng].append(dict(fn=fn, deps=deps, dma=dma))
        for r in reads:
            self.readers.setdefault(r, []).append(ev)
        for w_ in writes:
            self.last_w[w_] = ev
            self.readers[w_] = []
        return ev

    def op(self, eng, fn, reads=(), writes=()):
        return self._add(eng, fn, tuple(reads), tuple(writes), dma=False)

    def dma(self, eng, out, in_, reads=(), writes=(), **kw):
        return self._add(eng, lambda e: e.dma_start(out=out, in_=in_, **kw),
                         tuple(reads), tuple(writes), dma=True)

    def emit_stage(self):
        nc = self.nc
        ops = self.ops
        needed = {e: set() for e in ENGS}
        for e in ENGS:
            for o in ops[e]:
                for (de, di) in o["deps"]:
                    if not ops[de][di]["dma"]:
                        if de == "pe" and e == "pe":
                            continue
                        needed[de].add(di)
        for e in ENGS:
            for i in range(len(ops[e]) - 1, -1, -1):
                if not ops[e][i]["dma"]:
                    needed[e].add(i)
                    break
        count_at = {e: {} for e in ENGS}
        cend = {}
        for e in ENGS:
            c = self.ctot[e]
            for i, o in enumerate(ops[e]):
                if (not o["dma"]) and i in needed[e]:
                    c += 1
                    count_at[e][i] = c
            cend[e] = c
        dma_info = {}
        dend = {}
        for e in ENGS:
            n = self.dtot[e]
            for i, o in enumerate(ops[e]):
                if o["dma"]:
                    dma_info[(e, i)] = (e, n % NDMASEM, 16 * (n // NDMASEM + 1), n)
                    n += 1
            dend[e] = n
        csem, dsem = self.csem, self.dsem
        ftargets = []
        for e in ENGS:
            n = dend[e]
            for k in range(min(n, NDMASEM)):
                ftargets.append((dsem[(e, k)], 16 * ((n - 1 - k) // NDMASEM + 1)))

        def run_engine(ename, eng):
            known = {e: -1 for e in ENGS}
            known_dma = set()
            for i, o in enumerate(ops[ename]):
                for (de, di) in sorted(o["deps"]):
                    if ops[de][di]["dma"]:
                        if (de, di) in known_dma:
                            continue
                        q, k, tgt, n = dma_info[(de, di)]
                        eng.wait_ge(dsem[(q, k)], tgt)
                        known_dma.add((de, di))
                    else:
                        if de == "pe" and ename == "pe":
                            continue
                        if known[de] >= di:
                            continue
                        eng.wait_ge(csem[de], count_at[de][di])
                        known[de] = di
                if o["dma"]:
                    q, k, tgt, n = dma_info[(ename, i)]
                    if n >= NDMASEM:
                        eng.wait_ge(dsem[(q, k)], tgt - 16)
                    o["fn"](eng).then_inc(dsem[(q, k)], 16)
                else:
                    ins = o["fn"](eng)
                    if i in count_at[ename]:
                        ins.then_inc(csem[ename], 1)
            for e in ENGS:
                if cend[e] > 0:
                    eng.wait_ge(csem[e], cend[e])
            for (s, v) in ftargets:
                eng.wait_ge(s, v)

        with nc.Block() as block:
            @block.tensor
            def _(eng):
                run_engine("pe", eng)

            @block.scalar
            def _(eng):
                run_engine("act", eng)

            @block.vector
            def _(eng):
                run_engine("dve", eng)

            @block.gpsimd
            def _(eng):
                run_engine("pool", eng)

            @block.sync
            def _(eng):
                run_engine("sp", eng)

        self.ctot = cend
        self.dtot = dend
        n_ops = {e: len(ops[e]) for e in ENGS}
        self._reset()
        return n_ops


def const_tables():
    f = np.float64
    c = {}
    c["ident"] = np.eye(128)
    ob = np.zeros((128, 128))
    ob[:64, :64] = 1.0
    ob[64:, 64:] = 1.0
    c["onesblk"] = ob
    t = np.linspace(0.0, 1.0, L)[:, None]
    w = 2.0 * math.pi * np.arange(L) / L
    fr = np.linspace(1e-4, 15.0, 16)
    ang = w[:, None] * fr[None, :]
    c["featsT"] = np.concatenate([t, np.cos(ang), -np.sin(ang)], axis=-1).T
    deltas = np.abs(np.linspace(math.log(1e-2) / 1.5, math.log(1e-2) / 0.3, 512))
    win = np.exp(-t * deltas[None, :])
    c["win"] = win.reshape(64, 64, 16, 32).transpose(2, 0, 1, 3).reshape(16, 64, 64 * 32)
    i64 = np.arange(64)
    k1 = np.arange(65)
    F1 = np.zeros((64, 2, 65))
    th = 2 * math.pi * np.outer(i64, k1) / 128.0
    F1[:, 0], F1[:, 1] = np.cos(th), -np.sin(th)
    c["F1"] = F1.reshape(64, 130)
    th = 2 * math.pi * np.outer(i64, k1) / 8192.0
    c["TWf"] = np.stack([np.cos(th), np.sin(th)], axis=1)
    th = 2 * math.pi * np.outer(i64, i64) / 64.0
    C, S = np.cos(th), np.sin(th)
    cat = lambda p, q: np.concatenate([p, q], axis=1)
    c["S3"] = np.stack([cat(C, -S), cat(S, C), cat(-S, C), cat(C, S), cat(C, C), cat(S, S),
                        cat(S, -S), cat(-C, C), cat(-S, S), cat(C, -C)], axis=1)
    G = np.zeros((128, 128))
    G[:64, :64], G[:64, 64:], G[64:, :64], G[64:, 64:] = C, S, -S, C
    c["G"] = G
    th = 2 * math.pi * np.outer(k1, i64) / 8192.0
    c["TWi"] = np.stack([np.cos(th), np.sin(th)], axis=1)
    ck = np.full(65, 2.0)
    ck[0] = ck[64] = 1.0
    th = 2 * math.pi * np.outer(k1, i64) / 128.0
    c["I3t"] = np.stack([ck[:, None] / 8192.0 * np.cos(th), -ck[:, None] / 8192.0 * np.sin(th)], axis=1)
    row = np.arange(64)[:, None]
    col = np.arange(64)[None, :]
    def stk(f0, f1):
        m = np.zeros((2, 64, 2, 4, 64))
        m[:, :, 0, :, :] = f0[None, :, None, :]
        m[:, :, 1, :, :] = f1[None, :, None, :]
        return m.reshape(128, 512)
    c["mSU"] = stk((row < col) * 1.0, (row > col) * 1.0)
    c["mIU"] = stk((row <= col) * 1.0, (row >= col) * 1.0)
    c["mSM"] = stk((col < row) * 1.0, (col > row) * 1.0)
    c["Istk"] = stk((row == col) * 1.0, (row == col) * 1.0)
    cm = np.ones((128, 1024))
    cm[:, ::64] = 0.0
    c["cmask"] = cm
    return {k: np.ascontiguousarray(v, dtype=np.float32) for k, v in c.items()}


class K:
    def __init__(self, dbg=()):
        self.dbg = set(dbg)
        self.nc = bass.Bass("TRN2", target_bir_lowering=False)
        self.st = contextlib.ExitStack()
        self.P = Prog(self.nc, self.st)
        self.inputs = {}
        self.ps = [self.st.enter_context(self.nc.psum_tensor("ps%d" % i, [128, 512], F32)) for i in range(8)]
        self.psk = ["ps%d" % i for i in range(8)]
        self.bank_i = 0

    def bank(self):
        i = self.bank_i % 8
        self.bank_i += 1
        return self.ps[i], self.psk[i]

    def din(self, name, shape, dt=F32):
        return self.nc.dram_tensor(name, list(shape), dt, kind="ExternalInput").ap()

    def dscr(self, name, shape, dt=F32, ext_in=False):
        kind = "Internal"
        if name in self.dbg:
            kind = "ExternalOutput"
        if ext_in:
            kind = "ExternalInput"
        return self.nc.dram_tensor(name, list(shape), dt, kind=kind).ap()

    _uid = [0]

    @staticmethod
    def sb(s, nc, name, shape, dt=F32):
        K._uid[0] += 1
        return s.enter_context(nc.sbuf_tensor("s%d_%s" % (K._uid[0], name), list(shape), dt))


def layer_norm_rows(P, nc, pre, outt, lnw, lnb, sm, junk, epsb, key, eng2="pool"):
    P.op("dve", lambda e: e.memset(sm[:, 0:2], 0.0), writes=[key + "sm"])
    P.op("act", lambda e: e.activation(out=junk[:], in_=pre[:], func=AF.Identity, accum_out=sm[:, 0:1]),
         reads=[key + "pre", key + "sm"], writes=[key + "sm", key + "junk"])
    P.op("act", lambda e: e.activation(out=junk[:], in_=pre[:], func=AF.Square, accum_out=sm[:, 1:2]),
         reads=[key + "pre", key + "sm", key + "junk"], writes=[key + "sm", key + "junk"])
    P.op("dve", lambda e: e.tensor_scalar(out=sm[:, 2:4], in0=sm[:, 0:2], scalar1=1.0 / 1024, scalar2=None, op0=ALU.mult),
         reads=[key + "sm"], writes=[key + "sm"])
    P.op("dve", lambda e: e.tensor_tensor(out=sm[:, 4:5], in0=sm[:, 2:3], in1=sm[:, 2:3], op=ALU.mult),
         reads=[key + "sm"], writes=[key + "sm"])
    P.op("dve", lambda e: e.tensor_tensor(out=sm[:, 5:6], in0=sm[:, 3:4], in1=sm[:, 4:5], op=ALU.subtract),
         reads=[key + "sm"], writes=[key + "sm"])
    P.op("act", lambda e: e.activation(out=sm[:, 6:7], in_=sm[:, 5:6], func=AF.Sqrt, bias=epsb[:, 0:1], scale=1.0),
         reads=[key + "sm"], writes=[key + "sm"])
    P.op("dve", lambda e: e.reciprocal(sm[:, 7:8], sm[:, 6:7]), reads=[key + "sm"], writes=[key + "sm"])
    P.op("dve", lambda e: e.tensor_scalar(out=outt[:], in0=pre[:], scalar1=sm[:, 2:3], scalar2=sm[:, 7:8],
                                          op0=ALU.subtract, op1=ALU.mult),
         reads=[key + "pre", key + "sm"], writes=[key + "out"])
    P.op(eng2, lambda e: e.tensor_tensor(out=outt[:], in0=outt[:], in1=lnw[:], op=ALU.mult),
         reads=[key + "out", "lnw"], writes=[key + "out"])
    P.op(eng2, lambda e: e.tensor_tensor(out=outt[:], in0=outt[:], in1=lnb[:], op=ALU.add),
         reads=[key + "out", "lnb"], writes=[key + "out"])


def stage1(kb):
    nc, P = kb.nc, kb.P
    d = kb.d
    with contextlib.ExitStack() as s:
        sb = lambda n, sh, dt=F32: K.sb(s, nc, n, sh, dt)
        xT = sb("xT", [128, 8, 4096], BF16)
        wbuf = [sb("wb%d" % i, [128, 8, 512], BF16) for i in range(2)]
        raw = [sb("raw%d" % i, [128, 4098], F32) for i in range(2)]
        o = [sb("o%d" % i, [128, 4096], F32) for i in range(2)]
        uht = sb("uht", [64, 4, 64, 32], F32)
        pcs = sb("pcs", [128, 4, 27], F32)
        mus = sb("mus", [128, 15], F32)
        ident = sb("ident", [128, 128], F32)
        P.dma("sp", ident[:], d["ident"], writes=["ident"])
        P.dma("sp", pcs[:, 0:3, 0:12], d["cw"], writes=["pcs"])
        P.dma("sp", pcs[:, 3, 0:12], d["cb"], reads=["pcs"], writes=["pcs"])
        P.dma("sp", mus[:], d["mu"], writes=["mus"])
        P.op("dve", lambda e: e.memset(pcs[:, 3, 12:27], 0.0), reads=["pcs"], writes=["pcs"])
        P.op("dve", lambda e: e.tensor_scalar(out=pcs[:, 0, 12:27], in0=mus[:], scalar1=0.5, scalar2=None, op0=ALU.mult),
             reads=["mus", "pcs"], writes=["pcs"])
        P.op("dve", lambda e: e.tensor_scalar(out=pcs[:, 2, 12:27], in0=mus[:], scalar1=0.5, scalar2=None, op0=ALU.mult),
             reads=["mus", "pcs"], writes=["pcs"])
        P.op("dve", lambda e: e.tensor_scalar(out=pcs[:, 1, 12:27], in0=mus[:], scalar1=-1.0, scalar2=1.0, op0=ALU.mult, op1=ALU.add),
             reads=["mus", "pcs"], writes=["pcs"])
        for i in range(2):
            P.op("pool", lambda e, i=i: e.memset(raw[i][:, 0:1], 0.0), writes=["rawpad%d" % i])
            P.op("pool", lambda e, i=i: e.memset(raw[i][:, 4097:4098], 0.0), writes=["rawpad%d" % i])
        xTv = d["xT"].rearrange("(k p) t -> p k t", p=128)
        for k in range(8):
            P.dma("pool", xT[:, k, :], xTv[:, k, :], writes=[("xT", k)])
        wv = d["w_in"].rearrange("(k p) c -> p k c", p=128)
        evi = 0
        for cc in getattr(kb, "cc_list", range(43)):
            wb = cc // 4
            if cc % 4 == 0 or getattr(kb, "cc_list", None) is not None:
                ncol = min(512, 5504 - wb * 512)
                P.dma("pool", wbuf[wb % 2][:, :, 0:ncol], wv[:, :, wb * 512: wb * 512 + ncol], writes=["wb%d" % (wb % 2)])
            wt = wbuf[wb % 2]
            wk = "wb%d" % (wb % 2)
            c0 = (cc % 4) * 128
            rb = raw[cc % 2]
            ob = o[cc % 2]
            rk = "raw%d" % (cc % 2)
            ok = "o%d" % (cc % 2)
            gate = cc >= 27
            for tb in range(8):
                bk, bkk = kb.bank()
                for k in range(8):
                    P.op("pe", lambda e, bk=bk, wt=wt, k=k, c0=c0, tb=tb: e.matmul(
                        bk[:, :], lhsT=wt[:, k, c0:c0 + 128], rhs=xT[:, k, tb * 512:(tb + 1) * 512],
                        start=(k == 0), stop=(k == 7)), reads=[wk, ("xT", k)], writes=[bkk])
                if gate:
                    P.op("act", lambda e, bk=bk, ob=ob, tb=tb: e.activation(
                        out=ob[:, tb * 512:(tb + 1) * 512], in_=bk[:, :], func=AF.Sigmoid),
                        reads=[bkk], writes=[(ok, tb)])
                else:
                    eng = "act" if evi % 2 == 0 else "dve"
                    evi += 1
                    if eng == "act":
                        P.op("act", lambda e, bk=bk, rb=rb, tb=tb: e.copy(rb[:, 1 + tb * 512: 1 + (tb + 1) * 512], bk[:, :]),
                             reads=[bkk], writes=[(rk, tb)])
                    else:
                        P.op("dve", lambda e, bk=bk, rb=rb, tb=tb: e.tensor_copy(rb[:, 1 + tb * 512: 1 + (tb + 1) * 512], bk[:, :]),
                             reads=[bkk], writes=[(rk, tb)])
            okeys = [(ok, tb) for tb in range(8)]
            rkeys = [(rk, tb) for tb in range(8)] + ["rawpad%d" % (cc % 2)]
            if not gate:
                P.op("act", lambda e, rb=rb, ob=ob, cc=cc: e.activation(
                    out=ob[:, :], in_=rb[:, 1:4097], func=AF.Identity, bias=pcs[:, 3, cc:cc + 1], scale=pcs[:, 1, cc:cc + 1]),
                    reads=rkeys + ["pcs"], writes=okeys)
                P.op("dve", lambda e, rb=rb, ob=ob, cc=cc: e.scalar_tensor_tensor(
                    out=ob[:, :], in0=rb[:, 0:4096], scalar=pcs[:, 0, cc:cc + 1], in1=ob[:, :], op0=ALU.mult, op1=ALU.add),
                    reads=rkeys + ["pcs"] + okeys, writes=okeys)
                P.op("dve", lambda e, rb=rb, ob=ob, cc=cc: e.scalar_tensor_tensor(
                    out=ob[:, :], in0=rb[:, 2:4098], scalar=pcs[:, 2, cc:cc + 1], in1=ob[:, :], op0=ALU.mult, op1=ALU.add),
                    reads=rkeys + ["pcs"] + okeys, writes=okeys)
            if cc < 8:
                obv = ob[:, :].rearrange("p (b a) -> p a b", a=64)
                for a0 in range(0, 64, 4):
                    bk, bkk = kb.bank()
                    for ai in range(4):
                        P.op("pe", lambda e, bk=bk, obv=obv, a=a0 + ai, ai=ai: e.transpose(
                            bk[0:64, ai * 128:(ai + 1) * 128], obv[:, a, :], ident[:, :]),
                            reads=okeys + ["ident"], writes=[bkk])
                    eng = "act" if (a0 // 4) % 2 == 0 else "dve"
                    outap = uht[:, :, a0:a0 + 4, :].rearrange("p g a c -> p a g c")
                    inap = bk[0:64, :].rearrange("p (a g c) -> p a g c", a=4, g=4)
                    if eng == "act":
                        P.op("act", lambda e, outap=outap, inap=inap: e.copy(outap, inap), reads=[bkk], writes=[("uht", a0)])
                    else:
                        P.op("dve", lambda e, outap=outap, inap=inap: e.tensor_copy(outap, inap), reads=[bkk], writes=[("uht", a0)])
                P.dma("sp", d["UHg"][cc * 4:(cc + 1) * 4].rearrange("g b n -> b g n"),
                      uht[:].rearrange("b g a c -> b g (a c)"),
                      reads=[("uht", a0) for a0 in range(0, 64, 4)], writes=["UHg"])
            elif cc < 12:
                P.dma("sp", d["X2_fm"][(cc - 8) * 128:(cc - 7) * 128, :], ob[:, :], reads=okeys, writes=["X2_fm"])
            elif cc < 27:
                P.dma("sp", d["UR_fm"][(cc - 12) * 128:(cc - 11) * 128, :], ob[:, :], reads=okeys, writes=["UR_fm"])
            else:
                P.dma("sp", d["GT_fm"][(cc - 27) * 128:(cc - 26) * 128, :], ob[:, :], reads=okeys, writes=["GT_fm"])
        print("stage1", P.emit_stage())


def stage6(kb):
    nc, P = kb.nc, kb.P
    d = kb.d
    with contextlib.ExitStack() as s:
        sb = lambda n, sh, dt=F32: K.sb(s, nc, n, sh, dt)
        zhT = sb("zhT", [128, 4, 4096], BF16)
        zrT = sb("zrT", [128, 4, 4096], BF16)
        why = sb("why", [128, 4, 1024], BF16)
        wrw = sb("wrw", [128, 4, 1024], BF16)
        wo = sb("wo", [128, 8, 1024], BF16)
        mT = sb("mT", [128, 8, 4096], BF16)
        gh = [sb("gh%d" % i, [128, 512]) for i in range(2)]
        gr = [sb("gr%d" % i, [128, 512]) for i in range(2)]
        t1 = [sb("t1_%d" % i, [128, 512]) for i in range(2)]
        t2 = [sb("t2_%d" % i, [128, 512]) for i in range(2)]
        xt = [sb("xt%d" % i, [128, 1024]) for i in range(2)]
        pre = sb("pre", [128, 1024])
        junk = sb("junk", [128, 1024])
        x1o = [sb("x1o0", [128, 1024])] * 2
        lnw = sb("lnw", [128, 1024])
        lnb = sb("lnb", [128, 1024])
        sm = sb("sm", [128, 8])
        epsb = sb("epsb", [128, 1])
        P.op("pool", lambda e: e.memset(epsb[:], LN_EPS), writes=["epsb"])
        P.dma("sp", lnw[:], d["ln1_w"].partition_broadcast(128), writes=["lnw"])
        P.dma("sp", lnb[:], d["ln1_b"].partition_broadcast(128), writes=["lnb"])
        for k in range(4):
            P.dma("pool", zhT[:, k, :], d["ZH_fm"][k * 128:(k + 1) * 128, :], reads=["ZH_fm"], writes=["zhT"])
            P.dma("pool", zrT[:, k, :], d["ZR_fm"][k * 128:(k + 1) * 128, :], reads=["ZR_fm"], writes=["zrT"])
        P.dma("pool", why[:], d["w_hy_out"].rearrange("(k p) c -> p k c", p=128), writes=["why"])
        P.dma("pool", wrw[:], d["w_rw_out"].rearrange("(k p) c -> p k c", p=128), writes=["wrw"])
        P.dma("pool", wo[:], d["w_o"].rearrange("(k p) c -> p k c", p=128), writes=["wo"])
        it = 0
        for cc in range(8):
            for tb in range(8):
                i2 = it % 2
                it += 1
                ts_ = slice(tb * 512, (tb + 1) * 512)
                P.dma("sp", gh[i2][:], d["GT_fm"][cc * 128:(cc + 1) * 128, ts_], reads=["GT_fm"], writes=["gh%d" % i2])
                P.dma("sp", gr[i2][:], d["GT_fm"][1024 + cc * 128:1024 + (cc + 1) * 128, ts_], reads=["GT_fm"], writes=["gr%d" % i2])
                bh, bhk = kb.bank()
                br, brk = kb.bank()
                for k in range(4):
                    P.op("pe", lambda e, bh=bh, k=k, cc=cc, ts_=ts_: e.matmul(
                        bh[:, :], lhsT=why[:, k, cc * 128:(cc + 1) * 128], rhs=zhT[:, k, ts_], start=(k == 0), stop=(k == 3)),
                        reads=["why", "zhT"], writes=[bhk])
                for k in range(4):
                    P.op("pe", lambda e, br=br, k=k, cc=cc, ts_=ts_: e.matmul(
                        br[:, :], lhsT=wrw[:, k, cc * 128:(cc + 1) * 128], rhs=zrT[:, k, ts_], start=(k == 0), stop=(k == 3)),
                        reads=["wrw", "zrT"], writes=[brk])
                P.op("dve", lambda e, i2=i2, bh=bh: e.tensor_tensor(out=t1[i2][:], in0=bh[:, :], in1=gh[i2][:], op=ALU.mult),
                     reads=[bhk, "gh%d" % i2], writes=["t1_%d" % i2])
                P.op("dve", lambda e, i2=i2, br=br: e.tensor_tensor(out=t2[i2][:], in0=br[:, :], in1=gr[i2][:], op=ALU.mult),
                     reads=[brk, "gr%d" % i2], writes=["t2_%d" % i2])
                P.op("pool", lambda e, i2=i2, cc=cc, ts_=ts_: e.tensor_tensor(out=mT[:, cc, ts_], in0=t1[i2][:], in1=t2[i2][:], op=ALU.add),
                     reads=["t1_%d" % i2, "t2_%d" % i2], writes=[("mT", cc, tb)])
        mkeys = [("mT", cc, tb) for cc in range(8) for tb in range(8)]
        for blk in range(32):
            i2 = blk % 2
            P.dma("sp", xt[i2][:], d["x"][blk * 128:(blk + 1) * 128, :], writes=["xt%d" % i2])
            for nh in range(2):
                bk, bkk = kb.bank()
                for k in range(8):
                    P.op("pe", lambda e, bk=bk, k=k, blk=blk, nh=nh: e.matmul(
                        bk[:, :], lhsT=mT[:, k, blk * 128:(blk + 1) * 128], rhs=wo[:, k, nh * 512:(nh + 1) * 512],
                        start=(k == 0), stop=(k == 7)), reads=mkeys + ["wo"] if k == 0 else ["wo"], writes=[bkk])
                P.op("dve", lambda e, bk=bk, i2=i2, nh=nh: e.scalar_tensor_tensor(
                    out=pre[:, nh * 512:(nh + 1) * 512], in0=xt[i2][:, nh * 512:(nh + 1) * 512], scalar=ALPHA,
                    in1=bk[:, :], op0=ALU.mult, op1=ALU.add), reads=[bkk, "xt%d" % i2], writes=["Lpre"])
            layer_norm_rows(P, nc, pre, x1o[i2], lnw, lnb, sm, junk, epsb, "L")
            P.dma("sp", d["X1_tm"][blk * 128:(blk + 1) * 128, :], x1o[i2][:], reads=["Lout"], writes=["X1_tm"])
        print("stage6", P.emit_stage())


def stage7(kb):
    nc, P = kb.nc, kb.P
    d = kb.d
    with contextlib.ExitStack() as s:
        sb = lambda n, sh, dt=F32: K.sb(s, nc, n, sh, dt)
        x1q = sb("x1q", [128, 8, 1024])
        x1T = sb("x1T", [128, 8, 1024], BF16)
        hT = sb("hT", [128, 22, 1024], BF16)
        wdn = sb("wdn", [128, 22, 1024], BF16)
        wg = [sb("wg%d" % i, [128, 8, 128], BF16) for i in range(2)]
        wu = [sb("wu%d" % i, [128, 8, 128], BF16) for i in range(2)]
        sg = [sb("sg%d" % i, [128, 512]) for i in range(2)]
        pre = sb("pre7", [128, 1024])
        junk = sb("junk7", [128, 1024])
        xo = sb("xo7", [128, 1024])
        lnw = sb("lnw7", [128, 1024])
        lnb = sb("lnb7", [128, 1024])
        sm = sb("sm7", [128, 8])
        epsb = sb("epsb7", [128, 1])
        ident = sb("ident7", [128, 128])
        P.dma("sp", ident[:], d["ident"], writes=["ident"])
        P.op("pool", lambda e: e.memset(epsb[:], LN_EPS), writes=["epsb"])
        P.dma("sp", lnw[:], d["ln2_w"].partition_broadcast(128), writes=["lnw"])
        P.dma("sp", lnb[:], d["ln2_b"].partition_broadcast(128), writes=["lnb"])
        for f in range(22):
            P.dma("pool", wdn[:, f, :], d["ffn_w_down"][f * 128:(f + 1) * 128, :], writes=["wdn"])
        wgv = d["ffn_w_gate"].rearrange("(k p) f -> p k f", p=128)
        wuv = d["ffn_w_up"].rearrange("(k p) f -> p k f", p=128)
        wi = 0
        for q in range(4):
            for blk in range(8):
                r0 = q * 1024 + blk * 128
                P.dma("sp", x1q[:, blk, :], d["X1_tm"][r0:r0 + 128, :], reads=["X1_tm"], writes=[("x1q", blk)])
            for blk in range(8):
                for dc0 in range(0, 8, 4):
                    bk, bkk = kb.bank()
                    for j in range(4):
                        dc = dc0 + j
                        P.op("pe", lambda e, bk=bk, blk=blk, dc=dc, j=j: e.transpose(
                            bk[:, j * 128:(j + 1) * 128], x1q[:, blk, dc * 128:(dc + 1) * 128], ident[:, :]),
                            reads=[("x1q", blk), "ident"], writes=[bkk])
                    outap = x1T[:, dc0:dc0 + 4, blk * 128:(blk + 1) * 128]
                    inap = bk[:, :].rearrange("p (j t) -> p j t", j=4)
                    if (blk + dc0 // 4) % 2 == 0:
                        P.op("act", lambda e, outap=outap, inap=inap: e.copy(outap, inap), reads=[bkk], writes=[("x1T", blk, dc0)])
                    else:
                        P.op("dve", lambda e, outap=outap, inap=inap: e.tensor_copy(outap, inap), reads=[bkk], writes=[("x1T", blk, dc0)])
            xkeys = [("x1T", blk, dc0) for blk in range(8) for dc0 in (0, 4)]
            for f in range(22):
                i2 = wi % 2
                wi += 1
                P.dma("pool", wg[i2][:], wgv[:, :, f * 128:(f + 1) * 128], writes=["wg%d" % i2])
                P.dma("pool", wu[i2][:], wuv[:, :, f * 128:(f + 1) * 128], writes=["wu%d" % i2])
                for tb in range(2):
                    ts_ = slice(tb * 512, (tb + 1) * 512)
                    bg, bgk = kb.bank()
                    bu, buk = kb.bank()
                    for k in range(8):
                        P.op("pe", lambda e, bg=bg, k=k, i2=i2, ts_=ts_: e.matmul(
                            bg[:, :], lhsT=wg[i2][:, k, :], rhs=x1T[:, k, ts_], start=(k == 0), stop=(k == 7)),
                            reads=(xkeys if k == 0 else []) + ["wg%d" % i2], writes=[bgk])
                    for k in range(8):
                        P.op("pe", lambda e, bu=bu, k=k, i2=i2, ts_=ts_: e.matmul(
                            bu[:, :], lhsT=wu[i2][:, k, :], rhs=x1T[:, k, ts_], start=(k == 0), stop=(k == 7)),
                            reads=(xkeys if k == 0 else []) + ["wu%d" % i2], writes=[buk])
                    P.op("act", lambda e, bg=bg, tb=tb: e.activation(out=sg[tb][:], in_=bg[:, :], func=AF.Silu),
                         reads=[bgk], writes=["sg%d" % tb])
                    P.op("dve", lambda e, bu=bu, tb=tb, f=f, ts_=ts_: e.tensor_tensor(
                        out=hT[:, f, ts_], in0=bu[:, :], in1=sg[tb][:], op=ALU.mult),
                        reads=[buk, "sg%d" % tb], writes=[("hT", f, tb)])
            hkeys = [("hT", f, tb) for f in range(22) for tb in range(2)]
            for blk in range(8):
                for nh in range(2):
                    bk, bkk = kb.bank()
                    for f in range(22):
                        P.op("pe", lambda e, bk=bk, f=f, blk=blk, nh=nh: e.matmul(
                            bk[:, :], lhsT=hT[:, f, blk * 128:(blk + 1) * 128], rhs=wdn[:, f, nh * 512:(nh + 1) * 512],
                            start=(f == 0), stop=(f == 21)), reads=(hkeys if f == 0 else []) + ["wdn"], writes=[bkk])
                    P.op("dve", lambda e, bk=bk, blk=blk, nh=nh: e.scalar_tensor_tensor(
                        out=pre[:, nh * 512:(nh + 1) * 512], in0=x1q[:, blk, nh * 512:(nh + 1) * 512], scalar=ALPHA,
                        in1=bk[:, :], op0=ALU.mult, op1=ALU.add), reads=[bkk, ("x1q", blk)], writes=["Mpre"])
                layer_norm_rows(P, nc, pre, xo, lnw, lnb, sm, junk, epsb, "M")
                r0 = q * 1024 + blk * 128
                P.dma("sp", d["out"][r0:r0 + 128, :], xo[:], reads=["Mout"], writes=["out"])
        print("stage7", P.emit_stage())


def stage2(kb):
    nc, P = kb.nc, kb.P
    d = kb.d
    with contextlib.ExitStack() as s:
        sb = lambda n, sh, dt=F32: K.sb(s, nc, n, sh, dt)
        feats = sb("feats", [33, 4096])
        hbuf = [sb("hb%d" % i, [64, 4096]) for i in range(2)]
        ws = [sb("fw1", [33, 64]), sb("fw2", [64, 64]), sb("fw3", [64, 64])]
        fb = sb("fb", [64, 3])
        sf = sb("sf", [64, 3])
        sfb = sb("sfb", [64, 3])
        arg = [sb("arg%d" % i, [64, 512]) for i in range(2)]
        kq = [sb("kq%d" % i, [64, 512]) for i in range(2)]
        P.dma("sp", feats[:], d["featsT"], writes=["feats"])
        for i, nm in enumerate(("hy_filt_w1", "hy_filt_w2", "hy_filt_w3")):
            P.dma("sp", ws[i][:], d[nm], writes=["fw%d" % i])
        P.dma("sp", fb[:], d["hy_fb"], writes=["fb"])
        P.dma("sp", sf[:], d["hy_sf"], writes=["sf"])
        P.op("dve", lambda e: e.tensor_tensor(out=sfb[:], in0=sf[:], in1=fb[:], op=ALU.mult), reads=["fb", "sf"], writes=["sfb"])
        it = 0
        for l in range(3):
            kdim = 33 if l == 0 else 64
            hin = feats if l == 0 else hbuf[(l - 1) % 2]
            hink = "feats" if l == 0 else "hb%d" % ((l - 1) % 2)
            hout = hbuf[l % 2]
            houtk = "hb%d" % (l % 2)
            for tb in range(8):
                i2 = it % 2
                it += 1
                ts_ = slice(tb * 512, (tb + 1) * 512)
                bk, bkk = kb.bank()
                P.op("pe", lambda e, bk=bk, l=l, kdim=kdim, hin=hin, ts_=ts_: e.matmul(
                    bk[0:64, :], lhsT=ws[l][0:kdim, :], rhs=hin[0:kdim, ts_], start=True, stop=True),
                    reads=["fw%d" % l] + [(hink, tb)] + ([hink] if l == 0 else []), writes=[bkk])
                P.op("dve", lambda e, bk=bk, l=l, i2=i2: e.tensor_scalar(
                    out=arg[i2][:], in0=bk[0:64, :], scalar1=sf[:, l:l + 1], scalar2=sfb[:, l:l + 1], op0=ALU.mult, op1=ALU.add),
                    reads=[bkk, "sf", "sfb"], writes=["arg%d" % i2])
                P.op("dve", lambda e, i2=i2: e.tensor_scalar(
                    out=kq[i2][:], in0=arg[i2][:], scalar1=1.0 / (2 * PI), scalar2=MAGIC, op0=ALU.mult, op1=ALU.add),
                    reads=["arg%d" % i2], writes=["kq%d" % i2])
                P.op("dve", lambda e, i2=i2: e.tensor_scalar(
                    out=kq[i2][:], in0=kq[i2][:], scalar1=-MAGIC, scalar2=None, op0=ALU.add),
                    reads=["kq%d" % i2], writes=["kq%d" % i2])
                P.op("dve", lambda e, i2=i2: e.scalar_tensor_tensor(
                    out=arg[i2][:], in0=kq[i2][:], scalar=-2 * PI, in1=arg[i2][:], op0=ALU.mult, op1=ALU.add),
                    reads=["kq%d" % i2, "arg%d" % i2], writes=["arg%d" % i2])
                P.op("act", lambda e, i2=i2, hout=hout, ts_=ts_: e.activation(out=hout[:, ts_], in_=arg[i2][:], func=AF.Sin),
                     reads=["arg%d" % i2], writes=[(houtk, tb)])
        P.dma("sp", d["H3T"], hbuf[0][:], reads=[("hb0", tb) for tb in range(8)], writes=["H3T"])
        print("stage2", P.emit_stage())


CH7 = [(0, 7), (7, 7), (14, 7), (21, 7), (28, 4)]


def stage3(kb):
    nc, P = kb.nc, kb.P
    d = kb.d
    with contextlib.ExitStack() as s:
        sb = lambda n, sh, dt=F32: K.sb(s, nc, n, sh, dt)
        h3T = sb("h3T", [64, 4096])
        fw4 = sb("fw4", [64, 2048])
        F1 = sb("F1", [64, 130])
        TWf = sb("TWf", [64, 2, 65])
        S3 = sb("S3", [64, 10, 128])
        G = sb("G", [128, 128])
        TWi = sb("TWi", [65, 2, 64])
        I3t = sb("I3t", [65, 2, 64])
        skipb = [sb("skipb%d" % o, [128, 32]) for o in range(2)]
        win = sb("win", [64, 64, 32])
        hs = [sb("hs%d" % i, [64, 64, 32]) for i in range(2)]
        Zt = sb("Zt", [64, 64, 32])
        x1t = sb("x1t", [64, 64, 32])
        x2f = sb("x2f", [32, 4096])
        W1 = sb("W1", [128, 4160])
        W2 = sb("W2", [128, 4160])
        Bb = sb("Bb", [128, 4160])
        T1 = sb("T1", [128, 2080])
        T2 = sb("T2", [128, 2080])
        Ha = sb("Ha", [128, 2080])
        Hb = sb("Hb", [128, 2080])
        Y = sb("Y", [128, 2080])
        tmp = sb("tmp3", [128, 512])
        P.dma("sp", h3T[:], d["H3T"], reads=["H3T"], writes=["h3T"])
        P.dma("sp", fw4[:], d["hy_filt_w4"], writes=["fw4"])
        for nm, t_ in (("F1", F1), ("TWf", TWf), ("S3", S3), ("G", G), ("TWi", TWi), ("I3t", I3t)):
            P.dma("sp", t_[:], d[nm], writes=["tabs"])
        h3v = h3T[:, :].rearrange("p (b a) -> p a b", a=64)
        TWfc = TWf[:, 0, :].unsqueeze(1).to_broadcast([64, 32, 65])
        TWfs = TWf[:, 1, :].unsqueeze(1).to_broadcast([64, 32, 65])
        TWic = TWi[:, 0, :].unsqueeze(1).to_broadcast([65, 32, 64])
        TWis = TWi[:, 1, :].unsqueeze(1).to_broadcast([65, 32, 64])
        A3 = W1[0:64, :].rearrange("p (c k) -> p c k", k=130)
        E3 = W1[0:65, 0:4096].rearrange("p (c k) -> p c k", k=128)
        Et4 = W2[0:65, 0:4096].rearrange("p (r c k) -> p r c k", r=2, c=32)
        t1f = T1[0:64, :].rearrange("p (c k) -> p c k", k=65)
        t2f = T2[0:64, :].rearrange("p (c k) -> p c k", k=65)
        t1i = T1[0:65, 0:2048].rearrange("p (c k) -> p c k", k=64)
        t2i = T2[0:65, 0:2048].rearrange("p (c k) -> p c k", k=64)
        Y3 = Y[:, :].rearrange("p (c k) -> p c k", k=65)

        def fwd_AB(sig, sigkeys, Bt, Bkey):
            B4 = Bt[0:64, :].rearrange("p (r c k) -> p r c k", r=2, c=32)
            akeys = []
            for c0 in range(0, 32, 3):
                n = min(3, 32 - c0)
                bk, bkk = kb.bank()
                for j in range(n):
                    P.op("pe", lambda e, bk=bk, j=j, c=c0 + j: e.matmul(
                        bk[0:64, j * 130:(j + 1) * 130], lhsT=sig[:, :, c], rhs=F1[:, :], start=True, stop=True),
                        reads=list(sigkeys) + ["tabs"], writes=[bkk])
                P.op("act", lambda e, bk=bk, c0=c0, n=n: e.copy(
                    A3[:, c0:c0 + n, :], bk[0:64, 0:n * 130].rearrange("p (c k) -> p c k", k=130)),
                    reads=[bkk, "W1tokD", "W1tokP"], writes=[("W1", c0)])
                akeys.append(("W1", c0))
            Are, Aim = A3[:, :, 0:65], A3[:, :, 65:130]
            P.op("dve", lambda e: e.tensor_tensor(out=t1f, in0=Are, in1=TWfc, op=ALU.mult), reads=akeys + ["tabs"], writes=["T1"])
            P.op("pool", lambda e: e.tensor_tensor(out=t2f, in0=Aim, in1=TWfs, op=ALU.mult), reads=akeys + ["tabs"], writes=["T2"])
            P.op("dve", lambda e: e.tensor_tensor(out=B4[:, 0], in0=t1f, in1=t2f, op=ALU.add), reads=["T1", "T2"], writes=[(Bkey, 0)])
            P.op("pool", lambda e: e.tensor_tensor(out=B4[:, 1], in0=Aim, in1=TWfc, op=ALU.mult), reads=akeys + ["tabs"], writes=[(Bkey, 1), "W1tokP"])
            P.op("dve", lambda e: e.tensor_tensor(out=t1f, in0=Are, in1=TWfs, op=ALU.mult), reads=akeys + ["tabs"], writes=["T1", "W1tokD"])
            P.op("pool", lambda e: e.tensor_tensor(out=B4[:, 1], in0=B4[:, 1], in1=t1f, op=ALU.subtract),
                 reads=["T1", (Bkey, 1)], writes=[(Bkey, 1)])

        def bcols(Bt, r, c0, n):
            return Bt[0:64, r * 2080 + c0 * 65: r * 2080 + (c0 + n) * 65]

        for g in range(16):
            c0g = g * 32
            P.dma("sp", win[:].rearrange("p a c -> p (a c)"), d["win"][g], writes=["win"])
            P.dma("sp", Zt[:].rearrange("p a c -> p (a c)"), d["UHg"][g], reads=["UHg"], writes=["Zt"])
            P.dma("sp", x1t[:].rearrange("p a c -> p (a c)"), d["UHg"][16 + g], reads=["UHg"], writes=["x1t"])
            P.dma("sp", x2f[:], d["X2_fm"][c0g:c0g + 32, :], reads=["X2_fm"], writes=["x2f"])
            for o in range(2):
                P.dma("sp", skipb[o][:], d["hy_skip"][o:o + 1, c0g:c0g + 32].partition_broadcast(128), writes=["skipb%d" % o])
            for o in range(2):
                for dd in range(2):
                    col0 = o * 1024 + dd * 512 + c0g
                    for a0 in range(0, 64, 16):
                        bk, bkk = kb.bank()
                        for ai in range(16):
                            P.op("pe", lambda e, bk=bk, ai=ai, a=a0 + ai, col0=col0: e.matmul(
                                bk[0:64, ai * 32:(ai + 1) * 32], lhsT=h3v[:, a, :], rhs=fw4[:, col0:col0 + 32], start=True, stop=True),
                                reads=["h3T", "fw4"], writes=[bkk])
                        P.op("dve", lambda e, bk=bk, dd=dd, a0=a0: e.tensor_tensor(
                            out=hs[dd][:, a0:a0 + 16, :], in0=bk[0:64, :].rearrange("p (a c) -> p a c", c=32),
                            in1=win[:, a0:a0 + 16, :], op=ALU.mult), reads=[bkk, "win"], writes=[("hs%d" % dd, a0)])
                hk = lambda dd: [("hs%d" % dd, a0) for a0 in range(0, 64, 16)]
                fwd_AB(hs[0], hk(0), W2, "W2")
                fwd_AB(hs[1], hk(1), Bb, "Bb")
                for (cs, n) in CH7:
                    ncol = n * 65
                    ba, bak = kb.bank()
                    bb, bbk = kb.bank()
                    seq = [(4, W2, 0, "W2"), (5, W2, 1, "W2"), (4, Bb, 0, "Bb"), (5, Bb, 1, "Bb")]
                    for i, (ti, Bt, r, Bk) in enumerate(seq):
                        P.op("pe", lambda e, ba=ba, ti=ti, Bt=Bt, r=r, cs=cs, n=n, ncol=ncol, i=i: e.matmul(
                            ba[:, 0:ncol], lhsT=S3[:, ti, :], rhs=bcols(Bt, r, cs, n), start=(i == 0), stop=(i == 3)),
                            reads=[(Bk, r), "tabs"], writes=[bak])
                    seq = [(6, W2, 0, "W2"), (7, W2, 1, "W2"), (8, Bb, 0, "Bb"), (9, Bb, 1, "Bb")]
                    for i, (ti, Bt, r, Bk) in enumerate(seq):
                        P.op("pe", lambda e, bb=bb, ti=ti, Bt=Bt, r=r, cs=cs, n=n, ncol=ncol, i=i: e.matmul(
                            bb[:, 0:ncol], lhsT=S3[:, ti, :], rhs=bcols(Bt, r, cs, n), start=(i == 0), stop=(i == 3)),
                            reads=[(Bk, r), "tabs"], writes=[bbk])
                    P.op("dve", lambda e, ba=ba, o=o, cs=cs, n=n, ncol=ncol: e.tensor_tensor(
                        out=Ha[:, cs * 65:cs * 65 + ncol].rearrange("p (c k) -> p c k", k=65),
                        in0=ba[:, 0:ncol].rearrange("p (c k) -> p c k", k=65),
                        in1=skipb[o][:, cs:cs + n].unsqueeze(2).to_broadcast([128, n, 65]), op=ALU.add),
                        reads=[bak, "skipb%d" % o], writes=[("Ha", cs)])
                    P.op("act", lambda e, bb=bb, cs=cs, ncol=ncol: e.copy(Hb[:, cs * 65:cs * 65 + ncol], bb[:, 0:ncol]),
                         reads=[bbk], writes=[("Hb", cs)])
                sig = Zt
                fwd_AB(sig, ["Zt"], W2, "W2")
                for (cs, n) in CH7:
                    ncol = n * 65
                    bs, bsk = kb.bank()
                    bw, bwk = kb.bank()
                    for i, (ti, r) in enumerate([(0, 0), (1, 1)]):
                        P.op("pe", lambda e, bs=bs, ti=ti, r=r, cs=cs, n=n, ncol=ncol, i=i: e.matmul(
                            bs[:, 0:ncol], lhsT=S3[:, ti, :], rhs=bcols(W2, r, cs, n), start=(i == 0), stop=(i == 1)),
                            reads=[("W2", r), "tabs"], writes=[bsk])
                    for i, (ti, r) in enumerate([(2, 0), (3, 1)]):
                        P.op("pe", lambda e, bw=bw, ti=ti, r=r, cs=cs, n=n, ncol=ncol, i=i: e.matmul(
                            bw[:, 0:ncol], lhsT=S3[:, ti, :], rhs=bcols(W2, r, cs, n), start=(i == 0), stop=(i == 1)),
                            reads=[("W2", r), "tabs"], writes=[bwk])
                    ysl = Y[:, cs * 65:cs * 65 + ncol]
                    P.op("dve", lambda e, bw=bw, cs=cs, ncol=ncol: e.tensor_tensor(
                        out=tmp[:, 0:ncol], in0=bw[:, 0:ncol], in1=Hb[:, cs * 65:cs * 65 + ncol], op=ALU.mult),
                        reads=[bwk, ("Hb", cs)], writes=["tmp3"])
                    P.op("dve", lambda e, bs=bs, ysl=ysl, cs=cs, ncol=ncol: e.tensor_tensor(
                        out=ysl, in0=bs[:, 0:ncol], in1=Ha[:, cs * 65:cs * 65 + ncol], op=ALU.mult),
                        reads=[bsk, ("Ha", cs)], writes=[("Y", cs)])
                    P.op("pool", lambda e, ysl=ysl, ncol=ncol: e.tensor_tensor(out=ysl, in0=ysl, in1=tmp[:, 0:ncol], op=ALU.add),
                         reads=[("Y", cs), "tmp3"], writes=[("Y", cs)])
                ykeys = [("Y", cs) for (cs, n) in CH7]
                ekeys = []
                for c0 in range(0, 32, 4):
                    bk, bkk = kb.bank()
                    for j in range(4):
                        P.op("pe", lambda e, bk=bk, j=j, c=c0 + j: e.matmul(
                            bk[0:65, j * 128:(j + 1) * 128], lhsT=Y3[:, c, :], rhs=G[:, :], start=True, stop=True),
                            reads=ykeys + ["tabs"], writes=[bkk])
                    P.op("act", lambda e, bk=bk, c0=c0: e.copy(
                        E3[:, c0:c0 + 4, :], bk[0:65, :].rearrange("p (c k) -> p c k", k=128)),
                        reads=[bkk, "W1tokD", "W1tokP"], writes=[("W1", c0)])
                    ekeys.append(("W1", c0))
                Ere, Eim = E3[:, :, 0:64], E3[:, :, 64:128]
                P.op("dve", lambda e: e.tensor_tensor(out=t1i, in0=Ere, in1=TWic, op=ALU.mult), reads=ekeys + ["tabs"], writes=["T1"])
                P.op("pool", lambda e: e.tensor_tensor(out=t2i, in0=Eim, in1=TWis, op=ALU.mult), reads=ekeys + ["tabs"], writes=["T2"])
                P.op("dve", lambda e: e.tensor_tensor(out=Et4[:, 0], in0=t1i, in1=t2i, op=ALU.subtract),
                     reads=["T1", "T2"], writes=[("W2", 0)])
                P.op("pool", lambda e: e.tensor_tensor(out=Et4[:, 1], in0=Eim, in1=TWic, op=ALU.mult),
                     reads=ekeys + ["tabs"], writes=[("W2", 1), "W1tokP"])
                P.op("dve", lambda e: e.tensor_tensor(out=t1i, in0=Ere, in1=TWis, op=ALU.mult), reads=ekeys + ["tabs"], writes=["T1", "W1tokD"])
                P.op("pool", lambda e: e.tensor_tensor(out=Et4[:, 1], in0=Et4[:, 1], in1=t1i, op=ALU.add),
                     reads=["T1", ("W2", 1)], writes=[("W2", 1)])
                if o == 0:
                    for c8 in range(4):
                        bk, bkk = kb.bank()
                        for r in range(2):
                            P.op("pe", lambda e, bk=bk, r=r, c8=c8: e.matmul(
                                bk[0:64, :], lhsT=I3t[:, r, :], rhs=W2[0:65, r * 2048 + c8 * 512: r * 2048 + (c8 + 1) * 512], start=(r == 0), stop=(r == 1)),
                                reads=[("W2", r), "tabs"], writes=[bkk])
                        cs_ = slice(c8 * 8, (c8 + 1) * 8)
                        P.op("dve", lambda e, bk=bk, cs_=cs_: e.tensor_tensor(
                            out=Zt[:, :, cs_].rearrange("p a c -> p c a"),
                            in0=bk[0:64, :].rearrange("p (c a) -> p c a", a=64),
                            in1=x1t[:, :, cs_].rearrange("p a c -> p c a"), op=ALU.mult),
                            reads=[bkk, "x1t", "Zt"], writes=["Zt"])
                else:
                    x2v = x2f[:, :].rearrange("p (b a) -> p a b", a=64)
                    for a0 in range(0, 64, 8):
                        bk, bkk = kb.bank()
                        for ai in range(8):
                            for r in range(2):
                                P.op("pe", lambda e, bk=bk, ai=ai, a=a0 + ai, r=r: e.matmul(
                                    bk[0:32, ai * 64:(ai + 1) * 64], lhsT=Et4[:, r, :, a], rhs=I3t[:, r, :],
                                    start=(r == 0), stop=(r == 1)), reads=[("W2", r), "tabs"], writes=[bkk])
                        P.op("dve", lambda e, bk=bk, a0=a0: e.tensor_tensor(
                            out=x2v[:, a0:a0 + 8, :], in0=bk[0:32, :].rearrange("p (a b) -> p a b", b=64),
                            in1=x2v[:, a0:a0 + 8, :], op=ALU.mult), reads=[bkk, "x2f"], writes=["x2f"])
                    P.dma("sp", d["ZH_fm"][c0g:c0g + 32, :], x2f[:], reads=["x2f"], writes=["ZH_fm"])
        print("stage3", P.emit_stage())


DECAY_C = -math.exp(-0.5)


def stage4(kb):
    nc, P = kb.nc, kb.P
    d = kb.d
    with contextlib.ExitStack() as s:
        sb = lambda n, sh, dt=F32: K.sb(s, nc, n, sh, dt)
        ident = sb("ident", [128, 128])
        onesblk = sb("onesblk", [128, 128])
        mSU, mIU, mSM, Istk = sb("mSU", [128, 512]), sb("mIU", [128, 512]), sb("mSM", [128, 512]), sb("Istk", [128, 512])
        cmask = sb("cmask", [128, 1024])
        w2s, a2s = sb("w2s", [128, 512]), sb("a2s", [128, 512])
        w0s, a0s = sb("w0s", [128, 2, 4]), sb("a0s", [128, 2, 4])
        kks, kas = sb("kks", [128, 4]), sb("kas", [128, 4])
        S0T = sb("S0T", [128, 512])
        U = [sb("U%d" % i, [128, 14, 256]) for i in range(2)]
        nm7 = ("Qh", "Rh", "Bh", "Kh", "Bt", "Kt")
        OUT = [{n: sb("%s%d" % (n, i), [128, 4, 256]) for n in nm7} for i in range(2)]
        gE = [sb("gE%d" % i, [128, 16]) for i in range(2)]
        Yout = [sb("Yout%d" % i, [128, 4, 256]) for i in range(2)]
        tn = ("thT", )
        thT = sb("thT", [128, 256])
        TM = {n: sb("p_" + n, [128, 4, 256]) for n in ("lw", "a", "tk", "sq", "kk", "kd", "b", "cum", "cum2", "e")}
        totT = sb("totT", [128, 16])
        st_names = ("Vt", "Btt", "Ktt", "Nrb", "Mak", "Nrk", "Xs", "SA", "tmpS",
                    "Pu0", "Pu1", "Pm0", "Pm1", "T0", "T1", "Tt0", "Tt1")
        bfn = ("Pu0", "Pu1", "Pm0", "Pm1", "T0", "T1", "Tt0", "Tt1")
        ST = {n: sb("c_" + n, [128, 512], BF16 if n in bfn else F32) for n in st_names}
        ST["TtF"] = sb("c_TtF", [128, 512])
        Istk_bf = sb("Istk_bf", [128, 512], BF16)
        for nm, t_ in (("ident", ident), ("onesblk", onesblk), ("mSU", mSU), ("mIU", mIU), ("mSM", mSM), ("Istk", Istk), ("cmask", cmask)):
            P.dma("sp", t_[:], d[nm], writes=["consts"])
        P.dma("sp", w2s[:], d["rw_w2"], writes=["consts"])
        P.dma("sp", a2s[:], d["rw_a2"], writes=["consts"])
        P.dma("sp", w0s[:], d["rw_w0"], writes=["consts"])
        P.dma("sp", a0s[:], d["rw_a0"], writes=["consts"])
        P.dma("sp", kks[:], d["rw_k_k"], writes=["consts"])
        P.dma("sp", kas[:], d["rw_k_a"], writes=["consts"])
        P.op("pool", lambda e: e.memset(S0T[:], 0.0), writes=["S0T"])
        P.op("act", lambda e: e.copy(Istk_bf[:, :], Istk[:, :]), reads=["consts"], writes=["Istk_bf"])
        urv = d["UR_fm"].rearrange("(j p) t -> p j t", p=128)
        kk_bc = kks[:, :].unsqueeze(2).to_broadcast([128, 4, 256])
        ka_bc = kas[:, :].unsqueeze(2).to_broadcast([128, 4, 256])
        fl = lambda t_: t_[:].rearrange("p h t -> p (h t)")

        def prep(dr, blk):
            Ud, Uk = U[dr], "U%d" % dr
            O = OUT[dr]
            ok = lambda n: "%s%d" % (n, dr)
            dR = slice(dr * 64, (dr + 1) * 64)
            t0 = blk * 256
            P.dma("sp", Ud[:, 0:7, :], urv[:, 0:7, t0:t0 + 256], reads=["UR_fm"], writes=[Uk])
            P.dma("sp", Ud[:, 7:14, :], urv[:, 7:14, t0:t0 + 256], reads=["UR_fm", Uk], writes=[Uk])
            r_, k_, v_ = Ud[:, 0:4, :], Ud[:, 4:8, :], Ud[:, 8:12, :]
            P.op("act", lambda e: e.activation(out=thT[dR, :], in_=Ud[dR, 12, :], func=AF.Tanh), reads=[Uk], writes=["thT"])
            for (wsb, rhs_ap, rkeys, w0t, outn) in ((w2s, thT[dR, :], ["thT"], w0s, "lw"), (a2s, Ud[dR, 13, :], [Uk], a0s, "a")):
                for half in range(2):
                    bk, bkk = kb.bank()
                    for j in range(2):
                        hq = half * 2 + j
                        P.op("pe", lambda e, bk=bk, j=j, hq=hq, wsb=wsb, rhs_ap=rhs_ap: e.matmul(
                            bk[:, j * 256:(j + 1) * 256], lhsT=wsb[dR, hq * 128:(hq + 1) * 128], rhs=rhs_ap,
                            start=True, stop=True, tile_position=(dr * 64, 0)), reads=rkeys + ["consts"], writes=[bkk])
                    for j in range(2):
                        hq = half * 2 + j
                        P.op("act", lambda e, bk=bk, j=j, hq=hq, w0t=w0t, outn=outn: e.activation(
                            out=TM[outn][:, hq, :], in_=bk[:, j * 256:(j + 1) * 256], func=AF.Sigmoid,
                            bias=w0t[:, dr, hq:hq + 1], scale=1.0), reads=[bkk, "consts"], writes=[("p_" + outn, hq)])
            lwk = [("p_lw", hq) for hq in range(4)]
            ak = [("p_a", hq) for hq in range(4)]
            P.op("dve", lambda e: e.tensor_scalar(out=fl(TM["lw"]), in0=fl(TM["lw"]), scalar1=DECAY_C, scalar2=None, op0=ALU.mult),
                 reads=lwk, writes=lwk)
            P.op("dve", lambda e: e.tensor_tensor(out=TM["tk"][:], in0=k_, in1=kk_bc, op=ALU.mult), reads=[Uk, "consts"], writes=["p_tk"])
            P.op("pool", lambda e: e.tensor_tensor(out=TM["sq"][:], in0=TM["tk"][:], in1=TM["tk"][:], op=ALU.mult), reads=["p_tk"], writes=["p_sq"])
            for half in range(2):
                bk, bkk = kb.bank()
                for j in range(2):
                    hq = half * 2 + j
                    P.op("pe", lambda e, bk=bk, j=j, hq=hq: e.matmul(
                        bk[:, j * 256:(j + 1) * 256], lhsT=onesblk[:, :], rhs=TM["sq"][:, hq, :], start=True, stop=True),
                        reads=["p_sq", "consts"], writes=[bkk])
                P.op("act", lambda e, bk=bk, half=half: e.activation(
                    out=TM["kk"][:, half * 2:half * 2 + 2, :], in_=bk[:, :].rearrange("p (h t) -> p h t", h=2), func=AF.Sqrt),
                    reads=[bkk], writes=[("p_kk", half)])
            kkk = [("p_kk", 0), ("p_kk", 1)]
            P.op("dve", lambda e: e.tensor_scalar(out=fl(TM["kk"]), in0=fl(TM["kk"]), scalar1=1e-12, scalar2=None, op0=ALU.max), reads=kkk, writes=kkk)
            P.op("dve", lambda e: e.reciprocal(fl(TM["kk"]), fl(TM["kk"])), reads=kkk, writes=kkk)
            P.op("pool", lambda e: e.tensor_tensor(out=TM["kk"][:], in0=TM["kk"][:], in1=TM["tk"][:], op=ALU.mult), reads=kkk + ["p_tk"], writes=kkk)
            P.op("dve", lambda e: e.scalar_tensor_tensor(out=TM["kd"][:], in0=TM["a"][:], scalar=-1.0, in1=ka_bc, op0=ALU.add, op1=ALU.mult),
                 reads=ak + ["consts"], writes=["p_kd"])
            P.op("dve", lambda e: e.scalar_tensor_tensor(out=TM["kd"][:], in0=TM["kd"][:], scalar=1.0, in1=k_, op0=ALU.add, op1=ALU.mult),
                 reads=["p_kd", Uk], writes=["p_kd"])
            P.op("pool", lambda e: e.tensor_tensor(out=TM["b"][:], in0=TM["kk"][:], in1=TM["a"][:], op=ALU.mult), reads=kkk + ak, writes=["p_b"])
            P.op("dve", lambda e: e.tensor_tensor_scan(out=fl(TM["cum"]), data0=cmask[:, :], data1=fl(TM["lw"]), initial=0.0,
                                                       op0=ALU.mult, op1=ALU.add), reads=lwk + ["consts"], writes=["p_cum"])
            cum3 = fl(TM["cum"]).rearrange("p (c i) -> p c i", i=64)
            P.op("act", lambda e: e.copy(totT[:, :], cum3[:, :, 63]), reads=["p_cum"], writes=["totT"])
            tot_bc = totT[:, :].unsqueeze(2).to_broadcast([128, 16, 64])
            c2 = fl(TM["cum2"]).rearrange("p (c i) -> p c i", i=64)
            if dr == 1:
                P.op("dve", lambda e: e.tensor_tensor(out=c2, in0=tot_bc, in1=cum3, op=ALU.subtract), reads=["totT", "p_cum"], writes=["p_cum2"])
                P.op("dve", lambda e: e.tensor_tensor(out=TM["cum"][:], in0=TM["cum2"][:], in1=TM["lw"][:], op=ALU.add),
                     reads=["p_cum2"] + lwk, writes=["p_cum"])
            P.op("act", lambda e: e.activation(out=gE[dr][:, :], in_=totT[:, :], func=AF.Exp), reads=["totT"], writes=["gE%d" % dr])
            P.op("act", lambda e: e.activation(out=TM["e"][:], in_=TM["cum"][:], func=AF.Exp), reads=["p_cum"], writes=["p_e"])
            P.op("pool", lambda e: e.tensor_tensor(out=O["Rh"][:], in0=r_, in1=TM["e"][:], op=ALU.mult), reads=[Uk, "p_e"], writes=[ok("Rh")])
            P.op("act", lambda e: e.activation(out=TM["e"][:], in_=TM["cum"][:], func=AF.Exp, scale=-1.0), reads=["p_cum"], writes=["p_e"])
            P.op("dve", lambda e: e.tensor_tensor(out=O["Bh"][:], in0=TM["b"][:], in1=TM["e"][:], op=ALU.mult), reads=["p_b", "p_e"], writes=[ok("Bh")])
            P.op("pool", lambda e: e.tensor_tensor(out=O["Kh"][:], in0=TM["kd"][:], in1=TM["e"][:], op=ALU.mult), reads=["p_kd", "p_e"], writes=[ok("Kh")])
            P.op("dve", lambda e: e.tensor_tensor(out=TM["cum2"][:], in0=TM["cum"][:], in1=TM["lw"][:], op=ALU.subtract),
                 reads=["p_cum"] + lwk, writes=["p_cum2"])
            P.op("act", lambda e: e.activation(out=TM["e"][:], in_=TM["cum2"][:], func=AF.Exp), reads=["p_cum2"], writes=["p_e"])
            P.op("pool", lambda e: e.tensor_tensor(out=O["Qh"][:], in0=TM["kk"][:], in1=TM["e"][:], op=ALU.mult), reads=kkk + ["p_e"], writes=[ok("Qh")])
            P.op("dve", lambda e: e.tensor_tensor(out=c2, in0=tot_bc, in1=fl(TM["cum"]).rearrange("p (c i) -> p c i", i=64), op=ALU.subtract),
                 reads=["totT", "p_cum"], writes=["p_cum2"])
            P.op("act", lambda e: e.activation(out=TM["e"][:], in_=TM["cum2"][:], func=AF.Exp), reads=["p_cum2"], writes=["p_e"])
            P.op("dve", lambda e: e.tensor_tensor(out=O["Bt"][:], in0=TM["b"][:], in1=TM["e"][:], op=ALU.mult), reads=["p_b", "p_e"], writes=[ok("Bt")])
            P.op("pool", lambda e: e.tensor_tensor(out=O["Kt"][:], in0=TM["kd"][:], in1=TM["e"][:], op=ALU.mult), reads=["p_kd", "p_e"], writes=[ok("Kt")])

        heads = [(dr, hq, hp) for dr in range(2) for hq in range(4) for hp in range(2)]

        def per_head(bk, bkk, fn_list, reads):
            nacc = len(fn_list)
            for (dr, hq, hp) in heads:
                hpR = slice(hp * 64, (hp + 1) * 64)
                cb = (dr * 4 + hq) * 64
                for i, (lf, rf) in enumerate(fn_list):
                    lt, rt = lf(dr, hq, hpR, cb), rf(dr, hq, hpR, cb)
                    P.op("pe", lambda e, bk=bk, hpR=hpR, cb=cb, lt=lt, rt=rt, i=i, hp=hp: e.matmul(
                        bk[hpR, cb:cb + 64], lhsT=lt, rhs=rt,
                        start=(i == 0), stop=(i == nacc - 1), tile_position=(hp * 64, hp * 64)),
                        reads=reads, writes=[bkk])

        stk = lambda name: (lambda dr, hq, hpR, cb: ST[name][hpR, cb:cb + 64])
        S0f = lambda dr, hq, hpR, cb: S0T[hpR, cb:cb + 64]
        Isf = lambda dr, hq, hpR, cb: Istk_bf[hpR, cb:cb + 64]
        evi = [0]

        def evac(dst_key, dst_ap, bk, bkk, extra_reads=(), eng=None):
            e_ = eng or ("act" if evi[0] % 2 == 0 else "dve")
            evi[0] += 1
            if e_ == "act":
                P.op("act", lambda e: e.copy(dst_ap, bk[:, :]), reads=[bkk] + list(extra_reads), writes=[dst_key])
            else:
                P.op("dve", lambda e: e.tensor_copy(dst_ap, bk[:, :]), reads=[bkk] + list(extra_reads), writes=[dst_key])

        for j in range(getattr(kb, "rw_steps", 16)):
            blks = (j, 15 - j)
            for dr in range(2):
                prep(dr, blks[dr])
            for lc in range(4 if getattr(kb, "rw_phase", 9) > 0 else 0):
                lcd = (lc, 3 - lc)
                cs = [slice(lcd[dr] * 64, (lcd[dr] + 1) * 64) for dr in range(2)]
                fm = lambda name, cs=cs: (lambda dr, hq, hpR, cb, cs=cs: OUT[dr][name][hpR, hq, cs[dr]])
                Vf = lambda dr, hq, hpR, cb, cs=cs: U[dr][hpR, 8 + hq, cs[dr]]
                okeys = lambda n: ["%s0" % n, "%s1" % n]
                for (name, srcf, rk) in (("Vt", Vf, ["U0", "U1"]), ("Btt", fm("Bt"), okeys("Bt")), ("Ktt", fm("Kt"), okeys("Kt"))):
                    bk, bkk = kb.bank()
                    for (dr, hq, hp) in heads:
                        hpR = slice(hp * 64, (hp + 1) * 64)
                        cb = (dr * 4 + hq) * 64
                        src_ap = srcf(dr, hq, hpR, cb)
                        P.op("pe", lambda e, bk=bk, hpR=hpR, cb=cb, src_ap=src_ap, hp=hp: e.matmul(
                            bk[hpR, cb:cb + 64], lhsT=src_ap, rhs=ident[hpR, hpR], start=True, stop=True,
                            tile_position=(hp * 64, hp * 64)), reads=rk + ["consts"], writes=[bkk])
                    evac(name, ST[name][:, :], bk, bkk)
                if getattr(kb, "rw_phase", 9) < 2:
                    continue
                grams = (("Pu0", "Bh", "Qh", mSU), ("Nrb", "Bh", "Rh", mIU), ("Mak", "Kh", "Qh", mSU), ("Nrk", "Kh", "Rh", mIU), ("Pm0", "Qh", "Bh", mSM))
                for (dst, ln, rn, msk) in grams:
                    bk, bkk = kb.bank()
                    per_head(bk, bkk, [(fm(ln), fm(rn))], okeys(ln) + okeys(rn))
                    P.op("dve", lambda e, bk=bk, dst=dst, msk=msk: e.tensor_tensor(out=ST[dst][:, :], in0=bk[:, :], in1=msk[:, :], op=ALU.mult),
                         reads=[bkk, "consts"], writes=[dst])
                P.op("pool", lambda e: e.tensor_tensor(out=ST["T0"][:, :], in0=Istk[:, :], in1=ST["Pm0"][:, :], op=ALU.subtract),
                     reads=["Pm0", "consts"], writes=["T0"])
                P.op("pool", lambda e: e.tensor_tensor(out=ST["Tt0"][:, :], in0=Istk[:, :], in1=ST["Pu0"][:, :], op=ALU.subtract),
                     reads=["Pu0", "consts"], writes=["Tt0"])
                if getattr(kb, "rw_phase", 9) < 3:
                    continue
                cur = 0
                for lvl in range(1, 6):
                    nxt = 1 - cur
                    pu, pm, tt_, t_ = "Pu%d" % cur, "Pm%d" % cur, "Tt%d" % cur, "T%d" % cur
                    pun, pmn, ttn, tn_ = "Pu%d" % nxt, "Pm%d" % nxt, "Tt%d" % nxt, "T%d" % nxt
                    bk, bkk = kb.bank()
                    per_head(bk, bkk, [(stk(pm), stk(pu))], [pm, pu])
                    evac(pun, ST[pun][:, :], bk, bkk)
                    if lvl < 5:
                        bk, bkk = kb.bank()
                        per_head(bk, bkk, [(stk(pu), stk(pm))], [pm, pu])
                        evac(pmn, ST[pmn][:, :], bk, bkk)
                    bk, bkk = kb.bank()
                    per_head(bk, bkk, [(stk(t_), stk(pun)), (stk(t_), Isf)], [t_, pun, "Istk_bf"])
                    if lvl == 5:
                        ttn = "TtF"
                    evac(ttn, ST[ttn][:, :], bk, bkk)
                    if lvl < 5:
                        bk, bkk = kb.bank()
                        per_head(bk, bkk, [(stk(tt_), stk(pmn)), (stk(tt_), Isf)], [tt_, pmn, "Istk_bf"])
                        evac(tn_, ST[tn_][:, :], bk, bkk)
                    cur = nxt
                ttf = "TtF"
                if getattr(kb, "rw_phase", 9) < 4:
                    continue
                bk, bkk = kb.bank()
                per_head(bk, bkk, [(fm("Qh"), S0f), (stk("Mak"), stk("Vt"))], okeys("Qh") + ["S0T", "Mak", "Vt"])
                evac("Xs", ST["Xs"][:, :], bk, bkk)
                bk, bkk = kb.bank()
                per_head(bk, bkk, [(stk(ttf), stk("Xs"))], [ttf, "Xs"])
                P.op("act", lambda e, bk=bk: e.mul(ST["SA"][:, :], bk[:, :], -1.0), reads=[bkk], writes=["SA"])
                if getattr(kb, "rw_phase", 9) < 5:
                    continue
                bk1, bkk1 = kb.bank()
                per_head(bk1, bkk1, [(S0f, fm("Rh"))], okeys("Rh") + ["S0T"])
                bk, bkk = kb.bank()
                per_head(bk, bkk, [(stk("SA"), stk("Nrb")), (stk("Vt"), stk("Nrk"))], ["SA", "Nrb", "Vt", "Nrk"])
                P.op("act", lambda e, bk1=bk1: e.copy(ST["Xs"][:, :], bk1[:, :]), reads=[bkk1], writes=["Xs"])
                for dr in range(2):
                    src = bk[:, dr * 256:(dr + 1) * 256].rearrange("p (h t) -> p h t", h=4)
                    src2 = ST["Xs"][:, dr * 256:(dr + 1) * 256].rearrange("p (h t) -> p h t", h=4)
                    dst = Yout[dr][:, :, cs[dr]]
                    P.op("dve", lambda e, src=src, src2=src2, dst=dst: e.tensor_tensor(out=dst, in0=src, in1=src2, op=ALU.add),
                         reads=[bkk, "Xs"], writes=["Yout%d" % dr])
                if getattr(kb, "rw_phase", 9) < 6:
                    continue
                bk, bkk = kb.bank()
                per_head(bk, bkk, [(stk("Btt"), stk("SA")), (stk("Ktt"), stk("Vt"))], ["Btt", "SA", "Ktt", "Vt"])
                for dr in range(2):
                    gsl = gE[dr][:, :].rearrange("p (h c) -> p h c", c=4)[:, :, lcd[dr]].unsqueeze(2).to_broadcast([128, 4, 64])
                    sl = slice(dr * 256, (dr + 1) * 256)
                    P.op("pool", lambda e, gsl=gsl, sl=sl: e.tensor_tensor(
                        out=ST["tmpS"][:, sl].rearrange("p (h v) -> p h v", h=4), in0=S0T[:, sl].rearrange("p (h v) -> p h v", h=4),
                        in1=gsl, op=ALU.mult), reads=["S0T", "gE%d" % dr], writes=["tmpS"])
                P.op("dve", lambda e, bk=bk: e.tensor_tensor(out=S0T[:, :], in0=bk[:, :], in1=ST["tmpS"][:, :], op=ALU.add),
                     reads=[bkk, "tmpS", "S0T"], writes=["S0T"])
            for dr in range(2):
                t0 = blks[dr] * 256
                dst = d["YF_fm" if dr == 0 else "YB_fm"].rearrange("(h p) t -> p h t", p=128)[:, :, t0:t0 + 256]
                P.dma("sp", dst, Yout[dr][:], reads=["Yout%d" % dr], writes=["Y_fm%d" % dr])
        print("stage4", P.emit_stage())


def stage5(kb):
    nc, P = kb.nc, kb.P
    d = kb.d
    with contextlib.ExitStack() as s:
        sb = lambda n, sh, dt=F32: K.sb(s, nc, n, sh, dt)
        onesblk = sb("onesblk", [128, 128])
        a2s, g2s = sb("a2s", [128, 512]), sb("g2s", [128, 512])
        a0s = sb("a0s", [128, 2, 4])
        kas, rks, gnw, gnb = sb("kas", [128, 4]), sb("rks", [128, 4]), sb("gnw", [128, 4]), sb("gnb", [128, 4])
        epsb = sb("epsb", [128, 1])
        U5 = sb("U5", [128, 15, 512])
        yf, yb = sb("yf", [128, 4, 512]), sb("yb", [128, 4, 512])
        sq = sb("sq5", [128, 4, 512])
        af, ab = sb("af", [128, 4, 512]), sb("ab", [128, 4, 512])
        rb = sb("rb", [128, 4, 512])
        zr = sb("zr", [128, 4, 512])
        sgd = sb("sgd", [128, 512])
        mean = [sb("mean%d" % i, [128, 512]) for i in range(2)]
        var = [sb("var%d" % i, [128, 512]) for i in range(2)]
        bon = [sb("bon%d" % i, [128, 512]) for i in range(2)]
        P.dma("sp", onesblk[:], d["onesblk"], writes=["consts"])
        for nm, t_ in (("rw_a2", a2s), ("rw_g2", g2s), ("rw_a0", a0s), ("rw_k_a", kas), ("rw_r_k", rks), ("rw_gn_w", gnw), ("rw_gn_b", gnb)):
            P.dma("sp", t_[:], d[nm], writes=["consts"])
        P.op("pool", lambda e: e.memset(epsb[:], GN_EPS), writes=["consts"])
        urv = d["UR_fm"].rearrange("(j p) t -> p j t", p=128)
        yfv = d["YF_fm"].rearrange("(h p) t -> p h t", p=128)
        ybv = d["YB_fm"].rearrange("(h p) t -> p h t", p=128)
        zrv = d["ZR_fm"].rearrange("(h p) t -> p h t", p=128)
        ka_bc = kas[:, :].unsqueeze(2).to_broadcast([128, 4, 512])
        rk_bc = rks[:, :].unsqueeze(2).to_broadcast([128, 4, 512])
        for tb in range(8):
            ts_ = slice(tb * 512, (tb + 1) * 512)
            P.dma("sp", U5[:, 0:5, :], urv[:, 0:5, ts_], reads=["UR_fm"], writes=["U5"])
            P.dma("sp", U5[:, 5:10, :], urv[:, 5:10, ts_], reads=["UR_fm", "U5"], writes=["U5"])
            P.dma("sp", U5[:, 10:15, :], urv[:, 10:15, ts_], reads=["UR_fm", "U5"], writes=["U5"])
            P.dma("sp", yf[:], yfv[:, :, ts_], reads=["Y_fm0"], writes=["yf"])
            P.dma("sp", yb[:], ybv[:, :, ts_], reads=["Y_fm1"], writes=["yb"])
            r_, k_, v_ = U5[:, 0:4, :], U5[:, 4:8, :], U5[:, 8:12, :]
            P.op("pool", lambda e: e.tensor_tensor(out=yf[:], in0=yf[:], in1=yb[:], op=ALU.add), reads=["yf", "yb"], writes=["yf"])
            P.op("pool", lambda e: e.tensor_tensor(out=sq[:], in0=yf[:], in1=yf[:], op=ALU.mult), reads=["yf"], writes=["sq5"])
            for dr, at, atk in ((0, af, "af"), (1, ab, "ab")):
                dR = slice(dr * 64, (dr + 1) * 64)
                for hq in range(4):
                    bk, bkk = kb.bank()
                    P.op("pe", lambda e, bk=bk, hq=hq, dR=dR, dr=dr: e.matmul(
                        bk[:, :], lhsT=a2s[dR, hq * 128:(hq + 1) * 128], rhs=U5[dR, 13, :], start=True, stop=True,
                        tile_position=(dr * 64, 0)), reads=["U5", "consts"], writes=[bkk])
                    P.op("act", lambda e, bk=bk, hq=hq, at=at, dr=dr: e.activation(
                        out=at[:, hq, :], in_=bk[:, :], func=AF.Sigmoid, bias=a0s[:, dr, hq:hq + 1], scale=1.0),
                        reads=[bkk, "consts"], writes=[(atk, hq)])
            afk = [("af", h) for h in range(4)]
            abk = [("ab", h) for h in range(4)]
            P.op("pool", lambda e: e.tensor_tensor(out=af[:], in0=af[:], in1=ab[:], op=ALU.add), reads=afk + abk, writes=afk)
            P.op("dve", lambda e: e.scalar_tensor_tensor(out=af[:], in0=af[:], scalar=-2.0, in1=ka_bc, op0=ALU.add, op1=ALU.mult),
                 reads=afk + ["consts"], writes=afk)
            P.op("dve", lambda e: e.scalar_tensor_tensor(out=af[:], in0=af[:], scalar=2.0, in1=k_, op0=ALU.add, op1=ALU.mult),
                 reads=afk + ["U5"], writes=afk)
            P.op("pool", lambda e: e.tensor_tensor(out=rb[:], in0=r_, in1=rk_bc, op=ALU.mult), reads=["U5", "consts"], writes=["rb"])
            P.op("pool", lambda e: e.tensor_tensor(out=rb[:], in0=rb[:], in1=af[:], op=ALU.mult), reads=["rb"] + afk, writes=["rb"])
            P.op("act", lambda e: e.activation(out=sgd[:], in_=U5[:, 14, :], func=AF.Sigmoid), reads=["U5"], writes=["sgd"])
            for hq in range(4):
                i2 = hq % 2
                bm, bmk = kb.bank()
                bs, bsk = kb.bank()
                bb_, bbk = kb.bank()
                bg, bgk = kb.bank()
                P.op("pe", lambda e, bm=bm, hq=hq: e.matmul(bm[:, :], lhsT=onesblk[:, :], rhs=yf[:, hq, :], start=True, stop=True),
                     reads=["yf", "consts"], writes=[bmk])
                P.op("pe", lambda e, bs=bs, hq=hq: e.matmul(bs[:, :], lhsT=onesblk[:, :], rhs=sq[:, hq, :], start=True, stop=True),
                     reads=["sq5", "consts"], writes=[bsk])
                P.op("pe", lambda e, bb_=bb_, hq=hq: e.matmul(bb_[:, :], lhsT=onesblk[:, :], rhs=rb[:, hq, :], start=True, stop=True),
                     reads=["rb", "consts"], writes=[bbk])
                P.op("pe", lambda e, bg=bg, hq=hq: e.matmul(bg[:, :], lhsT=g2s[:, hq * 128:(hq + 1) * 128], rhs=sgd[:, :], start=True, stop=True),
                     reads=["sgd", "consts"], writes=[bgk])
                mk, vk, bk_ = "mean%d" % i2, "var%d" % i2, "bon%d" % i2
                P.op("act", lambda e, bm=bm, i2=i2: e.mul(mean[i2][:], bm[:, :], 1.0 / 64), reads=[bmk], writes=[mk])
                P.op("dve", lambda e, i2=i2: e.tensor_tensor(out=var[i2][:], in0=mean[i2][:], in1=mean[i2][:], op=ALU.mult), reads=[mk], writes=[vk])
                P.op("dve", lambda e, bs=bs, i2=i2: e.scalar_tensor_tensor(
                    out=var[i2][:], in0=bs[:, :], scalar=1.0 / 64, in1=var[i2][:], op0=ALU.mult, op1=ALU.subtract),
                    reads=[bsk, vk], writes=[vk])
                P.op("act", lambda e, i2=i2: e.activation(out=var[i2][:], in_=var[i2][:], func=AF.Sqrt, bias=epsb[:, 0:1], scale=1.0),
                     reads=[vk, "consts"], writes=[vk])
                P.op("dve", lambda e, i2=i2: e.reciprocal(var[i2][:], var[i2][:]), reads=[vk], writes=[vk])
                P.op("pool", lambda e, i2=i2, hq=hq: e.tensor_tensor(out=mean[i2][:], in0=yf[:, hq, :], in1=mean[i2][:], op=ALU.subtract),
                     reads=["yf", mk], writes=[mk])
                P.op("pool", lambda e, i2=i2: e.tensor_tensor(out=mean[i2][:], in0=mean[i2][:], in1=var[i2][:], op=ALU.mult),
                     reads=[mk, vk], writes=[mk])
                P.op("act", lambda e, i2=i2, hq=hq: e.activation(out=mean[i2][:], in_=mean[i2][:], func=AF.Identity,
                                                                  bias=gnb[:, hq:hq + 1], scale=gnw[:, hq:hq + 1]),
                     reads=[mk, "consts"], writes=[mk])
                P.op("dve", lambda e, bb_=bb_, i2=i2, hq=hq: e.tensor_tensor(out=bon[i2][:], in0=bb_[:, :], in1=U5[:, 8 + hq, :], op=ALU.mult),
                     reads=[bbk, "U5"], writes=[bk_])
                P.op("pool", lambda e, i2=i2: e.tensor_tensor(out=bon[i2][:], in0=bon[i2][:], in1=mean[i2][:], op=ALU.add),
                     reads=[bk_, mk], writes=[bk_])
                P.op("dve", lambda e, bg=bg, i2=i2, hq=hq: e.tensor_tensor(out=zr[:, hq, :], in0=bg[:, :], in1=bon[i2][:], op=ALU.mult),
                     reads=[bgk, bk_], writes=[("zr", hq)])
            P.dma("sp", zrv[:, :, ts_], zr[:], reads=[("zr", h) for h in range(4)], writes=["ZR_fm"])
        print("stage5", P.emit_stage())


def build(mode="full", dbg=()):
    kb = K(dbg)
    nc = kb.nc
    d = {}
    kb.d = d
    d["xT"] = kb.din("xT", [1024, 4096])
    d["x"] = kb.din("x", [4096, 1024])
    d["w_in"] = kb.din("w_in", [1024, 5504])
    d["cw"] = kb.din("cw", [128, 3, 12])
    d["cb"] = kb.din("cb", [128, 12])
    d["mu"] = kb.din("mu", [128, 15])
    d["ident"] = kb.din("ident", [128, 128])
    for nm, sh in (("w_hy_out", [512, 1024]), ("w_rw_out", [512, 1024]), ("w_o", [1024, 1024]),
                   ("ln1_w", [1, 1024]), ("ln1_b", [1, 1024]), ("ln2_w", [1, 1024]), ("ln2_b", [1, 1024]),
                   ("ffn_w_gate", [1024, 2816]), ("ffn_w_up", [1024, 2816]), ("ffn_w_down", [2816, 1024])):
        d[nm] = kb.din(nm, sh)
    ct_shapes = {"featsT": [33, 4096], "win": [16, 64, 2048], "F1": [64, 130], "TWf": [64, 2, 65], "S3": [64, 10, 128],
                 "G": [128, 128], "TWi": [65, 2, 64], "I3t": [65, 2, 64], "onesblk": [128, 128]}
    for nm, sh in ct_shapes.items():
        d[nm] = kb.din(nm, sh)
    for nm, sh in (("hy_filt_w1", [33, 64]), ("hy_filt_w2", [64, 64]), ("hy_filt_w3", [64, 64]), ("hy_fb", [64, 3]),
                   ("hy_sf", [64, 3]), ("hy_filt_w4", [64, 2048]), ("hy_skip", [2, 512])):
        d[nm] = kb.din(nm, sh)
    for nm, sh in (("mSU", [128, 512]), ("mIU", [128, 512]), ("mSM", [128, 512]), ("Istk", [128, 512]), ("cmask", [128, 1024]),
                   ("rw_w2", [128, 512]), ("rw_a2", [128, 512]), ("rw_g2", [128, 512]), ("rw_w0", [128, 2, 4]), ("rw_a0", [128, 2, 4]),
                   ("rw_k_k", [128, 4]), ("rw_k_a", [128, 4]), ("rw_r_k", [128, 4]), ("rw_gn_w", [128, 4]), ("rw_gn_b", [128, 4])):
        d[nm] = kb.din(nm, sh)
    d["YF_fm"] = kb.dscr("YF_fm", [512, 4096])
    d["YB_fm"] = kb.dscr("YB_fm", [512, 4096])
    d["H3T"] = kb.dscr("H3T", [64, 4096])
    skel = mode in ("skel", "s16")
    d["UHg"] = kb.dscr("UHg", [32, 64, 64 * 32])
    d["X2_fm"] = kb.dscr("X2_fm", [512, 4096])
    d["UR_fm"] = kb.dscr("UR_fm", [1920, 4096])
    d["GT_fm"] = kb.dscr("GT_fm", [2048, 4096])
    d["ZH_fm"] = kb.dscr("ZH_fm", [512, 4096], ext_in=skel)
    d["ZR_fm"] = kb.dscr("ZR_fm", [512, 4096], ext_in=skel)
    d["X1_tm"] = kb.dscr("X1_tm", [4096, 1024])
    d["out"] = nc.dram_tensor("out", [4096, 1024], F32, kind="ExternalOutput").ap()
    if mode == "s1" or mode.startswith("s1:"):
        if mode != "s1":
            kb.cc_list = [int(v) for v in mode.split(":")[1].split(",")]
        stage1(kb)
    elif mode.startswith("rw:"):
        kb.rw_steps = int(mode.split(":")[1])
        if len(mode.split(":")) > 2:
            kb.rw_phase = int(mode.split(":")[2])
        if len(mode.split(":")) > 3:
            kb.f_sel = [int(c) for c in mode.split(":")[3]]
        stage1(kb)
        stage4(kb)
    elif mode == "rw":
        stage1(kb)
        stage4(kb)
        stage5(kb)
    elif mode == "full":
        stage1(kb)
        stage2(kb)
        stage3(kb)
        stage4(kb)
        stage5(kb)
        stage6(kb)
        stage7(kb)
    elif mode == "hy":
        stage1(kb)
        stage2(kb)
        stage3(kb)
    elif mode == "s16":
        stage1(kb)
        stage6(kb)
    else:
        stage1(kb)
        stage6(kb)
        stage7(kb)
    kb.st.close()
    return nc


def host_inputs(inputs, b, mode="full"):
    g = lambda k: np.asarray(inputs[k][0], dtype=np.float32)
    ct = const_tables()
    x = np.asarray(inputs["x"][b], dtype=np.float32)
    m = {}
    m["xT"] = np.ascontiguousarray(x.T)
    m["x"] = np.ascontiguousarray(x)
    m["w_in"] = g("w_in")
    m["cw"] = np.ascontiguousarray(g("hy_conv_w").reshape(3, 12, 128).transpose(2, 0, 1))
    m["cb"] = np.ascontiguousarray(g("hy_conv_b").reshape(12, 128).T)
    m["mu"] = np.ascontiguousarray(g("rw_mu").reshape(15, 128).T)
    m["ident"] = ct["ident"]
    for nm in ("w_hy_out", "w_rw_out", "w_o", "ffn_w_gate", "ffn_w_up", "ffn_w_down"):
        m[nm] = g(nm)
    for nm in ("ln1_w", "ln1_b", "ln2_w", "ln2_b"):
        m[nm] = g(nm).reshape(1, 1024)
    for nm in ("featsT", "win", "F1", "TWf", "S3", "G", "TWi", "I3t", "onesblk"):
        m[nm] = ct[nm]
    for nm in ("hy_filt_w1", "hy_filt_w2", "hy_filt_w3", "hy_filt_w4", "hy_skip"):
        m[nm] = g(nm)
    m["hy_fb"] = np.ascontiguousarray(np.stack([g("hy_filt_b1"), g("hy_filt_b2"), g("hy_filt_b3")], axis=1))
    m["hy_sf"] = np.ascontiguousarray(g("hy_sin_freq").T)
    for nm in ("mSU", "mIU", "mSM", "Istk", "cmask"):
        m[nm] = ct[nm]
    m["rw_w2"] = np.ascontiguousarray(g("rw_w2").reshape(128, 512))
    m["rw_a2"] = np.ascontiguousarray(g("rw_a2").reshape(128, 512))
    m["rw_g2"] = g("rw_g2")
    m["rw_w0"] = np.ascontiguousarray(g("rw_w0").reshape(2, 4, 128).transpose(2, 0, 1))
    m["rw_a0"] = np.ascontiguousarray(g("rw_a0").reshape(2, 4, 128).transpose(2, 0, 1))
    for nm in ("rw_k_k", "rw_k_a", "rw_r_k", "rw_gn_w", "rw_gn_b"):
        m[nm] = np.ascontiguousarray(g(nm).reshape(4, 128).T)
    return m


_NC_CACHE = {}


def kernel(**inputs):
    if "nc" not in _NC_CACHE:
        _NC_CACHE["nc"] = build("full")
    nc = _NC_CACHE["nc"]
    in_maps = [host_inputs(inputs, b) for b in range(8)]
    res = run_bass_kernel_spmd(nc, in_maps, core_ids=list(range(8)))
    out = np.stack([np.asarray(r["out"], dtype=np.float32) for r in res.results], axis=0)
    return out
```

```python
import contextlib
import math
import numpy as np
import concourse.bass as bass
import concourse.mybir as mybir
from concourse.bass_utils import run_bass_kernel_spmd

F32 = mybir.dt.float32
BF16 = mybir.dt.bfloat16
ALU = mybir.AluOpType
AF = mybir.ActivationFunctionType

ENGS = ("pe", "act", "dve", "pool", "sp")
NDMASEM = 12
L = 4096
D = 1024
ALPHA = 2.0 ** 0.25
LN_EPS = 1e-5
GN_EPS = 64e-5
MAGIC = 12582912.0
PI = float(np.pi)


class Prog:
    def __init__(self, nc, stack):
        self.nc = nc
        self.csem = {e: stack.enter_context(nc.semaphore("c_" + e)) for e in ENGS}
        self.dsem = {}
        for e in ("sp", "pool", "act"):
            for k in range(NDMASEM):
                self.dsem[(e, k)] = stack.enter_context(nc.semaphore("d_%s_%d" % (e, k)))
        self.ctot = {e: 0 for e in ENGS}
        self.dtot = {e: 0 for e in ENGS}
        self._reset()

    def _reset(self):
        self.ops = {e: [] for e in ENGS}
        self.last_w = {}
        self.readers = {}

    def _add(self, eng, fn, reads, writes, dma=False):
        idx = len(self.ops[eng])
        ev = (eng, idx)
        deps = set()
        for r in reads:
            w = self.last_w.get(r)
            if w is not None:
                deps.add(w)
        for w_ in writes:
            w = self.last_w.get(w_)
            if w is not None:
                deps.add(w)
            for rd in self.readers.get(w_, ()):
                deps.add(rd)
        deps.discard(ev)
        self.ops[eng].append(dict(fn=fn, deps=deps, dma=dma))
        for r in reads:
            self.readers.setdefault(r, []).append(ev)
        for w_ in writes:
            self.last_w[w_] = ev
            self.readers[w_] = []
        return ev

    def op(self, eng, fn, reads=(), writes=()):
        return self._add(eng, fn, tuple(reads), tuple(writes), dma=False)

    def dma(self, eng, out, in_, reads=(), writes=(), **kw):
        return self._add(eng, lambda e: e.dma_start(out=out, in_=in_, **kw),
                         tuple(reads), tuple(writes), dma=True)

    def emit_stage(self):
        nc = self.nc
        ops = self.ops
        needed = {e: set() for e in ENGS}
        for e in ENGS:
            for o in ops[e]:
                for (de, di) in o["deps"]:
                    if not ops[de][di]["dma"]:
                        if de == "pe" and e == "pe":
                            continue
                        needed[de].add(di)
        for e in ENGS:
            for i in range(len(ops[e]) - 1, -1, -1):
                if not ops[e][i]["dma"]:
                    needed[e].add(i)
                    break
        count_at = {e: {} for e in ENGS}
        cend = {}
        for e in ENGS:
            c = self.ctot[e]
            for i, o in enumerate(ops[e]):
                if (not o["dma"]) and i in needed[e]:
                    c += 1
                    count_at[e][i] = c
            cend[e] = c
        dma_info = {}
        dend = {}
        for e in ENGS:
            n = self.dtot[e]
            for i, o in enumerate(ops[e]):
                if o["dma"]:
                    dma_info[(e, i)] = (e, n % NDMASEM, 16 * (n // NDMASEM + 1), n)
                    n += 1
            dend[e] = n
        csem, dsem = self.csem, self.dsem
        ftargets = []
        for e in ENGS:
            n = dend[e]
            for k in range(min(n, NDMASEM)):
                ftargets.append((dsem[(e, k)], 16 * ((n - 1 - k) // NDMASEM + 1)))

        def run_engine(ename, eng):
            known = {e: -1 for e in ENGS}
            known_dma = set()
            for i, o in enumerate(ops[ename]):
                for (de, di) in sorted(o["deps"]):
                    if ops[de][di]["dma"]:
                        if (de, di) in known_dma:
                            continue
                        q, k, tgt, n = dma_info[(de, di)]
                        eng.wait_ge(dsem[(q, k)], tgt)
                        known_dma.add((de, di))
                    else:
                        if de == "pe" and ename == "pe":
                            continue
                        if known[de] >= di:
                            continue
                        eng.wait_ge(csem[de], count_at[de][di])
                        known[de] = di
                if o["dma"]:
                    q, k, tgt, n = dma_info[(ename, i)]
                    if n >= NDMASEM:
                        eng.wait_ge(dsem[(q, k)], tgt - 16)
                    o["fn"](eng).then_inc(dsem[(q, k)], 16)
                else:
                    ins = o["fn"](eng)
                    if i in count_at[ename]:
                        ins.then_inc(csem[ename], 1)
            for e in ENGS:
                if cend[e] > 0:
                    eng.wait_ge(csem[e], cend[e])
            for (s, v) in ftargets:
                eng.wait_ge(s, v)

        with nc.Block() as block:
            @block.tensor
            def _(eng):
                run_engine("pe", eng)

            @block.scalar
            def _(eng):
                run_engine("act", eng)

            @block.vector
            def _(eng):
                run_engine("dve", eng)

            @block.gpsimd
            def _(eng):
                run_engine("pool", eng)

            @block.sync
            def _(eng):
                run_engine("sp", eng)

        self.ctot = cend
        self.dtot = dend
        n_ops = {e: len(ops[e]) for e in ENGS}
        self._reset()
        return n_ops


def const_tables():
    f = np.float64
    c = {}
    c["ident"] = np.eye(128)
    ob = np.zeros((128, 128))
    ob[:64, :64] = 1.0
    ob[64:, 64:] = 1.0
    c["onesblk"] = ob
    t = np.linspace(0.0, 1.0, L)[:, None]
    w = 2.0 * math.pi * np.arange(L) / L
    fr = np.linspace(1e-4, 15.0, 16)
    ang = w[:, None] * fr[None, :]
    c["featsT"] = np.concatenate([t, np.cos(ang), -np.sin(ang)], axis=-1).T
    deltas = np.abs(np.linspace(math.log(1e-2) / 1.5, math.log(1e-2) / 0.3, 512))
    win = np.exp(-t * deltas[None, :])
    c["win"] = win.reshape(64, 64, 16, 32).transpose(2, 0, 1, 3).reshape(16, 64, 64 * 32)
    i64 = np.arange(64)
    k1 = np.arange(65)
    F1 = np.zeros((64, 2, 65))
    th = 2 * math.pi * np.outer(i64, k1) / 128.0
    F1[:, 0], F1[:, 1] = np.cos(th), -np.sin(th)
    c["F1"] = F1.reshape(64, 130)
    th = 2 * math.pi * np.outer(i64, k1) / 8192.0
    c["TWf"] = np.stack([np.cos(th), np.sin(th)], axis=1)
    th = 2 * math.pi * np.outer(i64, i64) / 64.0
    C, S = np.cos(th), np.sin(th)
    cat = lambda p, q: np.concatenate([p, q], axis=1)
    c["S3"] = np.stack([cat(C, -S), cat(S, C), cat(-S, C), cat(C, S), cat(C, C), cat(S, S),
                        cat(S, -S), cat(-C, C), cat(-S, S), cat(C, -C)], axis=1)
    G = np.zeros((128, 128))
    G[:64, :64], G[:64, 64:], G[64:, :64], G[64:, 64:] = C, S, -S, C
    c["G"] = G
    th = 2 * math.pi * np.outer(k1, i64) / 8192.0
    c["TWi"] = np.stack([np.cos(th), np.sin(th)], axis=1)
    ck = np.full(65, 2.0)
    ck[0] = ck[64] = 1.0
    th = 2 * math.pi * np.outer(k1, i64) / 128.0
    c["I3t"] = np.stack([ck[:, None] / 8192.0 * np.cos(th), -ck[:, None] / 8192.0 * np.sin(th)], axis=1)
    row = np.arange(64)[:, None]
    col = np.arange(64)[None, :]
    def stk(f0, f1):
        m = np.zeros((2, 64, 2, 4, 64))
        m[:, :, 0, :, :] = f0[None, :, None, :]
        m[:, :, 1, :, :] = f1[None, :, None, :]
        return m.reshape(128, 512)
    c["mSU"] = stk((row < col) * 1.0, (row > col) * 1.0)
    c["mIU"] = stk((row <= col) * 1.0, (row >= col) * 1.0)
    c["mSM"] = stk((col < row) * 1.0, (col > row) * 1.0)
    c["Istk"] = stk((row == col) * 1.0, (row == col) * 1.0)
    cm = np.ones((128, 1024))
    cm[:, ::64] = 0.0
    c["cmask"] = cm
    return {k: np.ascontiguousarray(v, dtype=np.float32) for k, v in c.items()}


class K:
    def __init__(self, dbg=()):
        self.dbg = set(dbg)
        self.nc = bass.Bass("TRN2", target_bir_lowering=False)
        self.st = contextlib.ExitStack()
        self.P = Prog(self.nc, self.st)
        self.inputs = {}
        self.ps = [self.st.enter_context(self.nc.psum_tensor("ps%d" % i, [128, 512], F32)) for i in range(8)]
        self.psk = ["ps%d" % i for i in range(8)]
        self.bank_i = 0

    def bank(self):
        i = self.bank_i % 8
        self.bank_i += 1
        return self.ps[i], self.psk[i]

    def din(self, name, shape, dt=F32):
        return self.nc.dram_tensor(name, list(shape), dt, kind="ExternalInput").ap()

    def dscr(self, name, shape, dt=F32, ext_in=False):
        kind = "Internal"
        if name in self.dbg:
            kind = "ExternalOutput"
        if ext_in:
            kind = "ExternalInput"
        return self.nc.dram_tensor(name, list(shape), dt, kind=kind).ap()

    _uid = [0]

    @staticmethod
    def sb(s, nc, name, shape, dt=F32):
        K._uid[0] += 1
        return s.enter_context(nc.sbuf_tensor("s%d_%s" % (K._uid[0], name), list(shape), dt))


def layer_norm_rows(P, nc, pre, outt, lnw, lnb, sm, junk, epsb, key, eng2="pool"):
    P.op("dve", lambda e: e.memset(sm[:, 0:2], 0.0), writes=[key + "sm"])
    P.op("act", lambda e: e.activation(out=junk[:], in_=pre[:], func=AF.Identity, accum_out=sm[:, 0:1]),
         reads=[key + "pre", key + "sm"], writes=[key + "sm", key + "junk"])
    P.op("act", lambda e: e.activation(out=junk[:], in_=pre[:], func=AF.Square, accum_out=sm[:, 1:2]),
         reads=[key + "pre", key + "sm", key + "junk"], writes=[key + "sm", key + "junk"])
    P.op("dve", lambda e: e.tensor_scalar(out=sm[:, 2:4], in0=sm[:, 0:2], scalar1=1.0 / 1024, scalar2=None, op0=ALU.mult),
         reads=[key + "sm"], writes=[key + "sm"])
    P.op("dve", lambda e: e.tensor_tensor(out=sm[:, 4:5], in0=sm[:, 2:3], in1=sm[:, 2:3], op=ALU.mult),
         reads=[key + "sm"], writes=[key + "sm"])
    P.op("dve", lambda e: e.tensor_tensor(out=sm[:, 5:6], in0=sm[:, 3:4], in1=sm[:, 4:5], op=ALU.subtract),
         reads=[key + "sm"], writes=[key + "sm"])
    P.op("act", lambda e: e.activation(out=sm[:, 6:7], in_=sm[:, 5:6], func=AF.Sqrt, bias=epsb[:, 0:1], scale=1.0),
         reads=[key + "sm"], writes=[key + "sm"])
    P.op("dve", lambda e: e.reciprocal(sm[:, 7:8], sm[:, 6:7]), reads=[key + "sm"], writes=[key + "sm"])
    P.op("dve", lambda e: e.tensor_scalar(out=outt[:], in0=pre[:], scalar1=sm[:, 2:3], scalar2=sm[:, 7:8],
                                          op0=ALU.subtract, op1=ALU.mult),
         reads=[key + "pre", key + "sm"], writes=[key + "out"])
    P.op(eng2, lambda e: e.tensor_tensor(out=outt[:], in0=outt[:], in1=lnw[:], op=ALU.mult),
         reads=[key + "out", "lnw"], writes=[key + "out"])
    P.op(eng2, lambda e: e.tensor_tensor(out=outt[:], in0=outt[:], in1=lnb[:], op=ALU.add),
         reads=[key + "out", "lnb"], writes=[key + "out"])


def stage1(kb):
    nc, P = kb.nc, kb.P
    d = kb.d
    with contextlib.ExitStack() as s:
        sb = lambda n, sh, dt=F32: K.sb(s, nc, n, sh, dt)
        xT = sb("xT", [128, 8, 4096], BF16)
        wbuf = [sb("wb%d" % i, [128, 8, 512], BF16) for i in range(2)]
        raw = [sb("raw%d" % i, [128, 4098], F32) for i in range(2)]
        o = [sb("o%d" % i, [128, 4096], F32) for i in range(2)]
        uht = sb("uht", [64, 4, 64, 32], F32)
        pcs = sb("pcs", [128, 4, 27], F32)
        mus = sb("mus", [128, 15], F32)
        ident = sb("ident", [128, 128], F32)
        P.dma("sp", ident[:], d["ident"], writes=["ident"])
        P.dma("sp", pcs[:, 0:3, 0:12], d["cw"], writes=["pcs"])
        P.dma("sp", pcs[:, 3, 0:12], d["cb"], reads=["pcs"], writes=["pcs"])
        P.dma("sp", mus[:], d["mu"], writes=["mus"])
        P.op("dve", lambda e: e.memset(pcs[:, 3, 12:27], 0.0), reads=["pcs"], writes=["pcs"])
        P.op("dve", lambda e: e.tensor_scalar(out=pcs[:, 0, 12:27], in0=mus[:], scalar1=0.5, scalar2=None, op0=ALU.mult),
             reads=["mus", "pcs"], writes=["pcs"])
        P.op("dve", lambda e: e.tensor_scalar(out=pcs[:, 2, 12:27], in0=mus[:], scalar1=0.5, scalar2=None, op0=ALU.mult),
             reads=["mus", "pcs"], writes=["pcs"])
        P.op("dve", lambda e: e.tensor_scalar(out=pcs[:, 1, 12:27], in0=mus[:], scalar1=-1.0, scalar2=1.0, op0=ALU.mult, op1=ALU.add),
             reads=["mus", "pcs"], writes=["pcs"])
        for i in range(2):
            P.op("pool", lambda e, i=i: e.memset(raw[i][:, 0:1], 0.0), writes=["rawpad%d" % i])
            P.op("pool", lambda e, i=i: e.memset(raw[i][:, 4097:4098], 0.0), writes=["rawpad%d" % i])
        xTv = d["xT"].rearrange("(k p) t -> p k t", p=128)
        for k in range(8):
            P.dma("pool", xT[:, k, :], xTv[:, k, :], writes=[("xT", k)])
        wv = d["w_in"].rearrange("(k p) c -> p k c", p=128)
        evi = 0
        for cc in getattr(kb, "cc_list", range(43)):
            wb = cc // 4
            if cc % 4 == 0 or getattr(kb, "cc_list", None) is not None:
                ncol = min(512, 5504 - wb * 512)
                P.dma("pool", wbuf[wb % 2][:, :, 0:ncol], wv[:, :, wb * 512: wb * 512 + ncol], writes=["wb%d" % (wb % 2)])
            wt = wbuf[wb % 2]
            wk = "wb%d" % (wb % 2)
            c0 = (cc % 4) * 128
            rb = raw[cc % 2]
            ob = o[cc % 2]
            rk = "raw%d" % (cc % 2)
            ok = "o%d" % (cc % 2)
            gate = cc >= 27
            for tb in range(8):
                bk, bkk = kb.bank()
                for k in range(8):
                    P.op("pe", lambda e, bk=bk, wt=wt, k=k, c0=c0, tb=tb: e.matmul(
                        bk[:, :], lhsT=wt[:, k, c0:c0 + 128], rhs=xT[:, k, tb * 512:(tb + 1) * 512],
                        start=(k == 0), stop=(k == 7)), reads=[wk, ("xT", k)], writes=[bkk])
                if gate:
                    P.op("act", lambda e, bk=bk, ob=ob, tb=tb: e.activation(
                        out=ob[:, tb * 512:(tb + 1) * 512], in_=bk[:, :], func=AF.Sigmoid),
                        reads=[bkk], writes=[(ok, tb)])
                else:
                    eng = "act" if evi % 2 == 0 else "dve"
                    evi += 1
                    if eng == "act":
                        P.op("act", lambda e, bk=bk, rb=rb, tb=tb: e.copy(rb[:, 1 + tb * 512: 1 + (tb + 1) * 512], bk[:, :]),
                             reads=[bkk], writes=[(rk, tb)])
                    else:
                        P.op("dve", lambda e, bk=bk, rb=rb, tb=tb: e.tensor_copy(rb[:, 1 + tb * 512: 1 + (tb + 1) * 512], bk[:, :]),
                             reads=[bkk], writes=[(rk, tb)])
            okeys = [(ok, tb) for tb in range(8)]
            rkeys = [(rk, tb) for tb in range(8)] + ["rawpad%d" % (cc % 2)]
            if not gate:
                P.op("act", lambda e, rb=rb, ob=ob, cc=cc: e.activation(
                    out=ob[:, :], in_=rb[:, 1:4097], func=AF.Identity, bias=pcs[:, 3, cc:cc + 1], scale=pcs[:, 1, cc:cc + 1]),
                    reads=rkeys + ["pcs"], writes=okeys)
                P.op("dve", lambda e, rb=rb, ob=ob, cc=cc: e.scalar_tensor_tensor(
                    out=ob[:, :], in0=rb[:, 0:4096], scalar=pcs[:, 0, cc:cc + 1], in1=ob[:, :], op0=ALU.mult, op1=ALU.add),
                    reads=rkeys + ["pcs"] + okeys, writes=okeys)
                P.op("dve", lambda e, rb=rb, ob=ob, cc=cc: e.scalar_tensor_tensor(
                    out=ob[:, :], in0=rb[:, 2:4098], scalar=pcs[:, 2, cc:cc + 1], in1=ob[:, :], op0=ALU.mult, op1=ALU.add),
                    reads=rkeys + ["pcs"] + okeys, writes=okeys)
            if cc < 8:
                obv = ob[:, :].rearrange("p (b a) -> p a b", a=64)
                for a0 in range(0, 64, 4):
                    bk, bkk = kb.bank()
                    for ai in range(4):
                        P.op("pe", lambda e, bk=bk, obv=obv, a=a0 + ai, ai=ai: e.transpose(
                            bk[0:64, ai * 128:(ai + 1) * 128], obv[:, a, :], ident[:, :]),
                            reads=okeys + ["ident"], writes=[bkk])
                    eng = "act" if (a0 // 4) % 2 == 0 else "dve"
                    outap = uht[:, :, a0:a0 + 4, :].rearrange("p g a c -> p a g c")
                    inap = bk[0:64, :].rearrange("p (a g c) -> p a g c", a=4, g=4)
                    if eng == "act":
                        P.op("act", lambda e, outap=outap, inap=inap: e.copy(outap, inap), reads=[bkk], writes=[("uht", a0)])
                    else:
                        P.op("dve", lambda e, outap=outap, inap=inap: e.tensor_copy(outap, inap), reads=[bkk], writes=[("uht", a0)])
                P.dma("sp", d["UHg"][cc * 4:(cc + 1) * 4].rearrange("g b n -> b g n"),
                      uht[:].rearrange("b g a c -> b g (a c)"),
                      reads=[("uht", a0) for a0 in range(0, 64, 4)], writes=["UHg"])
            elif cc < 12:
                P.dma("sp", d["X2_fm"][(cc - 8) * 128:(cc - 7) * 128, :], ob[:, :], reads=okeys, writes=["X2_fm"])
            elif cc < 27:
                P.dma("sp", d["UR_fm"][(cc - 12) * 128:(cc - 11) * 128, :], ob[:, :], reads=okeys, writes=["UR_fm"])
            else:
                P.dma("sp", d["GT_fm"][(cc - 27) * 128:(cc - 26) * 128, :], ob[:, :], reads=okeys, writes=["GT_fm"])
        print("stage1", P.emit_stage())


def stage6(kb):
    nc, P = kb.nc, kb.P
    d = kb.d
    with contextlib.ExitStack() as s:
        sb = lambda n, sh, dt=F32: K.sb(s, nc, n, sh, dt)
        zhT = sb("zhT", [128, 4, 4096], BF16)
        zrT = sb("zrT", [128, 4, 4096], BF16)
        why = sb("why", [128, 4, 1024], BF16)
        wrw = sb("wrw", [128, 4, 1024], BF16)
        wo = sb("wo", [128, 8, 1024], BF16)
        mT = sb("mT", [128, 8, 4096], BF16)
        gh = [sb("gh%d" % i, [128, 512]) for i in range(2)]
        gr = [sb("gr%d" % i, [128, 512]) for i in range(2)]
        t1 = [sb("t1_%d" % i, [128, 512]) for i in range(2)]
        t2 = [sb("t2_%d" % i, [128, 512]) for i in range(2)]
        xt = [sb("xt%d" % i, [128, 1024]) for i in range(2)]
        pre = sb("pre", [128, 1024])
        junk = sb("junk", [128, 1024])
        x1o = [sb("x1o0", [128, 1024])] * 2
        lnw = sb("lnw", [128, 1024])
        lnb = sb("lnb", [128, 1024])
        sm = sb("sm", [128, 8])
        epsb = sb("epsb", [128, 1])
        P.op("pool", lambda e: e.memset(epsb[:], LN_EPS), writes=["epsb"])
        P.dma("sp", lnw[:], d["ln1_w"].partition_broadcast(128), writes=["lnw"])
        P.dma("sp", lnb[:], d["ln1_b"].partition_broadcast(128), writes=["lnb"])
        for k in range(4):
            P.dma("pool", zhT[:, k, :], d["ZH_fm"][k * 128:(k + 1) * 128, :], reads=["ZH_fm"], writes=["zhT"])
            P.dma("pool", zrT[:, k, :], d["ZR_fm"][k * 128:(k + 1) * 128, :], reads=["ZR_fm"], writes=["zrT"])
        P.dma("pool", why[:], d["w_hy_out"].rearrange("(k p) c -> p k c", p=128), writes=["why"])
        P.dma("pool", wrw[:], d["w_rw_out"].rearrange("(k p) c -> p k c", p=128), writes=["wrw"])
        P.dma("pool", wo[:], d["w_o"].rearrange("(k p) c -> p k c", p=128), writes=["wo"])
        it = 0
        for cc in range(8):
            for tb in range(8):
                i2 = it % 2
                it += 1
                ts_ = slice(tb * 512, (tb + 1) * 512)
                P.dma("sp", gh[i2][:], d["GT_fm"][cc * 128:(cc + 1) * 128, ts_], reads=["GT_fm"], writes=["gh%d" % i2])
                P.dma("sp", gr[i2][:], d["GT_fm"][1024 + cc * 128:1024 + (cc + 1) * 128, ts_], reads=["GT_fm"], writes=["gr%d" % i2])
                bh, bhk = kb.bank()
                br, brk = kb.bank()
                for k in range(4):
                    P.op("pe", lambda e, bh=bh, k=k, cc=cc, ts_=ts_: e.matmul(
                        bh[:, :], lhsT=why[:, k, cc * 128:(cc + 1) * 128], rhs=zhT[:, k, ts_], start=(k == 0), stop=(k == 3)),
                        reads=["why", "zhT"], writes=[bhk])
                for k in range(4):
                    P.op("pe", lambda e, br=br, k=k, cc=cc, ts_=ts_: e.matmul(
                        br[:, :], lhsT=wrw[:, k, cc * 128:(cc + 1) * 128], rhs=zrT[:, k, ts_], start=(k == 0), stop=(k == 3)),
                        reads=["wrw", "zrT"], writes=[brk])
                P.op("dve", lambda e, i2=i2, bh=bh: e.tensor_tensor(out=t1[i2][:], in0=bh[:, :], in1=gh[i2][:], op=ALU.mult),
                     reads=[bhk, "gh%d" % i2], writes=["t1_%d" % i2])
                P.op("dve", lambda e, i2=i2, br=br: e.tensor_tensor(out=t2[i2][:], in0=br[:, :], in1=gr[i2][:], op=ALU.mult),
                     reads=[brk, "gr%d" % i2], writes=["t2_%d" % i2])
                P.op("pool", lambda e, i2=i2, cc=cc, ts_=ts_: e.tensor_tensor(out=mT[:, cc, ts_], in0=t1[i2][:], in1=t2[i2][:], op=ALU.add),
                     reads=["t1_%d" % i2, "t2_%d" % i2], writes=[("mT", cc, tb)])
        mkeys = [("mT", cc, tb) for cc in range(8) for tb in range(8)]
        for blk in range(32):
            i2 = blk % 2
            P.dma("sp", xt[i2][:], d["x"][blk * 128:(blk + 1) * 128, :], writes=["xt%d" % i2])
            for nh in range(2):
                bk, bkk = kb.bank()
                for k in range(8):
                    P.op("pe", lambda e, bk=bk, k=k, blk=blk, nh=nh: e.matmul(
                        bk[:, :], lhsT=mT[:, k, blk * 128:(blk + 1) * 128], rhs=wo[:, k, nh * 512:(nh + 1) * 512],
                        start=(k == 0), stop=(k == 7)), reads=mkeys + ["wo"] if k == 0 else ["wo"], writes=[bkk])
                P.op("dve", lambda e, bk=bk, i2=i2, nh=nh: e.scalar_tensor_tensor(
                    out=pre[:, nh * 512:(nh + 1) * 512], in0=xt[i2][:, nh * 512:(nh + 1) * 512], scalar=ALPHA,
                    in1=bk[:, :], op0=ALU.mult, op1=ALU.add), reads=[bkk, "xt%d" % i2], writes=["Lpre"])
            layer_norm_rows(P, nc, pre, x1o[i2], lnw, lnb, sm, junk, epsb, "L")
            P.dma("sp", d["X1_tm"][blk * 128:(blk + 1) * 128, :], x1o[i2][:], reads=["Lout"], writes=["X1_tm"])
        print("stage6", P.emit_stage())


def stage7(kb):
    nc, P = kb.nc, kb.P
    d = kb.d
    with contextlib.ExitStack() as s:
        sb = lambda n, sh, dt=F32: K.sb(s, nc, n, sh, dt)
        x1q = sb("x1q", [128, 8, 1024])
        x1T = sb("x1T", [128, 8, 1024], BF16)
        hT = sb("hT", [128, 22, 1024], BF16)
        wdn = sb("wdn", [128, 22, 1024], BF16)
        wg = [sb("wg%d" % i, [128, 8, 128], BF16) for i in range(2)]
        wu = [sb("wu%d" % i, [128, 8, 128], BF16) for i in range(2)]
        sg = [sb("sg%d" % i, [128, 512]) for i in range(2)]
        pre = sb("pre7", [128, 1024])
        junk = sb("junk7", [128, 1024])
        xo = sb("xo7", [128, 1024])
        lnw = sb("lnw7", [128, 1024])
        lnb = sb("lnb7", [128, 1024])
        sm = sb("sm7", [128, 8])
        epsb = sb("epsb7", [128, 1])
        ident = sb("ident7", [128, 128])
        P.dma("sp", ident[:], d["ident"], writes=["ident"])
        P.op("pool", lambda e: e.memset(epsb[:], LN_EPS), writes=["epsb"])
        P.dma("sp", lnw[:], d["ln2_w"].partition_broadcast(128), writes=["lnw"])
        P.dma("sp", lnb[:], d["ln2_b"].partition_broadcast(128), writes=["lnb"])
        for f in range(22):
            P.dma("pool", wdn[:, f, :], d["ffn_w_down"][f * 128:(f + 1) * 128, :], writes=["wdn"])
        wgv = d["ffn_w_gate"].rearrange("(k p) f -> p k f", p=128)
        wuv = d["ffn_w_up"].rearrange("(k p) f -> p k f", p=128)
        wi = 0
        for q in range(4):
            for blk in range(8):
                r0 = q * 1024 + blk * 128
                P.dma("sp", x1q[:, blk, :], d["X1_tm"][r0:r0 + 128, :], reads=["X1_tm"], writes=[("x1q", blk)])
            for blk in range(8):
                for dc0 in range(0, 8, 4):
                    bk, bkk = kb.bank()
                    for j in range(4):
                        dc = dc0 + j
                        P.op("pe", lambda e, bk=bk, blk=blk, dc=dc, j=j: e.transpose(
                            bk[:, j * 128:(j + 1) * 128], x1q[:, blk, dc * 128:(dc + 1) * 128], ident[:, :]),
                            reads=[("x1q", blk), "ident"], writes=[bkk])
                    outap = x1T[:, dc0:dc0 + 4, blk * 128:(blk + 1) * 128]
                    inap = bk[:, :].rearrange("p (j t) -> p j t", j=4)
                    if (blk + dc0 // 4) % 2 == 0:
                        P.op("act", lambda e, outap=outap, inap=inap: e.copy(outap, inap), reads=[bkk], writes=[("x1T", blk, dc0)])
                    else:
                        P.op("dve", lambda e, outap=outap, inap=inap: e.tensor_copy(outap, inap), reads=[bkk], writes=[("x1T", blk, dc0)])
            xkeys = [("x1T", blk, dc0) for blk in range(8) for dc0 in (0, 4)]
            for f in range(22):
                i2 = wi % 2
                wi += 1
                P.dma("pool", wg[i2][:], wgv[:, :, f * 128:(f + 1) * 128], writes=["wg%d" % i2])
                P.dma("pool", wu[i2][:], wuv[:, :, f * 128:(f + 1) * 128], writes=["wu%d" % i2])
                for tb in range(2):
                    ts_ = slice(tb * 512, (tb + 1) * 512)
                    bg, bgk = kb.bank()
                    bu, buk = kb.bank()
                    for k in range(8):
                        P.op("pe", lambda e, bg=bg, k=k, i2=i2, ts_=ts_: e.matmul(
                            bg[:, :], lhsT=wg[i2][:, k, :], rhs=x1T[:, k, ts_], start=(k == 0), stop=(k == 7)),
                            reads=(xkeys if k == 0 else []) + ["wg%d" % i2], writes=[bgk])
                    for k in range(8):
                        P.op("pe", lambda e, bu=bu, k=k, i2=i2, ts_=ts_: e.matmul(
                            bu[:, :], lhsT=wu[i2][:, k, :], rhs=x1T[:, k, ts_], start=(k == 0), stop=(k == 7)),
                            reads=(xkeys if k == 0 else []) + ["wu%d" % i2], writes=[buk])
                    P.op("act", lambda e, bg=bg, tb=tb: e.activation(out=sg[tb][:], in_=bg[:, :], func=AF.Silu),
                         reads=[bgk], writes=["sg%d" % tb])
                    P.op("dve", lambda e, bu=bu, tb=tb, f=f, ts_=ts_: e.tensor_tensor(
                        out=hT[:, f, ts_], in0=bu[:, :], in1=sg[tb][:], op=ALU.mult),
                        reads=[buk, "sg%d" % tb], writes=[("hT", f, tb)])
            hkeys = [("hT", f, tb) for f in range(22) for tb in range(2)]
            for blk in range(8):
                for nh in range(2):
                    bk, bkk = kb.bank()
                    for f in range(22):
                        P.op("pe", lambda e, bk=bk, f=f, blk=blk, nh=nh: e.matmul(
                            bk[:, :], lhsT=hT[:, f, blk * 128:(blk + 1) * 128], rhs=wdn[:, f, nh * 512:(nh + 1) * 512],
                            start=(f == 0), stop=(f == 21)), reads=(hkeys if f == 0 else []) + ["wdn"], writes=[bkk])
                    P.op("dve", lambda e, bk=bk, blk=blk, nh=nh: e.scalar_tensor_tensor(
                        out=pre[:, nh * 512:(nh + 1) * 512], in0=x1q[:, blk, nh * 512:(nh + 1) * 512], scalar=ALPHA,
                        in1=bk[:, :], op0=ALU.mult, op1=ALU.add), reads=[bkk, ("x1q", blk)], writes=["Mpre"])
                layer_norm_rows(P, nc, pre, xo, lnw, lnb, sm, junk, epsb, "M")
                r0 = q * 1024 + blk * 128
                P.dma("sp", d["out"][r0:r0 + 128, :], xo[:], reads=["Mout"], writes=["out"])
        print("stage7", P.emit_stage())


def stage2(kb):
    nc, P = kb.nc, kb.P
    d = kb.d
    with contextlib.ExitStack() as s:
        sb = lambda n, sh, dt=F32: K.sb(s, nc, n, sh, dt)
        feats = sb("feats", [33, 4096])
        hbuf = [sb("hb%d" % i, [64, 4096]) for i in range(2)]
        ws = [sb("fw1", [33, 64]), sb("fw2", [64, 64]), sb("fw3", [64, 64])]
        fb = sb("fb", [64, 3])
        sf = sb("sf", [64, 3])
        sfb = sb("sfb", [64, 3])
        arg = [sb("arg%d" % i, [64, 512]) for i in range(2)]
        kq = [sb("kq%d" % i, [64, 512]) for i in range(2)]
        P.dma("sp", feats[:], d["featsT"], writes=["feats"])
        for i, nm in enumerate(("hy_filt_w1", "hy_filt_w2", "hy_filt_w3")):
            P.dma("sp", ws[i][:], d[nm], writes=["fw%d" % i])
        P.dma("sp", fb[:], d["hy_fb"], writes=["fb"])
        P.dma("sp", sf[:], d["hy_sf"], writes=["sf"])
        P.op("dve", lambda e: e.tensor_tensor(out=sfb[:], in0=sf[:], in1=fb[:], op=ALU.mult), reads=["fb", "sf"], writes=["sfb"])
        it = 0
        for l in range(3):
            kdim = 33 if l == 0 else 64
            hin = feats if l == 0 else hbuf[(l - 1) % 2]
            hink = "feats" if l == 0 else "hb%d" % ((l - 1) % 2)
            hout = hbuf[l % 2]
            houtk = "hb%d" % (l % 2)
            for tb in range(8):
                i2 = it % 2
                it += 1
                ts_ = slice(tb * 512, (tb + 1) * 512)
                bk, bkk = kb.bank()
                P.op("pe", lambda e, bk=bk, l=l, kdim=kdim, hin=hin, ts_=ts_: e.matmul(
                    bk[0:64, :], lhsT=ws[l][0:kdim, :], rhs=hin[0:kdim, ts_], start=True, stop=True),
                    reads=["fw%d" % l] + [(hink, tb)] + ([hink] if l == 0 else []), writes=[bkk])
                P.op("dve", lambda e, bk=bk, l=l, i2=i2: e.tensor_scalar(
                    out=arg[i2][:], in0=bk[0:64, :], scalar1=sf[:, l:l + 1], scalar2=sfb[:, l:l + 1], op0=ALU.mult, op1=ALU.add),
                    reads=[bkk, "sf", "sfb"], writes=["arg%d" % i2])
                P.op("dve", lambda e, i2=i2: e.tensor_scalar(
                    out=kq[i2][:], in0=arg[i2][:], scalar1=1.0 / (2 * PI), scalar2=MAGIC, op0=ALU.mult, op1=ALU.add),
                    reads=["arg%d" % i2], writes=["kq%d" % i2])
                P.op("dve", lambda e, i2=i2: e.tensor_scalar(
                    out=kq[i2][:], in0=kq[i2][:], scalar1=-MAGIC, scalar2=None, op0=ALU.add),
                    reads=["kq%d" % i2], writes=["kq%d" % i2])
                P.op("dve", lambda e, i2=i2: e.scalar_tensor_tensor(
                    out=arg[i2][:], in0=kq[i2][:], scalar=-2 * PI, in1=arg[i2][:], op0=ALU.mult, op1=ALU.add),
                    reads=["kq%d" % i2, "arg%d" % i2], writes=["arg%d" % i2])
                P.op("act", lambda e, i2=i2, hout=hout, ts_=ts_: e.activation(out=hout[:, ts_], in_=arg[i2][:], func=AF.Sin),
                     reads=["arg%d" % i2], writes=[(houtk, tb)])
        P.dma("sp", d["H3T"], hbuf[0][:], reads=[("hb0", tb) for tb in range(8)], writes=["H3T"])
        print("stage2", P.emit_stage())


CH7 = [(0, 7), (7, 7), (14, 7), (21, 7), (28, 4)]


def stage3(kb):
    nc, P = kb.nc, kb.P
    d = kb.d
    with contextlib.ExitStack() as s:
        sb = lambda n, sh, dt=F32: K.sb(s, nc, n, sh, dt)
        h3T = sb("h3T", [64, 4096], BF16)
        fw4 = sb("fw4", [64, 2048], BF16)
        F1 = sb("F1", [64, 130], BF16)
        TWf = sb("TWf", [64, 2, 65])
        S3 = sb("S3", [64, 10, 128], BF16)
        G = sb("G", [128, 128], BF16)
        TWi = sb("TWi", [65, 2, 64])
        I3t = sb("I3t", [65, 2, 64], BF16)
        skipb = [sb("skipb%d" % o, [128, 32]) for o in range(2)]
        win = sb("win", [64, 64, 32])
        hs = [sb("hs%d" % i, [64, 64, 32], BF16) for i in range(2)]
        Zt = sb("Zt", [64, 64, 32], BF16)
        x1t = sb("x1t", [64, 64, 32])
        x2f = sb("x2f", [32, 4096])
        W1 = sb("W1", [128, 4160])
        W2 = sb("W2", [128, 4160], BF16)
        Bb = sb("Bb", [128, 4160], BF16)
        T1 = sb("T1", [128, 2080])
        T2 = sb("T2", [128, 2080])
        Ha = sb("Ha", [128, 2080])
        Hb = sb("Hb", [128, 2080])
        Y = sb("Y", [128, 2080], BF16)
        tmp = sb("tmp3", [128, 512])
        tmpb = sb("tmp3b", [128, 512])
        P.dma("pool", h3T[:], d["H3T"], reads=["H3T"], writes=["h3T"])
        P.dma("pool", fw4[:], d["hy_filt_w4"], writes=["fw4"])
        for nm, t_ in (("F1", F1), ("TWf", TWf), ("S3", S3), ("G", G), ("TWi", TWi), ("I3t", I3t)):
            P.dma("pool" if nm in ("F1", "S3", "G", "I3t") else "sp", t_[:], d[nm], writes=["tabs"])
        h3v = h3T[:, :].rearrange("p (b a) -> p a b", a=64)
        TWfc = TWf[:, 0, :].unsqueeze(1).to_broadcast([64, 32, 65])
        TWfs = TWf[:, 1, :].unsqueeze(1).to_broadcast([64, 32, 65])
        TWic = TWi[:, 0, :].unsqueeze(1).to_broadcast([65, 32, 64])
        TWis = TWi[:, 1, :].unsqueeze(1).to_broadcast([65, 32, 64])
        A3 = W1[0:64, :].rearrange("p (c k) -> p c k", k=130)
        E3 = W1[0:65, 0:4096].rearrange("p (c k) -> p c k", k=128)
        Et4 = W2[0:65, 0:4096].rearrange("p (r c k) -> p r c k", r=2, c=32)
        t1f = T1[0:64, :].rearrange("p (c k) -> p c k", k=65)
        t2f = T2[0:64, :].rearrange("p (c k) -> p c k", k=65)
        t1i = T1[0:65, 0:2048].rearrange("p (c k) -> p c k", k=64)
        t2i = T2[0:65, 0:2048].rearrange("p (c k) -> p c k", k=64)
        Y3 = Y[:, :].rearrange("p (c k) -> p c k", k=65)

        def fwd_AB(sig, sigkeys, Bt, Bkey):
            B4 = Bt[0:64, :].rearrange("p (r c k) -> p r c k", r=2, c=32)
            akeys = []
            for c0 in range(0, 32, 3):
                n = min(3, 32 - c0)
                bk, bkk = kb.bank()
                for j in range(n):
                    P.op("pe", lambda e, bk=bk, j=j, c=c0 + j: e.matmul(
                        bk[0:64, j * 130:(j + 1) * 130], lhsT=sig[:, :, c], rhs=F1[:, :], start=True, stop=True),
                        reads=list(sigkeys) + ["tabs"], writes=[bkk])
                P.op("act", lambda e, bk=bk, c0=c0, n=n: e.copy(
                    A3[:, c0:c0 + n, :], bk[0:64, 0:n * 130].rearrange("p (c k) -> p c k", k=130)),
                    reads=[bkk, "W1tokD", "W1tokP"], writes=[("W1", c0)])
                akeys.append(("W1", c0))
            Are, Aim = A3[:, :, 0:65], A3[:, :, 65:130]
            P.op("dve", lambda e: e.tensor_tensor(out=t1f, in0=Are, in1=TWfc, op=ALU.mult), reads=akeys + ["tabs"], writes=["T1"])
            P.op("pool", lambda e: e.tensor_tensor(out=t2f, in0=Aim, in1=TWfs, op=ALU.mult), reads=akeys + ["tabs"], writes=["T2"])
            P.op("dve", lambda e: e.tensor_tensor(out=B4[:, 0], in0=t1f, in1=t2f, op=ALU.add), reads=["T1", "T2"], writes=[(Bkey, 0)])
            P.op("pool", lambda e: e.tensor_tensor(out=t2f, in0=Aim, in1=TWfc, op=ALU.mult), reads=akeys + ["tabs"], writes=["T2", "W1tokP"])
            P.op("dve", lambda e: e.tensor_tensor(out=t1f, in0=Are, in1=TWfs, op=ALU.mult), reads=akeys + ["tabs"], writes=["T1", "W1tokD"])
            P.op("pool", lambda e: e.tensor_tensor(out=B4[:, 1], in0=t2f, in1=t1f, op=ALU.subtract),
                 reads=["T1", "T2"], writes=[(Bkey, 1)])

        def bcols(Bt, r, c0, n):
            return Bt[0:64, r * 2080 + c0 * 65: r * 2080 + (c0 + n) * 65]

        for g in range(16):
            c0g = g * 32
            P.dma("sp", win[:].rearrange("p a c -> p (a c)"), d["win"][g], writes=["win"])
            P.dma("pool", Zt[:].rearrange("p a c -> p (a c)"), d["UHg"][g], reads=["UHg"], writes=["Zt"])
            P.dma("sp", x1t[:].rearrange("p a c -> p (a c)"), d["UHg"][16 + g], reads=["UHg"], writes=["x1t"])
            P.dma("sp", x2f[:], d["X2_fm"][c0g:c0g + 32, :], reads=["X2_fm"], writes=["x2f"])
            for o in range(2):
                P.dma("sp", skipb[o][:], d["hy_skip"][o:o + 1, c0g:c0g + 32].partition_broadcast(128), writes=["skipb%d" % o])
            for o in range(2):
                for dd in range(2):
                    col0 = o * 1024 + dd * 512 + c0g
                    for a0 in range(0, 64, 16):
                        bk, bkk = kb.bank()
                        for ai in range(16):
                            P.op("pe", lambda e, bk=bk, ai=ai, a=a0 + ai, col0=col0: e.matmul(
                                bk[0:64, ai * 32:(ai + 1) * 32], lhsT=h3v[:, a, :], rhs=fw4[:, col0:col0 + 32], start=True, stop=True),
                                reads=["h3T", "fw4"], writes=[bkk])
                        P.op("dve", lambda e, bk=bk, dd=dd, a0=a0: e.tensor_tensor(
                            out=hs[dd][:, a0:a0 + 16, :], in0=bk[0:64, :].rearrange("p (a c) -> p a c", c=32),
                            in1=win[:, a0:a0 + 16, :], op=ALU.mult), reads=[bkk, "win"], writes=[("hs%d" % dd, a0)])
                hk = lambda dd: [("hs%d" % dd, a0) for a0 in range(0, 64, 16)]
                fwd_AB(hs[0], hk(0), W2, "W2")
                fwd_AB(hs[1], hk(1), Bb, "Bb")
                for (cs, n) in CH7:
                    ncol = n * 65
                    ba, bak = kb.bank()
                    bb, bbk = kb.bank()
                    seq = [(4, W2, 0, "W2"), (5, W2, 1, "W2"), (4, Bb, 0, "Bb"), (5, Bb, 1, "Bb")]
                    for i, (ti, Bt, r, Bk) in enumerate(seq):
                        P.op("pe", lambda e, ba=ba, ti=ti, Bt=Bt, r=r, cs=cs, n=n, ncol=ncol, i=i: e.matmul(
                            ba[:, 0:ncol], lhsT=S3[:, ti, :], rhs=bcols(Bt, r, cs, n), start=(i == 0), stop=(i == 3)),
                            reads=[(Bk, r), "tabs"], writes=[bak])
                    seq = [(6, W2, 0, "W2"), (7, W2, 1, "W2"), (8, Bb, 0, "Bb"), (9, Bb, 1, "Bb")]
                    for i, (ti, Bt, r, Bk) in enumerate(seq):
                        P.op("pe", lambda e, bb=bb, ti=ti, Bt=Bt, r=r, cs=cs, n=n, ncol=ncol, i=i: e.matmul(
                            bb[:, 0:ncol], lhsT=S3[:, ti, :], rhs=bcols(Bt, r, cs, n), start=(i == 0), stop=(i == 3)),
                            reads=[(Bk, r), "tabs"], writes=[bbk])
                    P.op("dve", lambda e, ba=ba, o=o, cs=cs, n=n, ncol=ncol: e.tensor_tensor(
                        out=Ha[:, cs * 65:cs * 65 + ncol].rearrange("p (c k) -> p c k", k=65),
                        in0=ba[:, 0:ncol].rearrange("p (c k) -> p c k", k=65),
                        in1=skipb[o][:, cs:cs + n].unsqueeze(2).to_broadcast([128, n, 65]), op=ALU.add),
                        reads=[bak, "skipb%d" % o], writes=[("Ha", cs)])
                    P.op("act", lambda e, bb=bb, cs=cs, ncol=ncol: e.copy(Hb[:, cs * 65:cs * 65 + ncol], bb[:, 0:ncol]),
                         reads=[bbk], writes=[("Hb", cs)])
                sig = Zt
                fwd_AB(sig, ["Zt"], W2, "W2")
                for (cs, n) in CH7:
                    ncol = n * 65
                    bs, bsk = kb.bank()
                    bw, bwk = kb.bank()
                    for i, (ti, r) in enumerate([(0, 0), (1, 1)]):
                        P.op("pe", lambda e, bs=bs, ti=ti, r=r, cs=cs, n=n, ncol=ncol, i=i: e.matmul(
                            bs[:, 0:ncol], lhsT=S3[:, ti, :], rhs=bcols(W2, r, cs, n), start=(i == 0), stop=(i == 1)),
                            reads=[("W2", r), "tabs"], writes=[bsk])
                    for i, (ti, r) in enumerate([(2, 0), (3, 1)]):
                        P.op("pe", lambda e, bw=bw, ti=ti, r=r, cs=cs, n=n, ncol=ncol, i=i: e.matmul(
                            bw[:, 0:ncol], lhsT=S3[:, ti, :], rhs=bcols(W2, r, cs, n), start=(i == 0), stop=(i == 1)),
                            reads=[("W2", r), "tabs"], writes=[bwk])
                    ysl = Y[:, cs * 65:cs * 65 + ncol]
                    P.op("dve", lambda e, bw=bw, cs=cs, ncol=ncol: e.tensor_tensor(
                        out=tmp[:, 0:ncol], in0=bw[:, 0:ncol], in1=Hb[:, cs * 65:cs * 65 + ncol], op=ALU.mult),
                        reads=[bwk, ("Hb", cs)], writes=["tmp3"])
                    P.op("dve", lambda e, bs=bs, cs=cs, ncol=ncol: e.tensor_tensor(
                        out=tmpb[:, 0:ncol], in0=bs[:, 0:ncol], in1=Ha[:, cs * 65:cs * 65 + ncol], op=ALU.mult),
                        reads=[bsk, ("Ha", cs)], writes=["tmp3b"])
                    P.op("pool", lambda e, ysl=ysl, ncol=ncol: e.tensor_tensor(out=ysl, in0=tmpb[:, 0:ncol], in1=tmp[:, 0:ncol], op=ALU.add),
                         reads=["tmp3b", "tmp3"], writes=[("Y", cs)])
                ykeys = [("Y", cs) for (cs, n) in CH7]
                ekeys = []
                for c0 in range(0, 32, 4):
                    bk, bkk = kb.bank()
                    for j in range(4):
                        P.op("pe", lambda e, bk=bk, j=j, c=c0 + j: e.matmul(
                            bk[0:65, j * 128:(j + 1) * 128], lhsT=Y3[:, c, :], rhs=G[:, :], start=True, stop=True),
                            reads=ykeys + ["tabs"], writes=[bkk])
                    P.op("act", lambda e, bk=bk, c0=c0: e.copy(
                        E3[:, c0:c0 + 4, :], bk[0:65, :].rearrange("p (c k) -> p c k", k=128)),
                        reads=[bkk, "W1tokD", "W1tokP"], writes=[("W1", c0)])
                    ekeys.append(("W1", c0))
                Ere, Eim = E3[:, :, 0:64], E3[:, :, 64:128]
                P.op("dve", lambda e: e.tensor_tensor(out=t1i, in0=Ere, in1=TWic, op=ALU.mult), reads=ekeys + ["tabs"], writes=["T1"])
                P.op("pool", lambda e: e.tensor_tensor(out=t2i, in0=Eim, in1=TWis, op=ALU.mult), reads=ekeys + ["tabs"], writes=["T2"])
                P.op("dve", lambda e: e.tensor_tensor(out=Et4[:, 0], in0=t1i, in1=t2i, op=ALU.subtract),
                     reads=["T1", "T2"], writes=[("W2", 0)])
                P.op("pool", lambda e: e.tensor_tensor(out=t2i, in0=Eim, in1=TWic, op=ALU.mult),
                     reads=ekeys + ["tabs"], writes=["T2", "W1tokP"])
                P.op("dve", lambda e: e.tensor_tensor(out=t1i, in0=Ere, in1=TWis, op=ALU.mult), reads=ekeys + ["tabs"], writes=["T1", "W1tokD"])
                P.op("pool", lambda e: e.tensor_tensor(out=Et4[:, 1], in0=t2i, in1=t1i, op=ALU.add),
                     reads=["T1", "T2"], writes=[("W2", 1)])
                if o == 0:
                    for c8 in range(4):
                        bk, bkk = kb.bank()
                        for r in range(2):
                            P.op("pe", lambda e, bk=bk, r=r, c8=c8: e.matmul(
                                bk[0:64, :], lhsT=I3t[:, r, :], rhs=W2[0:65, r * 2048 + c8 * 512: r * 2048 + (c8 + 1) * 512], start=(r == 0), stop=(r == 1)),
                                reads=[("W2", r), "tabs"], writes=[bkk])
                        cs_ = slice(c8 * 8, (c8 + 1) * 8)
                        P.op("dve", lambda e, bk=bk, cs_=cs_: e.tensor_tensor(
                            out=Zt[:, :, cs_].rearrange("p a c -> p c a"),
                            in0=bk[0:64, :].rearrange("p (c a) -> p c a", a=64),
                            in1=x1t[:, :, cs_].rearrange("p a c -> p c a"), op=ALU.mult),
                            reads=[bkk, "x1t", "Zt"], writes=["Zt"])
                else:
                    x2v = x2f[:, :].rearrange("p (b a) -> p a b", a=64)
                    for a0 in range(0, 64, 8):
                        bk, bkk = kb.bank()
                        for ai in range(8):
                            for r in range(2):
                                P.op("pe", lambda e, bk=bk, ai=ai, a=a0 + ai, r=r: e.matmul(
                                    bk[0:32, ai * 64:(ai + 1) * 64], lhsT=Et4[:, r, :, a], rhs=I3t[:, r, :],
                                    start=(r == 0), stop=(r == 1)), reads=[("W2", r), "tabs"], writes=[bkk])
                        P.op("dve", lambda e, bk=bk, a0=a0: e.tensor_tensor(
                            out=x2v[:, a0:a0 + 8, :], in0=bk[0:32, :].rearrange("p (a b) -> p a b", b=64),
                            in1=x2v[:, a0:a0 + 8, :], op=ALU.mult), reads=[bkk, "x2f"], writes=["x2f"])
                    P.dma("sp", d["ZH_fm"][c0g:c0g + 32, :], x2f[:], reads=["x2f"], writes=["ZH_fm"])
        print("stage3", P.emit_stage())


DECAY_C = -math.exp(-0.5)


def stage4(kb):
    nc, P = kb.nc, kb.P
    d = kb.d
    with contextlib.ExitStack() as s:
        sb = lambda n, sh, dt=F32: K.sb(s, nc, n, sh, dt)
        ident = sb("ident", [128, 128])
        onesblk = sb("onesblk", [128, 128])
        mSU, mIU, mSM, Istk = sb("mSU", [128, 512]), sb("mIU", [128, 512]), sb("mSM", [128, 512]), sb("Istk", [128, 512])
        cmask = sb("cmask", [128, 1024])
        w2s, a2s = sb("w2s", [128, 512]), sb("a2s", [128, 512])
        w0s, a0s = sb("w0s", [128, 2, 4]), sb("a0s", [128, 2, 4])
        kks, kas = sb("kks", [128, 4]), sb("kas", [128, 4])
        S0T = sb("S0T", [128, 512])
        U = [sb("U%d" % i, [128, 14, 256]) for i in range(2)]
        nm7 = ("Qh", "Rh", "Bh", "Kh", "Bt", "Kt")
        OUT = [{n: sb("%s%d" % (n, i), [128, 4, 256]) for n in nm7} for i in range(2)]
        gE = [sb("gE%d" % i, [128, 16]) for i in range(2)]
        Yout = [sb("Yout%d" % i, [128, 4, 256]) for i in range(2)]
        tn = ("thT", )
        thT = sb("thT", [128, 256])
        TM = {n: sb("p_" + n, [128, 4, 256]) for n in ("lw", "a", "tk", "sq", "kk", "kd", "b", "cum", "cum2", "e")}
        totT = sb("totT", [128, 16])
        st_names = ("Vt", "Btt", "Ktt", "Nrb", "Mak", "Nrk", "Xs", "SA", "tmpS",
                    "Pu0", "Pu1", "Pm0", "Pm1", "T0", "T1", "Tt0", "Tt1")
        bfn = ("Pu0", "Pu1", "Pm0", "Pm1", "T0", "T1", "Tt0", "Tt1")
        ST = {n: sb("c_" + n, [128, 512], BF16 if n in bfn else F32) for n in st_names}
        ST["TtF"] = sb("c_TtF", [128, 512])
        Istk_bf = sb("Istk_bf", [128, 512], BF16)
        for nm, t_ in (("ident", ident), ("onesblk", onesblk), ("mSU", mSU), ("mIU", mIU), ("mSM", mSM), ("Istk", Istk), ("cmask", cmask)):
            P.dma("sp", t_[:], d[nm], writes=["consts"])
        P.dma("sp", w2s[:], d["rw_w2"], writes=["consts"])
        P.dma("sp", a2s[:], d["rw_a2"], writes=["consts"])
        P.dma("sp", w0s[:], d["rw_w0"], writes=["consts"])
        P.dma("sp", a0s[:], d["rw_a0"], writes=["consts"])
        P.dma("sp", kks[:], d["rw_k_k"], writes=["consts"])
        P.dma("sp", kas[:], d["rw_k_a"], writes=["consts"])
        P.op("pool", lambda e: e.memset(S0T[:], 0.0), writes=["S0T"])
        P.op("act", lambda e: e.copy(Istk_bf[:, :], Istk[:, :]), reads=["consts"], writes=["Istk_bf"])
        urv = d["UR_fm"].rearrange("(j p) t -> p j t", p=128)
        kk_bc = kks[:, :].unsqueeze(2).to_broadcast([128, 4, 256])
        ka_bc = kas[:, :].unsqueeze(2).to_broadcast([128, 4, 256])
        fl = lambda t_: t_[:].rearrange("p h t -> p (h t)")

        def prep(dr, blk):
            Ud, Uk = U[dr], "U%d" % dr
            O = OUT[dr]
            ok = lambda n: "%s%d" % (n, dr)
            dR = slice(dr * 64, (dr + 1) * 64)
            t0 = blk * 256
            P.dma("sp", Ud[:, 0:7, :], urv[:, 0:7, t0:t0 + 256], reads=["UR_fm"], writes=[Uk])
            P.dma("sp", Ud[:, 7:14, :], urv[:, 7:14, t0:t0 + 256], reads=["UR_fm", Uk], writes=[Uk])
            r_, k_, v_ = Ud[:, 0:4, :], Ud[:, 4:8, :], Ud[:, 8:12, :]
            P.op("act", lambda e: e.activation(out=thT[dR, :], in_=Ud[dR, 12, :], func=AF.Tanh), reads=[Uk], writes=["thT"])
            for (wsb, rhs_ap, rkeys, w0t, outn) in ((w2s, thT[dR, :], ["thT"], w0s, "lw"), (a2s, Ud[dR, 13, :], [Uk], a0s, "a")):
                for half in range(2):
                    bk, bkk = kb.bank()
                    for j in range(2):
                        hq = half * 2 + j
                        P.op("pe", lambda e, bk=bk, j=j, hq=hq, wsb=wsb, rhs_ap=rhs_ap: e.matmul(
                            bk[:, j * 256:(j + 1) * 256], lhsT=wsb[dR, hq * 128:(hq + 1) * 128], rhs=rhs_ap,
                            start=True, stop=True, tile_position=(dr * 64, 0)), reads=rkeys + ["consts"], writes=[bkk])
                    for j in range(2):
                        hq = half * 2 + j
                        P.op("act", lambda e, bk=bk, j=j, hq=hq, w0t=w0t, outn=outn: e.activation(
                            out=TM[outn][:, hq, :], in_=bk[:, j * 256:(j + 1) * 256], func=AF.Sigmoid,
                            bias=w0t[:, dr, hq:hq + 1], scale=1.0), reads=[bkk, "consts"], writes=[("p_" + outn, hq)])
            lwk = [("p_lw", hq) for hq in range(4)]
            ak = [("p_a", hq) for hq in range(4)]
            P.op("dve", lambda e: e.tensor_scalar(out=fl(TM["lw"]), in0=fl(TM["lw"]), scalar1=DECAY_C, scalar2=None, op0=ALU.mult),
                 reads=lwk, writes=lwk)
            P.op("dve", lambda e: e.tensor_tensor(out=TM["tk"][:], in0=k_, in1=kk_bc, op=ALU.mult), reads=[Uk, "consts"], writes=["p_tk"])
            P.op("pool", lambda e: e.tensor_tensor(out=TM["sq"][:], in0=TM["tk"][:], in1=TM["tk"][:], op=ALU.mult), reads=["p_tk"], writes=["p_sq"])
            for half in range(2):
                bk, bkk = kb.bank()
                for j in range(2):
                    hq = half * 2 + j
                    P.op("pe", lambda e, bk=bk, j=j, hq=hq: e.matmul(
                        bk[:, j * 256:(j + 1) * 256], lhsT=onesblk[:, :], rhs=TM["sq"][:, hq, :], start=True, stop=True),
                        reads=["p_sq", "consts"], writes=[bkk])
                P.op("act", lambda e, bk=bk, half=half: e.activation(
                    out=TM["kk"][:, half * 2:half * 2 + 2, :], in_=bk[:, :].rearrange("p (h t) -> p h t", h=2), func=AF.Sqrt),
                    reads=[bkk], writes=[("p_kk", half)])
            kkk = [("p_kk", 0), ("p_kk", 1)]
            P.op("dve", lambda e: e.tensor_scalar(out=fl(TM["kk"]), in0=fl(TM["kk"]), scalar1=1e-12, scalar2=None, op0=ALU.max), reads=kkk, writes=kkk)
            P.op("dve", lambda e: e.reciprocal(fl(TM["kk"]), fl(TM["kk"])), reads=kkk, writes=kkk)
            P.op("pool", lambda e: e.tensor_tensor(out=TM["kk"][:], in0=TM["kk"][:], in1=TM["tk"][:], op=ALU.mult), reads=kkk + ["p_tk"], writes=kkk)
            P.op("dve", lambda e: e.scalar_tensor_tensor(out=TM["kd"][:], in0=TM["a"][:], scalar=-1.0, in1=ka_bc, op0=ALU.add, op1=ALU.mult),
                 reads=ak + ["consts"], writes=["p_kd"])
            P.op("dve", lambda e: e.scalar_tensor_tensor(out=TM["kd"][:], in0=TM["kd"][:], scalar=1.0, in1=k_, op0=ALU.add, op1=ALU.mult),
                 reads=["p_kd", Uk], writes=["p_kd"])
            P.op("pool", lambda e: e.tensor_tensor(out=TM["b"][:], in0=TM["kk"][:], in1=TM["a"][:], op=ALU.mult), reads=kkk + ak, writes=["p_b"])
            P.op("dve", lambda e: e.tensor_tensor_scan(out=fl(TM["cum"]), data0=cmask[:, :], data1=fl(TM["lw"]), initial=0.0,
                                                       op0=ALU.mult, op1=ALU.add), reads=lwk + ["consts"], writes=["p_cum"])
            cum3 = fl(TM["cum"]).rearrange("p (c i) -> p c i", i=64)
            P.op("act", lambda e: e.copy(totT[:, :], cum3[:, :, 63]), reads=["p_cum"], writes=["totT"])
            tot_bc = totT[:, :].unsqueeze(2).to_broadcast([128, 16, 64])
            c2 = fl(TM["cum2"]).rearrange("p (c i) -> p c i", i=64)
            if dr == 1:
                P.op("dve", lambda e: e.tensor_tensor(out=c2, in0=tot_bc, in1=cum3, op=ALU.subtract), reads=["totT", "p_cum"], writes=["p_cum2"])
                P.op("dve", lambda e: e.tensor_tensor(out=TM["cum"][:], in0=TM["cum2"][:], in1=TM["lw"][:], op=ALU.add),
                     reads=["p_cum2"] + lwk, writes=["p_cum"])
            P.op("act", lambda e: e.activation(out=gE[dr][:, :], in_=totT[:, :], func=AF.Exp), reads=["totT"], writes=["gE%d" % dr])
            P.op("act", lambda e: e.activation(out=TM["e"][:], in_=TM["cum"][:], func=AF.Exp), reads=["p_cum"], writes=["p_e"])
            P.op("pool", lambda e: e.tensor_tensor(out=O["Rh"][:], in0=r_, in1=TM["e"][:], op=ALU.mult), reads=[Uk, "p_e"], writes=[ok("Rh")])
            P.op("act", lambda e: e.activation(out=TM["e"][:], in_=TM["cum"][:], func=AF.Exp, scale=-1.0), reads=["p_cum"], writes=["p_e"])
            P.op("dve", lambda e: e.tensor_tensor(out=O["Bh"][:], in0=TM["b"][:], in1=TM["e"][:], op=ALU.mult), reads=["p_b", "p_e"], writes=[ok("Bh")])
            P.op("pool", lambda e: e.tensor_tensor(out=O["Kh"][:], in0=TM["kd"][:], in1=TM["e"][:], op=ALU.mult), reads=["p_kd", "p_e"], writes=[ok("Kh")])
            P.op("dve", lambda e: e.tensor_tensor(out=TM["cum2"][:], in0=TM["cum"][:], in1=TM["lw"][:], op=ALU.subtract),
                 reads=["p_cum"] + lwk, writes=["p_cum2"])
            P.op("act", lambda e: e.activation(out=TM["e"][:], in_=TM["cum2"][:], func=AF.Exp), reads=["p_cum2"], writes=["p_e"])
            P.op("pool", lambda e: e.tensor_tensor(out=O["Qh"][:], in0=TM["kk"][:], in1=TM["e"][:], op=ALU.mult), reads=kkk + ["p_e"], writes=[ok("Qh")])
            P.op("dve", lambda e: e.tensor_tensor(out=c2, in0=tot_bc, in1=fl(TM["cum"]).rearrange("p (c i) -> p c i", i=64), op=ALU.subtract),
                 reads=["totT", "p_cum"], writes=["p_cum2"])
            P.op("act", lambda e: e.activation(out=TM["e"][:], in_=TM["cum2"][:], func=AF.Exp), reads=["p_cum2"], writes=["p_e"])
            P.op("dve", lambda e: e.tensor_tensor(out=O["Bt"][:], in0=TM["b"][:], in1=TM["e"][:], op=ALU.mult), reads=["p_b", "p_e"], writes=[ok("Bt")])
            P.op("pool", lambda e: e.tensor_tensor(out=O["Kt"][:], in0=TM["kd"][:], in1=TM["e"][:], op=ALU.mult), reads=["p_kd", "p_e"], writes=[ok("Kt")])

        heads = [(dr, hq, hp) for dr in range(2) for hq in range(4) for hp in range(2)]

        def per_head(bk, bkk, fn_list, reads):
            nacc = len(fn_list)
            for (dr, hq, hp) in heads:
                hpR = slice(hp * 64, (hp + 1) * 64)
                cb = (dr * 4 + hq) * 64
                for i, (lf, rf) in enumerate(fn_list):
                    lt, rt = lf(dr, hq, hpR, cb), rf(dr, hq, hpR, cb)
                    P.op("pe", lambda e, bk=bk, hpR=hpR, cb=cb, lt=lt, rt=rt, i=i, hp=hp: e.matmul(
                        bk[hpR, cb:cb + 64], lhsT=lt, rhs=rt,
                        start=(i == 0), stop=(i == nacc - 1), tile_position=(hp * 64, hp * 64)),
                        reads=reads, writes=[bkk])

        stk = lambda name: (lambda dr, hq, hpR, cb: ST[name][hpR, cb:cb + 64])
        S0f = lambda dr, hq, hpR, cb: S0T[hpR, cb:cb + 64]
        Isf = lambda dr, hq, hpR, cb: Istk_bf[hpR, cb:cb + 64]
        evi = [0]

        def evac(dst_key, dst_ap, bk, bkk, extra_reads=(), eng=None):
            e_ = eng or ("act" if evi[0] % 2 == 0 else "dve")
            evi[0] += 1
            if e_ == "act":
                P.op("act", lambda e: e.copy(dst_ap, bk[:, :]), reads=[bkk] + list(extra_reads), writes=[dst_key])
            else:
                P.op("dve", lambda e: e.tensor_copy(dst_ap, bk[:, :]), reads=[bkk] + list(extra_reads), writes=[dst_key])

        for j in range(getattr(kb, "rw_steps", 16)):
            blks = (j, 15 - j)
            for dr in range(2):
                prep(dr, blks[dr])
            for lc in range(4 if getattr(kb, "rw_phase", 9) > 0 else 0):
                lcd = (lc, 3 - lc)
                cs = [slice(lcd[dr] * 64, (lcd[dr] + 1) * 64) for dr in range(2)]
                fm = lambda name, cs=cs: (lambda dr, hq, hpR, cb, cs=cs: OUT[dr][name][hpR, hq, cs[dr]])
                Vf = lambda dr, hq, hpR, cb, cs=cs: U[dr][hpR, 8 + hq, cs[dr]]
                okeys = lambda n: ["%s0" % n, "%s1" % n]
                for (name, srcf, rk) in (("Vt", Vf, ["U0", "U1"]), ("Btt", fm("Bt"), okeys("Bt")), ("Ktt", fm("Kt"), okeys("Kt"))):
                    bk, bkk = kb.bank()
                    for (dr, hq, hp) in heads:
                        hpR = slice(hp * 64, (hp + 1) * 64)
                        cb = (dr * 4 + hq) * 64
                        src_ap = srcf(dr, hq, hpR, cb)
                        P.op("pe", lambda e, bk=bk, hpR=hpR, cb=cb, src_ap=src_ap, hp=hp: e.matmul(
                            bk[hpR, cb:cb + 64], lhsT=src_ap, rhs=ident[hpR, hpR], start=True, stop=True,
                            tile_position=(hp * 64, hp * 64)), reads=rk + ["consts"], writes=[bkk])
                    evac(name, ST[name][:, :], bk, bkk)
                if getattr(kb, "rw_phase", 9) < 2:
                    continue
                grams = (("Pu0", "Bh", "Qh", mSU), ("Nrb", "Bh", "Rh", mIU), ("Mak", "Kh", "Qh", mSU), ("Nrk", "Kh", "Rh", mIU), ("Pm0", "Qh", "Bh", mSM))
                for (dst, ln, rn, msk) in grams:
                    bk, bkk = kb.bank()
                    per_head(bk, bkk, [(fm(ln), fm(rn))], okeys(ln) + okeys(rn))
                    P.op("dve", lambda e, bk=bk, dst=dst, msk=msk: e.tensor_tensor(out=ST[dst][:, :], in0=bk[:, :], in1=msk[:, :], op=ALU.mult),
                         reads=[bkk, "consts"], writes=[dst])
                P.op("pool", lambda e: e.tensor_tensor(out=ST["T0"][:, :], in0=Istk[:, :], in1=ST["Pm0"][:, :], op=ALU.subtract),
                     reads=["Pm0", "consts"], writes=["T0"])
                P.op("pool", lambda e: e.tensor_tensor(out=ST["Tt0"][:, :], in0=Istk[:, :], in1=ST["Pu0"][:, :], op=ALU.subtract),
                     reads=["Pu0", "consts"], writes=["Tt0"])
                if getattr(kb, "rw_phase", 9) < 3:
                    continue
                cur = 0
                for lvl in range(1, 6):
                    nxt = 1 - cur
                    pu, pm, tt_, t_ = "Pu%d" % cur, "Pm%d" % cur, "Tt%d" % cur, "T%d" % cur
                    pun, pmn, ttn, tn_ = "Pu%d" % nxt, "Pm%d" % nxt, "Tt%d" % nxt, "T%d" % nxt
                    bk, bkk = kb.bank()
                    per_head(bk, bkk, [(stk(pm), stk(pu))], [pm, pu])
                    evac(pun, ST[pun][:, :], bk, bkk)
                    if lvl < 5:
                        bk, bkk = kb.bank()
                        per_head(bk, bkk, [(stk(pu), stk(pm))], [pm, pu])
                        evac(pmn, ST[pmn][:, :], bk, bkk)
                    bk, bkk = kb.bank()
                    per_head(bk, bkk, [(stk(t_), stk(pun)), (stk(t_), Isf)], [t_, pun, "Istk_bf"])
                    if lvl == 5:
                        ttn = "TtF"
                    evac(ttn, ST[ttn][:, :], bk, bkk)
                    if lvl < 5:
                        bk, bkk = kb.bank()
                        per_head(bk, bkk, [(stk(tt_), stk(pmn)), (stk(tt_), Isf)], [tt_, pmn, "Istk_bf"])
                        evac(tn_, ST[tn_][:, :], bk, bkk)
                    cur = nxt
                ttf = "TtF"
                if getattr(kb, "rw_phase", 9) < 4:
                    continue
                bk, bkk = kb.bank()
                per_head(bk, bkk, [(fm("Qh"), S0f), (stk("Mak"), stk("Vt"))], okeys("Qh") + ["S0T", "Mak", "Vt"])
                evac("Xs", ST["Xs"][:, :], bk, bkk)
                bk, bkk = kb.bank()
                per_head(bk, bkk, [(stk(ttf), stk("Xs"))], [ttf, "Xs"])
                P.op("act", lambda e, bk=bk: e.mul(ST["SA"][:, :], bk[:, :], -1.0), reads=[bkk], writes=["SA"])
                if getattr(kb, "rw_phase", 9) < 5:
                    continue
                bk1, bkk1 = kb.bank()
                per_head(bk1, bkk1, [(S0f, fm("Rh"))], okeys("Rh") + ["S0T"])
                bk, bkk = kb.bank()
                per_head(bk, bkk, [(stk("SA"), stk("Nrb")), (stk("Vt"), stk("Nrk"))], ["SA", "Nrb", "Vt", "Nrk"])
                P.op("act", lambda e, bk1=bk1: e.copy(ST["Xs"][:, :], bk1[:, :]), reads=[bkk1], writes=["Xs"])
                for dr in range(2):
                    src = bk[:, dr * 256:(dr + 1) * 256].rearrange("p (h t) -> p h t", h=4)
                    src2 = ST["Xs"][:, dr * 256:(dr + 1) * 256].rearrange("p (h t) -> p h t", h=4)
                    dst = Yout[dr][:, :, cs[dr]]
                    P.op("dve", lambda e, src=src, src2=src2, dst=dst: e.tensor_tensor(out=dst, in0=src, in1=src2, op=ALU.add),
                         reads=[bkk, "Xs"], writes=["Yout%d" % dr])
                if getattr(kb, "rw_phase", 9) < 6:
                    continue
                bk, bkk = kb.bank()
                per_head(bk, bkk, [(stk("Btt"), stk("SA")), (stk("Ktt"), stk("Vt"))], ["Btt", "SA", "Ktt", "Vt"])
                for dr in range(2):
                    gsl = gE[dr][:, :].rearrange("p (h c) -> p h c", c=4)[:, :, lcd[dr]].unsqueeze(2).to_broadcast([128, 4, 64])
                    sl = slice(dr * 256, (dr + 1) * 256)
                    P.op("pool", lambda e, gsl=gsl, sl=sl: e.tensor_tensor(
                        out=ST["tmpS"][:, sl].rearrange("p (h v) -> p h v", h=4), in0=S0T[:, sl].rearrange("p (h v) -> p h v", h=4),
                        in1=gsl, op=ALU.mult), reads=["S0T", "gE%d" % dr], writes=["tmpS"])
                P.op("dve", lambda e, bk=bk: e.tensor_tensor(out=S0T[:, :], in0=bk[:, :], in1=ST["tmpS"][:, :], op=ALU.add),
                     reads=[bkk, "tmpS", "S0T"], writes=["S0T"])
            for dr in range(2):
                t0 = blks[dr] * 256
                dst = d["YF_fm" if dr == 0 else "YB_fm"].rearrange("(h p) t -> p h t", p=128)[:, :, t0:t0 + 256]
                P.dma("sp", dst, Yout[dr][:], reads=["Yout%d" % dr], writes=["Y_fm%d" % dr])
        print("stage4", P.emit_stage())


def stage5(kb):
    nc, P = kb.nc, kb.P
    d = kb.d
    with contextlib.ExitStack() as s:
        sb = lambda n, sh, dt=F32: K.sb(s, nc, n, sh, dt)
        onesblk = sb("onesblk", [128, 128])
        a2s, g2s = sb("a2s", [128, 512]), sb("g2s", [128, 512])
        a0s = sb("a0s", [128, 2, 4])
        kas, rks, gnw, gnb = sb("kas", [128, 4]), sb("rks", [128, 4]), sb("gnw", [128, 4]), sb("gnb", [128, 4])
        epsb = sb("epsb", [128, 1])
        U5 = sb("U5", [128, 15, 512])
        yf, yb = sb("yf", [128, 4, 512]), sb("yb", [128, 4, 512])
        sq = sb("sq5", [128, 4, 512])
        af, ab = sb("af", [128, 4, 512]), sb("ab", [128, 4, 512])
        rb = sb("rb", [128, 4, 512])
        zr = sb("zr", [128, 4, 512])
        sgd = sb("sgd", [128, 512])
        mean = [sb("mean%d" % i, [128, 512]) for i in range(2)]
        var = [sb("var%d" % i, [128, 512]) for i in range(2)]
        bon = [sb("bon%d" % i, [128, 512]) for i in range(2)]
        P.dma("sp", onesblk[:], d["onesblk"], writes=["consts"])
        for nm, t_ in (("rw_a2", a2s), ("rw_g2", g2s), ("rw_a0", a0s), ("rw_k_a", kas), ("rw_r_k", rks), ("rw_gn_w", gnw), ("rw_gn_b", gnb)):
            P.dma("sp", t_[:], d[nm], writes=["consts"])
        P.op("pool", lambda e: e.memset(epsb[:], GN_EPS), writes=["consts"])
        urv = d["UR_fm"].rearrange("(j p) t -> p j t", p=128)
        yfv = d["YF_fm"].rearrange("(h p) t -> p h t", p=128)
        ybv = d["YB_fm"].rearrange("(h p) t -> p h t", p=128)
        zrv = d["ZR_fm"].rearrange("(h p) t -> p h t", p=128)
        ka_bc = kas[:, :].unsqueeze(2).to_broadcast([128, 4, 512])
        rk_bc = rks[:, :].unsqueeze(2).to_broadcast([128, 4, 512])
        for tb in range(8):
            ts_ = slice(tb * 512, (tb + 1) * 512)
            P.dma("sp", U5[:, 0:5, :], urv[:, 0:5, ts_], reads=["UR_fm"], writes=["U5"])
            P.dma("sp", U5[:, 5:10, :], urv[:, 5:10, ts_], reads=["UR_fm", "U5"], writes=["U5"])
            P.dma("sp", U5[:, 10:15, :], urv[:, 10:15, ts_], reads=["UR_fm", "U5"], writes=["U5"])
            P.dma("sp", yf[:], yfv[:, :, ts_], reads=["Y_fm0"], writes=["yf"])
            P.dma("sp", yb[:], ybv[:, :, ts_], reads=["Y_fm1"], writes=["yb"])
            r_, k_, v_ = U5[:, 0:4, :], U5[:, 4:8, :], U5[:, 8:12, :]
            P.op("pool", lambda e: e.tensor_tensor(out=yf[:], in0=yf[:], in1=yb[:], op=ALU.add), reads=["yf", "yb"], writes=["yf"])
            P.op("pool", lambda e: e.tensor_tensor(out=sq[:], in0=yf[:], in1=yf[:], op=ALU.mult), reads=["yf"], writes=["sq5"])
            for dr, at, atk in ((0, af, "af"), (1, ab, "ab")):
                dR = slice(dr * 64, (dr + 1) * 64)
                for hq in range(4):
                    bk, bkk = kb.bank()
                    P.op("pe", lambda e, bk=bk, hq=hq, dR=dR, dr=dr: e.matmul(
                        bk[:, :], lhsT=a2s[dR, hq * 128:(hq + 1) * 128], rhs=U5[dR, 13, :], start=True, stop=True,
                        tile_position=(dr * 64, 0)), reads=["U5", "consts"], writes=[bkk])
                    P.op("act", lambda e, bk=bk, hq=hq, at=at, dr=dr: e.activation(
                        out=at[:, hq, :], in_=bk[:, :], func=AF.Sigmoid, bias=a0s[:, dr, hq:hq + 1], scale=1.0),
                        reads=[bkk, "consts"], writes=[(atk, hq)])
            afk = [("af", h) for h in range(4)]
            abk = [("ab", h) for h in range(4)]
            P.op("pool", lambda e: e.tensor_tensor(out=af[:], in0=af[:], in1=ab[:], op=ALU.add), reads=afk + abk, writes=afk)
            P.op("dve", lambda e: e.scalar_tensor_tensor(out=af[:], in0=af[:], scalar=-2.0, in1=ka_bc, op0=ALU.add, op1=ALU.mult),
                 reads=afk + ["consts"], writes=afk)
            P.op("dve", lambda e: e.scalar_tensor_tensor(out=af[:], in0=af[:], scalar=2.0, in1=k_, op0=ALU.add, op1=ALU.mult),
                 reads=afk + ["U5"], writes=afk)
            P.op("pool", lambda e: e.tensor_tensor(out=rb[:], in0=r_, in1=rk_bc, op=ALU.mult), reads=["U5", "consts"], writes=["rb"])
            P.op("pool", lambda e: e.tensor_tensor(out=rb[:], in0=rb[:], in1=af[:], op=ALU.mult), reads=["rb"] + afk, writes=["rb"])
            P.op("act", lambda e: e.activation(out=sgd[:], in_=U5[:, 14, :], func=AF.Sigmoid), reads=["U5"], writes=["sgd"])
            for hq in range(4):
                i2 = hq % 2
                bm, bmk = kb.bank()
                bs, bsk = kb.bank()
                bb_, bbk = kb.bank()
                bg, bgk = kb.bank()
                P.op("pe", lambda e, bm=bm, hq=hq: e.matmul(bm[:, :], lhsT=onesblk[:, :], rhs=yf[:, hq, :], start=True, stop=True),
                     reads=["yf", "consts"], writes=[bmk])
                P.op("pe", lambda e, bs=bs, hq=hq: e.matmul(bs[:, :], lhsT=onesblk[:, :], rhs=sq[:, hq, :], start=True, stop=True),
                     reads=["sq5", "consts"], writes=[bsk])
                P.op("pe", lambda e, bb_=bb_, hq=hq: e.matmul(bb_[:, :], lhsT=onesblk[:, :], rhs=rb[:, hq, :], start=True, stop=True),
                     reads=["rb", "consts"], writes=[bbk])
                P.op("pe", lambda e, bg=bg, hq=hq: e.matmul(bg[:, :], lhsT=g2s[:, hq * 128:(hq + 1) * 128], rhs=sgd[:, :], start=True, stop=True),
                     reads=["sgd", "consts"], writes=[bgk])
                mk, vk, bk_ = "mean%d" % i2, "var%d" % i2, "bon%d" % i2
                P.op("act", lambda e, bm=bm, i2=i2: e.mul(mean[i2][:], bm[:, :], 1.0 / 64), reads=[bmk], writes=[mk])
                P.op("dve", lambda e, i2=i2: e.tensor_tensor(out=var[i2][:], in0=mean[i2][:], in1=mean[i2][:], op=ALU.mult), reads=[mk], writes=[vk])
                P.op("dve", lambda e, bs=bs, i2=i2: e.scalar_tensor_tensor(
                    out=var[i2][:], in0=bs[:, :], scalar=1.0 / 64, in1=var[i2][:], op0=ALU.mult, op1=ALU.subtract),
                    reads=[bsk, vk], writes=[vk])
                P.op("act", lambda e, i2=i2: e.activation(out=var[i2][:], in_=var[i2][:], func=AF.Sqrt, bias=epsb[:, 0:1], scale=1.0),
                     reads=[vk, "consts"], writes=[vk])
                P.op("dve", lambda e, i2=i2: e.reciprocal(var[i2][:], var[i2][:]), reads=[vk], writes=[vk])
                P.op("pool", lambda e, i2=i2, hq=hq: e.tensor_tensor(out=mean[i2][:], in0=yf[:, hq, :], in1=mean[i2][:], op=ALU.subtract),
                     reads=["yf", mk], writes=[mk])
                P.op("pool", lambda e, i2=i2: e.tensor_tensor(out=mean[i2][:], in0=mean[i2][:], in1=var[i2][:], op=ALU.mult),
                     reads=[mk, vk], writes=[mk])
                P.op("act", lambda e, i2=i2, hq=hq: e.activation(out=mean[i2][:], in_=mean[i2][:], func=AF.Identity,
                                                                  bias=gnb[:, hq:hq + 1], scale=gnw[:, hq:hq + 1]),
                     reads=[mk, "consts"], writes=[mk])
                P.op("dve", lambda e, bb_=bb_, i2=i2, hq=hq: e.tensor_tensor(out=bon[i2][:], in0=bb_[:, :], in1=U5[:, 8 + hq, :], op=ALU.mult),
                     reads=[bbk, "U5"], writes=[bk_])
                P.op("pool", lambda e, i2=i2: e.tensor_tensor(out=bon[i2][:], in0=bon[i2][:], in1=mean[i2][:], op=ALU.add),
                     reads=[bk_, mk], writes=[bk_])
                P.op("dve", lambda e, bg=bg, i2=i2, hq=hq: e.tensor_tensor(out=zr[:, hq, :], in0=bg[:, :], in1=bon[i2][:], op=ALU.mult),
                     reads=[bgk, bk_], writes=[("zr", hq)])
            P.dma("sp", zrv[:, :, ts_], zr[:], reads=[("zr", h) for h in range(4)], writes=["ZR_fm"])
        print("stage5", P.emit_stage())


def build(mode="full", dbg=()):
    kb = K(dbg)
    nc = kb.nc
    d = {}
    kb.d = d
    d["xT"] = kb.din("xT", [1024, 4096])
    d["x"] = kb.din("x", [4096, 1024])
    d["w_in"] = kb.din("w_in", [1024, 5504])
    d["cw"] = kb.din("cw", [128, 3, 12])
    d["cb"] = kb.din("cb", [128, 12])
    d["mu"] = kb.din("mu", [128, 15])
    d["ident"] = kb.din("ident", [128, 128])
    for nm, sh in (("w_hy_out", [512, 1024]), ("w_rw_out", [512, 1024]), ("w_o", [1024, 1024]),
                   ("ln1_w", [1, 1024]), ("ln1_b", [1, 1024]), ("ln2_w", [1, 1024]), ("ln2_b", [1, 1024]),
                   ("ffn_w_gate", [1024, 2816]), ("ffn_w_up", [1024, 2816]), ("ffn_w_down", [2816, 1024])):
        d[nm] = kb.din(nm, sh)
    ct_shapes = {"featsT": [33, 4096], "win": [16, 64, 2048], "F1": [64, 130], "TWf": [64, 2, 65], "S3": [64, 10, 128],
                 "G": [128, 128], "TWi": [65, 2, 64], "I3t": [65, 2, 64], "onesblk": [128, 128]}
    for nm, sh in ct_shapes.items():
        d[nm] = kb.din(nm, sh)
    for nm, sh in (("hy_filt_w1", [33, 64]), ("hy_filt_w2", [64, 64]), ("hy_filt_w3", [64, 64]), ("hy_fb", [64, 3]),
                   ("hy_sf", [64, 3]), ("hy_filt_w4", [64, 2048]), ("hy_skip", [2, 512])):
        d[nm] = kb.din(nm, sh)
    for nm, sh in (("mSU", [128, 512]), ("mIU", [128, 512]), ("mSM", [128, 512]), ("Istk", [128, 512]), ("cmask", [128, 1024]),
                   ("rw_w2", [128, 512]), ("rw_a2", [128, 512]), ("rw_g2", [128, 512]), ("rw_w0", [128, 2, 4]), ("rw_a0", [128, 2, 4]),
                   ("rw_k_k", [128, 4]), ("rw_k_a", [128, 4]), ("rw_r_k", [128, 4]), ("rw_gn_w", [128, 4]), ("rw_gn_b", [128, 4])):
        d[nm] = kb.din(nm, sh)
    d["YF_fm"] = kb.dscr("YF_fm", [512, 4096])
    d["YB_fm"] = kb.dscr("YB_fm", [512, 4096])
    d["H3T"] = kb.dscr("H3T", [64, 4096])
    skel = mode in ("skel", "s16")
    d["UHg"] = kb.dscr("UHg", [32, 64, 64 * 32])
    d["X2_fm"] = kb.dscr("X2_fm", [512, 4096])
    d["UR_fm"] = kb.dscr("UR_fm", [1920, 4096])
    d["GT_fm"] = kb.dscr("GT_fm", [2048, 4096])
    d["ZH_fm"] = kb.dscr("ZH_fm", [512, 4096], ext_in=skel)
    d["ZR_fm"] = kb.dscr("ZR_fm", [512, 4096], ext_in=skel)
    d["X1_tm"] = kb.dscr("X1_tm", [4096, 1024])
    d["out"] = nc.dram_tensor("out", [4096, 1024], F32, kind="ExternalOutput").ap()
    if mode == "s1" or mode.startswith("s1:"):
        if mode != "s1":
            kb.cc_list = [int(v) for v in mode.split(":")[1].split(",")]
        stage1(kb)
    elif mode.startswith("rw:"):
        kb.rw_steps = int(mode.split(":")[1])
        if len(mode.split(":")) > 2:
            kb.rw_phase = int(mode.split(":")[2])
        if len(mode.split(":")) > 3:
            kb.f_sel = [int(c) for c in mode.split(":")[3]]
        stage1(kb)
        stage4(kb)
    elif mode == "rw":
        stage1(kb)
        stage4(kb)
        stage5(kb)
    elif mode == "full":
        stage1(kb)
        stage2(kb)
        stage3(kb)
        stage4(kb)
        stage5(kb)
        stage6(kb)
        stage7(kb)
    elif mode == "hy":
        stage1(kb)
        stage2(kb)
        stage3(kb)
    elif mode == "s16":
        stage1(kb)
        stage6(kb)
    else:
        stage1(kb)
        stage6(kb)
        stage7(kb)
    kb.st.close()
    return nc


def host_inputs(inputs, b, mode="full"):
    g = lambda k: np.asarray(inputs[k][0], dtype=np.float32)
    ct = const_tables()
    x = np.asarray(inputs["x"][b], dtype=np.float32)
    m = {}
    m["xT"] = np.ascontiguousarray(x.T)
    m["x"] = np.ascontiguousarray(x)
    m["w_in"] = g("w_in")
    m["cw"] = np.ascontiguousarray(g("hy_conv_w").reshape(3, 12, 128).transpose(2, 0, 1))
    m["cb"] = np.ascontiguousarray(g("hy_conv_b").reshape(12, 128).T)
    m["mu"] = np.ascontiguousarray(g("rw_mu").reshape(15, 128).T)
    m["ident"] = ct["ident"]
    for nm in ("w_hy_out", "w_rw_out", "w_o", "ffn_w_gate", "ffn_w_up", "ffn_w_down"):
        m[nm] = g(nm)
    for nm in ("ln1_w", "ln1_b", "ln2_w", "ln2_b"):
        m[nm] = g(nm).reshape(1, 1024)
    for nm in ("featsT", "win", "F1", "TWf", "S3", "G", "TWi", "I3t", "onesblk"):
        m[nm] = ct[nm]
    for nm in ("hy_filt_w1", "hy_filt_w2", "hy_filt_w3", "hy_filt_w4", "hy_skip"):
        m[nm] = g(nm)
    m["hy_fb"] = np.ascontiguousarray(np.stack([g("hy_filt_b1"), g("hy_filt_b2"), g("hy_filt_b3")], axis=1))
    m["hy_sf"] = np.ascontiguousarray(g("hy_sin_freq").T)
    for nm in ("mSU", "mIU", "mSM", "Istk", "cmask"):
        m[nm] = ct[nm]
    m["rw_w2"] = np.ascontiguousarray(g("rw_w2").reshape(128, 512))
    m["rw_a2"] = np.ascontiguousarray(g("rw_a2").reshape(128, 512))
    m["rw_g2"] = g("rw_g2")
    m["rw_w0"] = np.ascontiguousarray(g("rw_w0").reshape(2, 4, 128).transpose(2, 0, 1))
    m["rw_a0"] = np.ascontiguousarray(g("rw_a0").reshape(2, 4, 128).transpose(2, 0, 1))
    for nm in ("rw_k_k", "rw_k_a", "rw_r_k", "rw_gn_w", "rw_gn_b"):
        m[nm] = np.ascontiguousarray(g(nm).reshape(4, 128).T)
    return m


_NC_CACHE = {}


def kernel(**inputs):
    if "nc" not in _NC_CACHE:
        _NC_CACHE["nc"] = build("full")
    nc = _NC_CACHE["nc"]
    in_maps = [host_inputs(inputs, b) for b in range(8)]
    res = run_bass_kernel_spmd(nc, in_maps, core_ids=list(range(8)))
    out = np.stack([np.asarray(r["out"], dtype=np.float32) for r in res.results], axis=0)
    return out
```
